# Optimizing a Trainium2 kernel written in Bass

```python
import jax, jax.numpy as jnp
from jax import lax
import numpy as np

D_MODEL = 2048
BATCH = 1
SEQ = 8192
DEPTH = 2

CONV_DIM = D_MODEL // 2
CONV_WIDTH = 31
NSA_HEADS = 16
NSA_HEAD_DIM = 64
NSA_KV_HEADS = 4
NSA_GROUP = NSA_HEADS // NSA_KV_HEADS
N_BRANCH = 3
CMP_LEN = 32
CMP_STRIDE = 16
CMP_HIDDEN = 2 * NSA_HEAD_DIM
SEL_LEN = 64
SEL_TOPK = 16
WINDOW = 512
Q_BLOCK = 128
FORCE_BONUS = 1e3
NEG_INF = -1e30
POOL_WINDOWS = (2, 4, 8, 16)
POOL_GROUPS = len(POOL_WINDOWS)
POOL_DIM = D_MODEL // POOL_GROUPS
D_FF = 4 * D_MODEL
EPS = 1e-6

Q_COLS = NSA_HEADS * NSA_HEAD_DIM
KV_COLS = NSA_KV_HEADS * NSA_HEAD_DIM
GATE_COLS = NSA_HEADS * N_BRANCH
IN_COLS = 2 * CONV_DIM + Q_COLS + 2 * N_BRANCH * KV_COLS + GATE_COLS
N_EVEN = (DEPTH + 1) // 2
N_ODD = DEPTH // 2

kernel_name = "hybrid_conv_nsa_pool_block"


def rmsnorm(x, g):
    xf = x.astype(jnp.float32)
    r = lax.rsqrt(jnp.mean(xf * xf, axis=-1, keepdims=True) + EPS)
    return (xf * r * g.astype(jnp.float32)).astype(x.dtype)


def layernorm(x, g, b):
    xf = x.astype(jnp.float32)
    mu = jnp.mean(xf, axis=-1, keepdims=True)
    xc = xf - mu
    var = jnp.mean(xc * xc, axis=-1, keepdims=True)
    y = xc * lax.rsqrt(var + EPS) * g.astype(jnp.float32) + b.astype(jnp.float32)
    return y.astype(x.dtype)


def masked_softmax(s, mask):
    s = jnp.where(mask, s.astype(jnp.float32), NEG_INF)
    p = jax.nn.softmax(s, axis=-1)
    return jnp.where(mask, p, 0.0)


def causal_depthwise_conv(u, w, b):
    C = u.shape[-1]
    lhs = jnp.pad(u, ((0, 0), (CONV_WIDTH - 1, 0), (0, 0)))
    out = lax.conv_general_dilated(
        lhs, w[:, None, :].astype(u.dtype), window_strides=(1,), padding='VALID',
        dimension_numbers=('NWC', 'WIO', 'NWC'), feature_group_count=C)
    return out + b


def compress_blocks(k, pos, w1, w2):
    B, S, G, dh = k.shape
    ch = k.reshape(B, S // CMP_STRIDE, CMP_STRIDE, G, dh)
    blk = jnp.concatenate([ch[:, :-1], ch[:, 1:]], axis=2)
    blk = blk + pos[None, None, :, None, :]
    nc = blk.shape[1]
    blk = blk.transpose(0, 1, 3, 2, 4).reshape(B, nc, G, CMP_LEN * dh)
    h = jax.nn.silu(blk @ w1)
    return h @ w2


def nsa(q, k_c, v_c, k_s, v_s, k_w, v_w, gates, pos_k, pos_v, kw1, kw2, vw1, vw2):
    B, S, H, dh = q.shape
    G, R = NSA_KV_HEADS, NSA_GROUP
    scale = dh ** -0.5
    q = q.reshape(B, S, G, R, dh)
    gates = gates.reshape(B, S, G, R, N_BRANCH)

    kc = compress_blocks(k_c, pos_k, kw1, kw2)
    vc = compress_blocks(v_c, pos_v, vw1, vw2)
    nc = kc.shape[1]
    cmp_start = jnp.arange(nc) * CMP_STRIDE
    cmp_end = cmp_start + CMP_LEN - 1

    n_sel = S // SEL_LEN
    topk = min(SEL_TOPK, n_sel)
    sel_start = jnp.arange(n_sel) * SEL_LEN
    overlap = ((cmp_start[:, None] <= sel_start[None, :] + SEL_LEN - 1)
               & (cmp_end[:, None] >= sel_start[None, :])).astype(jnp.float32)
    ks_blk = k_s.reshape(B, n_sel, SEL_LEN, G, dh).transpose(0, 3, 1, 2, 4)
    vs_blk = v_s.reshape(B, n_sel, SEL_LEN, G, dh).transpose(0, 3, 1, 2, 4)
    gather = jax.vmap(jax.vmap(lambda blk, i: blk[i]))

    kw_pad = jnp.pad(k_w, ((0, 0), (WINDOW, 0), (0, 0), (0, 0)))
    vw_pad = jnp.pad(v_w, ((0, 0), (WINDOW, 0), (0, 0), (0, 0)))
    sel_j = jnp.arange(n_sel)
    sel_off = jnp.arange(SEL_LEN)
    win_off = jnp.arange(WINDOW + Q_BLOCK)

    def block_fn(bi):
        s0 = bi * Q_BLOCK
        qb = lax.dynamic_slice_in_dim(q, s0, Q_BLOCK, axis=1)
        gb = lax.dynamic_slice_in_dim(gates, s0, Q_BLOCK, axis=1)
        t = s0 + jnp.arange(Q_BLOCK)

        sc = jnp.einsum('bqgrd,bngd->bgrqn', qb, kc) * scale
        pc = masked_softmax(sc, cmp_end[None, :] <= t[:, None])
        oc = jnp.einsum('bgrqn,bngd->bqgrd', pc.astype(vc.dtype), vc)

        imp = jnp.einsum('bgrqn,nj->bgqj', pc, overlap)
        cur = t // SEL_LEN
        forced = (sel_j[None, :] == 0) | (sel_j[None, :] == cur[:, None]) | (sel_j[None, :] == cur[:, None] - 1)
        future = sel_j[None, :] > cur[:, None]
        imp = jnp.where(future, NEG_INF, imp + jnp.where(forced, FORCE_BONUS, 0.0))
        _, idx = lax.top_k(imp, topk)
        ks_sel = gather(ks_blk, idx)
        vs_sel = gather(vs_blk, idx)
        ss = jnp.einsum('bqgrd,bgqkld->bgrqkl', qb, ks_sel) * scale
        kpos = idx[..., None] * SEL_LEN + sel_off
        ms = (kpos <= t[:, None, None])[:, :, None]
        ps = masked_softmax(ss.reshape(B, G, R, Q_BLOCK, topk * SEL_LEN),
                            ms.reshape(B, G, 1, Q_BLOCK, topk * SEL_LEN))
        ps = ps.reshape(B, G, R, Q_BLOCK, topk, SEL_LEN).astype(vs_sel.dtype)
        os_ = jnp.einsum('bgrqkl,bgqkld->bqgrd', ps, vs_sel)

        kwb = lax.dynamic_slice_in_dim(kw_pad, s0, WINDOW + Q_BLOCK, axis=1)
        vwb = lax.dynamic_slice_in_dim(vw_pad, s0, WINDOW + Q_BLOCK, axis=1)
        wpos = s0 - WINDOW + win_off
        mw = ((wpos[None, :] <= t[:, None]) & (wpos[None, :] > t[:, None] - WINDOW)
              & (wpos[None, :] >= 0))
        sw = jnp.einsum('bqgrd,bkgd->bgrqk', qb, kwb) * scale
        pw = masked_softmax(sw, mw).astype(vwb.dtype)
        ow = jnp.einsum('bgrqk,bkgd->bqgrd', pw, vwb)

        return gb[..., 0:1] * oc + gb[..., 1:2] * os_ + gb[..., 2:3] * ow

    out = lax.map(block_fn, jnp.arange(S // Q_BLOCK))
    out = out.transpose(1, 0, 2, 3, 4, 5).reshape(B, S, H * dh)
    return out


def even_mixer(x, g, w_in, w_out, conv_w, conv_b, ln_g, ln_b,
               pos_k, pos_v, kw1, kw2, vw1, vw2):
    B, S, _ = x.shape
    xn = rmsnorm(x, g)
    proj = xn @ w_in
    sizes = [CONV_DIM, CONV_DIM, Q_COLS] + [KV_COLS] * (2 * N_BRANCH) + [GATE_COLS]
    cuts = list(np.cumsum(sizes)[:-1])
    a, gt, q, k_c, v_c, k_s, v_s, k_w, v_w, gl = jnp.split(proj, cuts, axis=-1)
    c = a * jax.nn.sigmoid(gt)
    c = causal_depthwise_conv(c, conv_w, conv_b)
    c = jax.nn.silu(layernorm(c, ln_g, ln_b))
    kv = lambda t_: t_.reshape(B, S, NSA_KV_HEADS, NSA_HEAD_DIM)
    o = nsa(q.reshape(B, S, NSA_HEADS, NSA_HEAD_DIM), kv(k_c), kv(v_c), kv(k_s), kv(v_s),
            kv(k_w), kv(v_w), jax.nn.sigmoid(gl.reshape(B, S, NSA_HEADS, N_BRANCH)),
            pos_k, pos_v, kw1, kw2, vw1, vw2)
    return jnp.concatenate([c, o], axis=-1) @ w_out


def odd_mixer(x, g, pool_w, pool_scale):
    B, S, _ = x.shape
    xn = rmsnorm(x, g).astype(jnp.float32)
    cs = jnp.cumsum(xn, axis=1)
    pos1 = jnp.arange(1, S + 1)
    parts = []
    for gi, w in enumerate(POOL_WINDOWS):
        sl = slice(gi * POOL_DIM, (gi + 1) * POOL_DIM)
        csg = cs[..., sl]
        shifted = jnp.pad(csg, ((0, 0), (w, 0), (0, 0)))[:, :S]
        cnt = jnp.minimum(pos1, w).astype(jnp.float32)
        parts.append((csg - shifted) / cnt[None, :, None] - xn[..., sl])
    p = jnp.stack(parts, axis=2).astype(x.dtype)
    y = jnp.einsum('bsgp,gpo->bsgo', p, pool_w).reshape(B, S, D_MODEL)
    return y * pool_scale


def mlp(x, g, w1, w2):
    h = rmsnorm(x, g) @ w1
    h = jnp.square(jax.nn.relu(h))
    return h @ w2


def setup_inputs(seed: int = 0) -> dict:
    key = jax.random.key(seed)
    ks = jax.random.split(key, 24)
    nrm = lambda k, shape, s: jax.random.normal(k, shape, jnp.float32) * s
    dh = NSA_HEAD_DIM
    return {
        "x": nrm(ks[0], (BATCH, SEQ, D_MODEL), 1.0),
        "mix_norm": 1.0 + nrm(ks[1], (DEPTH, D_MODEL), 0.02),
        "mlp_norm": 1.0 + nrm(ks[2], (DEPTH, D_MODEL), 0.02),
        "w_mlp_in": nrm(ks[3], (DEPTH, D_MODEL, D_FF), D_MODEL ** -0.5),
        "w_mlp_out": nrm(ks[4], (DEPTH, D_FF, D_MODEL), D_FF ** -0.5),
        "w_in": nrm(ks[5], (N_EVEN, D_MODEL, IN_COLS), D_MODEL ** -0.5),
        "w_out": nrm(ks[6], (N_EVEN, D_MODEL, D_MODEL), D_MODEL ** -0.5),
        "conv_w": nrm(ks[7], (N_EVEN, CONV_WIDTH, CONV_DIM), CONV_WIDTH ** -0.5),
        "conv_b": nrm(ks[8], (N_EVEN, CONV_DIM), 0.01),
        "conv_ln_g": 1.0 + nrm(ks[9], (N_EVEN, CONV_DIM), 0.02),
        "conv_ln_b": nrm(ks[10], (N_EVEN, CONV_DIM), 0.01),
        "cmp_pos_k": nrm(ks[11], (N_EVEN, CMP_LEN, dh), 0.1),
        "cmp_pos_v": nrm(ks[12], (N_EVEN, CMP_LEN, dh), 0.1),
        "cmp_k_w1": nrm(ks[13], (N_EVEN, CMP_LEN * dh, CMP_HIDDEN), (CMP_LEN * dh) ** -0.5),
        "cmp_k_w2": nrm(ks[14], (N_EVEN, CMP_HIDDEN, dh), CMP_HIDDEN ** -0.5),
        "cmp_v_w1": nrm(ks[15], (N_EVEN, CMP_LEN * dh, CMP_HIDDEN), (CMP_LEN * dh) ** -0.5),
        "cmp_v_w2": nrm(ks[16], (N_EVEN, CMP_HIDDEN, dh), CMP_HIDDEN ** -0.5),
        "pool_w": nrm(ks[17], (N_ODD, POOL_GROUPS, POOL_DIM, POOL_DIM), POOL_DIM ** -0.5),
        "pool_scale": 1.0 + nrm(ks[18], (N_ODD, D_MODEL), 0.02),
        "final_norm": 1.0 + nrm(ks[19], (D_MODEL,), 0.02),
    }


def reference(x, mix_norm, mlp_norm, w_mlp_in, w_mlp_out, w_in, w_out, conv_w, conv_b,
              conv_ln_g, conv_ln_b, cmp_pos_k, cmp_pos_v, cmp_k_w1, cmp_k_w2,
              cmp_v_w1, cmp_v_w2, pool_w, pool_scale, final_norm):
    for layer in range(DEPTH):
        i = layer // 2
        if layer % 2 == 0:
            x = x + even_mixer(x, mix_norm[layer], w_in[i], w_out[i], conv_w[i], conv_b[i],
                               conv_ln_g[i], conv_ln_b[i], cmp_pos_k[i], cmp_pos_v[i],
                               cmp_k_w1[i], cmp_k_w2[i], cmp_v_w1[i], cmp_v_w2[i])
        else:
            x = x + odd_mixer(x, mix_norm[layer], pool_w[i], pool_scale[i])
        x = x + mlp(x, mlp_norm[layer], w_mlp_in[layer], w_mlp_out[layer])
    return rmsnorm(x, final_norm)
```

```python
import numpy as np
from contextlib import ExitStack
import ml_dtypes
import concourse.bass as bass
import concourse.mybir as mybir
from concourse.bass_utils import run_bass_kernel_spmd

F32 = mybir.dt.float32
BF16 = mybir.dt.bfloat16
ALU = mybir.AluOpType
AF = mybir.ActivationFunctionType
AX = mybir.AxisListType

NCORES = 8
D = 2048
S = 8192
NSLOT = 8
TL = 1024
DC = 16
HALO = 32
IN_COLS = 4656
DFF = 8192
EPS = 1e-6
NEG = -30000.0
SCALE = 0.125


class Buf:
    __slots__ = ("name", "w", "r")

    def __init__(self, name=""):
        self.name = name
        self.w = None
        self.r = []


class EngS:
    def __init__(self, nc, eng, name, ndma=0):
        self.nc = nc
        self.e = eng
        self.name = name
        self.sem = nc.alloc_semaphore("sem_" + name)
        self.count = 0
        self.waited = {}
        self.dsems = [[nc.alloc_semaphore("dsem_%s_%d" % (name, i)), 0] for i in range(ndma)]
        self.dnext = 0

    def wait(self, tk):
        if tk is None:
            return
        sem, val = tk
        key = id(sem)
        if self.waited.get(key, 0) >= val:
            return
        self.waited[key] = val
        self.e.wait_ge(sem, val)


class Ctx:
    def __init__(self, nc):
        self.nc = nc
        self.pe = EngS(nc, nc.tensor, "pe")
        self.act = EngS(nc, nc.scalar, "act", ndma=4)
        self.dve = EngS(nc, nc.vector, "dve")
        self.pool = EngS(nc, nc.gpsimd, "pool", ndma=8)
        self.sp = EngS(nc, nc.sync, "sp", ndma=12)
        self.out_tickets = []

    def _deps(self, E, reads, writes, same_engine_ok=False):
        for b in reads:
            if b.w is not None:
                if not (same_engine_ok and b.w[0] is E.sem):
                    E.wait(b.w)
        for b in writes:
            if b.w is not None:
                if not (same_engine_ok and b.w[0] is E.sem):
                    E.wait(b.w)
            for t in b.r:
                if not (same_engine_ok and t[0] is E.sem):
                    E.wait(t)

    def _commit(self, tk, reads, writes):
        for b in reads:
            b.r.append(tk)
            if len(b.r) > 24:
                last = {}
                for t in b.r:
                    k = id(t[0])
                    if k not in last or last[k][1] < t[1]:
                        last[k] = t
                b.r = list(last.values())
        for b in writes:
            b.w = tk
            b.r = []

    def op(self, E, fn, reads=(), writes=(), mark=True):
        self._deps(E, reads, writes, same_engine_ok=(E is self.pe))
        ins = fn()
        if mark:
            E.count += 1
            ins.then_inc(E.sem, 1)
            tk = (E.sem, E.count)
        else:
            tk = (E.sem, E.count + 1)
        self._commit(tk, reads, writes)
        return tk

    def dma(self, Q, pairs, reads=(), writes=(), is_out=False):
        self._deps(Q, reads, writes)
        slot = Q.dsems[Q.dnext]
        Q.dnext = (Q.dnext + 1) % len(Q.dsems)
        sem = slot[0]
        Q.wait((sem, slot[1]))
        for (o, i) in pairs:
            Q.e.dma_start(out=o, in_=i).then_inc(sem, 16)
            slot[1] += 16
        tk = (sem, slot[1])
        self._commit(tk, reads, writes)
        if is_out:
            self.out_tickets.append(tk)
        return tk

    def barrier(self):
        engs = (self.pe, self.act, self.dve, self.pool, self.sp)
        for E in engs:
            for Fo in engs:
                if Fo is not E and Fo.count:
                    E.wait((Fo.sem, Fo.count))
                for sv in Fo.dsems:
                    if sv[1]:
                        E.wait((sv[0], sv[1]))

    def finish(self):
        for tk in self.out_tickets:
            self.sp.wait(tk)
        for E in (self.pe, self.act, self.dve, self.pool):
            if E.count:
                self.sp.wait((E.sem, E.count))
            for s, v in E.dsems:
                if v:
                    self.sp.wait((s, v))
        for s, v in self.sp.dsems:
            if v:
                self.sp.wait((s, v))


_SBN = [0]


def _sb(st, name, shape, dt):
    _SBN[0] += 1
    return st.enter_context(st.nc.sbuf_tensor("s%d_%s" % (_SBN[0], name), shape, dt))


def _scope(nc):
    st = ExitStack()
    st.nc = nc
    return st


def _ps(nc, name, shape, dt):
    return nc.alloc_psum_tensor(name, shape, dt)


class WStream:
    def __init__(self, cx, st, name, kch, cols, nbuf):
        nc = st
        self.cx = cx
        self.kch = kch
        self.cols = cols
        self.t = [_sb(nc, "%s_%d" % (name, j), [128, kch, cols], BF16) for j in range(nbuf)]
        self.b = [Buf() for _ in range(nbuf)]
        self.n = 0

    def load(self, w_ap, c0, ncols, kch=None, r0=0):
        kch = kch or self.kch
        j = self.n % len(self.t)
        self.n += 1
        src = w_ap[r0:r0 + kch * 128, c0:c0 + ncols].rearrange("(k p) c -> p k c", p=128)
        self.cx.dma(self.cx.pool, [(self.t[j][:, 0:kch, 0:ncols], src)], writes=[self.b[j]])
        return self.t[j], self.b[j]

    def load_q(self, w_ap, c0):
        j = self.n % len(self.t)
        self.n += 1
        src = w_ap[:, c0:c0 + 512].rearrange("(k p) (two r e) -> p k r two e", p=128, two=2, r=4)
        dst = self.t[j][:, :, 0:512].rearrange("p k (r two e) -> p k r two e", two=2, r=4)
        self.cx.dma(self.cx.pool, [(dst[:, :, r, two], src[:, :, r, two]) for r in range(4) for two in range(2)], writes=[self.b[j]])
        return self.t[j], self.b[j]


def rms_stats_fm(cx, xT_fn, xbufs, ncols, onesD, scratch):
    nc = cx.nc
    sq, sqb = scratch["sq"], scratch["sqb"]
    ps, psb = scratch["ps"], scratch["psb"]
    rstd, rstdb = scratch["rstd"], scratch["rstdb"]
    nblk = (ncols + 511) // 512
    for k in range(DC):
        j = k % 2
        cx.op(cx.act, lambda: nc.scalar.activation(out=sq[j][:, 0:ncols], in_=xT_fn(k), func=AF.Square),
              reads=[xbufs[k]], writes=[sqb[j]])
        for b in range(nblk):
            c0, c1 = b * 512, min(ncols, b * 512 + 512)
            cx.op(cx.pe, lambda: nc.tensor.matmul(ps[b][:, 0:c1 - c0], lhsT=onesD[:], rhs=sq[j][:, c0:c1],
                                                  start=(k == 0), stop=(k == DC - 1)),
                  reads=[sqb[j]], writes=[psb[b]])
    for b in range(nblk):
        c0, c1 = b * 512, min(ncols, b * 512 + 512)
        cx.op(cx.act, lambda: nc.scalar.activation(out=rstd[:, c0:c1], in_=ps[b][:, 0:c1 - c0], func=AF.Sqrt,
                                                   bias=scratch["eps"][:, 0:1], scale=1.0),
              reads=[psb[b]], writes=[rstdb])
    cx.op(cx.dve, lambda: nc.vector.reciprocal(out=rstd[:, 0:ncols], in_=rstd[:, 0:ncols]),
          reads=[rstdb], writes=[rstdb])


def rms_apply_fm(cx, k, xT_ap, xbuf, g_sb, out_ap, obuf, rstd_ap, rstdb):
    nc = cx.nc
    cx.op(cx.dve, lambda: nc.vector.scalar_tensor_tensor(out=out_ap, in0=xT_ap, scalar=g_sb[:, k:k + 1],
                                                         in1=rstd_ap, op0=ALU.mult, op1=ALU.mult),
          reads=[xbuf, rstdb], writes=[obuf])


def rms_norm_fm(cx, xT_fn, xbufs, ncols, g_sb, out_fn, obufs, onesD, scratch):
    rms_stats_fm(cx, xT_fn, xbufs, ncols, onesD, scratch)
    for k in range(DC):
        rms_apply_fm(cx, k, xT_fn(k), xbufs[k], g_sb, out_fn(k), obufs[k], scratch["rstd"][:, 0:ncols], scratch["rstdb"])


def emit_p1(nc, cx, T, ps, psB):
    W = HALO + 128
    NCOL = NSLOT * W
    xh, w_in, g0, convw, cpar, ident_d = T["xh"], T["w_in"], T["g0"], T["convw"], T["cpar"], T["ident"]
    xT_o, qT_o, cT_o, gates_o, kfm_o, vtm_o = T["xT"], T["qT"], T["cT"], T["gates"], T["kfm"], T["vtm"]
    kvB = T["kvB"]

    s0 = _scope(nc)
    ident = _sb(s0, "ident", [128, 128], F32); identB = Buf()
    onesD = _sb(s0, "onesD", [128, 128], F32); onesB = Buf()
    ones1k = _sb(s0, "ones1k", [128, 128], F32)
    epsT = _sb(s0, "epsT", [128, 1], F32)
    g_sb = _sb(s0, "g_sb", [128, DC], F32); gB = Buf()
    cw_sb = _sb(s0, "cw_sb", [128, 8, 31], F32)
    cp_sb = _sb(s0, "cp_sb", [128, 3, 8], F32)
    cfm = _sb(s0, "cfm", [128, 8, NSLOT, W], F32); cfmB = [Buf() for _ in range(8)]
    s1 = _scope(nc)
    xnO = _sb(s1, "xnO", [128, DC, TL], BF16); xnB = [Buf() for _ in range(DC)]
    xnH = _sb(s1, "xnH", [128, DC, NSLOT * HALO], BF16); xnHB = [Buf() for _ in range(DC)]
    sA = _scope(nc)
    xtok = [_sb(sA, "xtok%d" % j, [128, D], F32) for j in range(1)] * 2; xtokB = [Buf()] * 2
    xhal = [_sb(sA, "xhal%d" % j, [HALO, D], F32) for j in range(1)] * 2; xhalB = [Buf()] * 2
    xTo = _sb(sA, "xTo", [128, DC, TL], F32); xToB = [Buf() for _ in range(DC)]
    xTl = _sb(sA, "xTl", [128, DC, NSLOT * HALO], F32); xTlB = [Buf() for _ in range(DC)]
    sq = [_sb(sA, "sq%d" % j, [128, NCOL], F32) for j in range(2)]; sqB = [Buf(), Buf()]
    rstd = _sb(sA, "rstd", [128, NCOL], F32); rstdB = Buf()

    cx.dma(cx.sp, [(ident[:], ident_d)], writes=[identB])
    cx.dma(cx.sp, [(g_sb[:], g0), (cw_sb[:], convw), (cp_sb[:], cpar)], writes=[gB])
    cx.op(cx.dve, lambda: nc.vector.memset(onesD[:], 1.0 / D), writes=[onesB])
    cx.op(cx.dve, lambda: nc.vector.memset(ones1k[:], 1.0 / 1024), writes=[onesB])
    cx.op(cx.dve, lambda: nc.vector.memset(epsT[:], EPS), writes=[onesB])

    for i in range(NSLOT):
        j = i % 2
        cx.dma(cx.sp, [(xtok[j][:], xh[i, HALO:W, :])], writes=[xtokB[j]])
        cx.dma(cx.sp, [(xhal[j][:], xh[i, 0:HALO, :])], writes=[xhalB[j]])
        for k4 in range(4):
            pb = (i * 4 + k4) % 4
            for kk in range(4):
                k = k4 * 4 + kk
                cx.op(cx.pe, lambda: nc.tensor.transpose(out=ps[pb][:, kk * 128:(kk + 1) * 128],
                                                         in_=xtok[j][:, k * 128:(k + 1) * 128], identity=ident[:]),
                      reads=[xtokB[j], identB], writes=[psB[pb]], mark=(kk == 3))
            cx.op(cx.dve, lambda: nc.vector.tensor_copy(
                out=xTo[:, k4 * 4:k4 * 4 + 4, i * 128:(i + 1) * 128],
                in_=ps[pb][:].rearrange("p (a b) -> p a b", a=4)),
                reads=[psB[pb]], writes=xToB[k4 * 4:k4 * 4 + 4])
        pb = 4 + (i % 2)
        for k in range(DC):
            cx.op(cx.pe, lambda: nc.tensor.transpose(out=ps[pb][:, k * HALO:(k + 1) * HALO],
                                                     in_=xhal[j][:, k * 128:(k + 1) * 128],
                                                     identity=ident[0:HALO, 0:HALO]),
                  reads=[xhalB[j], identB], writes=[psB[pb]], mark=(k == DC - 1))
        cx.op(cx.act, lambda: nc.scalar.copy(out=xTl[:, :, i * HALO:(i + 1) * HALO],
                                             in_=ps[pb][:].rearrange("p (a b) -> p a b", a=DC)),
              reads=[psB[pb]], writes=xTlB)

    cx.dma(cx.sp, [(xT_o, xTo[:])], reads=xToB, is_out=True)

    scratch = dict(sq=sq, sqb=sqB, ps=ps[0:3], psb=psB[0:3], rstd=rstd, rstdb=rstdB, eps=epsT)
    rms_norm_fm(cx, lambda k: xTo[:, k, :], xToB, TL, g_sb, lambda k: xnO[:, k, :], xnB, onesD, scratch)
    rms_norm_fm(cx, lambda k: xTl[:, k, :], xTlB, NSLOT * HALO, g_sb, lambda k: xnH[:, k, :], xnHB, onesD, scratch)

    cx.barrier()
    sA.close()
    sB = _scope(nc)
    ws = WStream(cx, sB, "w1", DC, 512, 3)
    a_sb = _sb(sB, "a_sb", [128, NCOL], F32); aB = Buf()
    sg_sb = _sb(sB, "sg_sb", [128, NCOL], F32); sgB = Buf()
    qT = _sb(sB, "qT", [128, NSLOT, 8, 128], BF16); qB = Buf()
    kfm = _sb(sB, "kfm", [128, 4, NSLOT, 2, 128], BF16); kfB = Buf()
    vtm = _sb(sB, "vtm", [128, 2, NSLOT, 256], BF16); vtB = Buf()
    gat = _sb(sB, "gat", [128, NSLOT, 48], F32); gatB = Buf()
    blocks = [(0, 512), (512, 512), (1024, 256)]
    pscnt = [0]

    def formB_all(wt, wb, c0, evac):
        for bi, (b0, bn) in enumerate(blocks):
            pb = pscnt[0] % 7; pscnt[0] += 1
            for k in range(DC):
                rhs = xnO[:, k, b0:b0 + bn] if bi < 2 else xnH[:, k, :]
                cx.op(cx.pe, lambda: nc.tensor.matmul(ps[pb][:, 0:bn], lhsT=wt[:, k, c0:c0 + 128], rhs=rhs,
                                                      start=(k == 0), stop=(k == DC - 1)),
                      reads=[wb, xnB[k], xnHB[k]], writes=[psB[pb]], mark=(k == DC - 1))
            evac(ps[pb][:, 0:bn], psB[pb], b0, b0 + bn)

    for half in range(2):
        wa, wab = ws.load(w_in, half * 512, 512)
        wg, wgb = ws.load(w_in, 1024 + half * 512, 512)
        for mm in range(4):
            m = half * 4 + mm
            formB_all(wa, wab, mm * 128,
                      lambda p, pbuf, lo, hi: cx.op(cx.act, lambda: nc.scalar.copy(out=a_sb[:, lo:hi], in_=p),
                                                    reads=[pbuf], writes=[aB]))
            formB_all(wg, wgb, mm * 128,
                      lambda p, pbuf, lo, hi: cx.op(cx.act, lambda: nc.scalar.activation(out=sg_sb[:, lo:hi], in_=p,
                                                                                        func=AF.Sigmoid),
                                                    reads=[pbuf], writes=[sgB]))
            cx.op(cx.dve, lambda: nc.vector.tensor_mul(out=cfm[:, m, :, HALO:W],
                                                       in0=a_sb[:, 0:TL].rearrange("p (i t) -> p i t", i=NSLOT),
                                                       in1=sg_sb[:, 0:TL].rearrange("p (i t) -> p i t", i=NSLOT)),
                  reads=[aB, sgB], writes=[cfmB[m]])
            cx.op(cx.dve, lambda: nc.vector.tensor_mul(out=cfm[:, m, :, 0:HALO],
                                                       in0=a_sb[:, TL:NCOL].rearrange("p (i t) -> p i t", i=NSLOT),
                                                       in1=sg_sb[:, TL:NCOL].rearrange("p (i t) -> p i t", i=NSLOT)),
                  reads=[aB, sgB], writes=[cfmB[m]])

    for gh in range(2):
        wq, wqb = ws.load_q(w_in, 2048 + gh * 512)
        for r in range(4):
            lhs_fn = lambda k: wq[:, k, r * 128:(r + 1) * 128]
            for tb in range(2):
                pb = pscnt[0] % 7; pscnt[0] += 1
                for k in range(DC):
                    cx.op(cx.pe, lambda: nc.tensor.matmul(ps[pb][:], lhsT=lhs_fn(k),
                                                          rhs=xnO[:, k, tb * 512:(tb + 1) * 512],
                                                          start=(k == 0), stop=(k == DC - 1)),
                          reads=[wqb, xnB[k]], writes=[psB[pb]], mark=(k == DC - 1))
                cx.op(cx.act, lambda: nc.scalar.copy(out=qT[:, tb * 4:tb * 4 + 4, gh * 4 + r, :],
                                                     in_=ps[pb][:].rearrange("p (a b) -> p a b", a=4)),
                      reads=[psB[pb]], writes=[qB])

    kind_of = {(0, 0): 0, (0, 1): 1, (1, 0): 2, (2, 0): 3}
    for blk in range(3):
        wk, wkb = ws.load(w_in, 3072 + blk * 512, 512)
        for hf in range(2):
            if (blk, hf) in kind_of:
                kind = kind_of[(blk, hf)]
                for gh in range(2):
                    c0 = hf * 256 + gh * 128
                    for tb in range(2):
                        pb = pscnt[0] % 7; pscnt[0] += 1
                        for k in range(DC):
                            cx.op(cx.pe, lambda: nc.tensor.matmul(ps[pb][:], lhsT=wk[:, k, c0:c0 + 128],
                                                                  rhs=xnO[:, k, tb * 512:(tb + 1) * 512],
                                                                  start=(k == 0), stop=(k == DC - 1)),
                                  reads=[wkb, xnB[k]], writes=[psB[pb]], mark=(k == DC - 1))
                        cx.op(cx.act, lambda: nc.scalar.copy(out=kfm[:, kind, tb * 4:tb * 4 + 4, gh, :],
                                                             in_=ps[pb][:].rearrange("p (a b) -> p a b", a=4)),
                              reads=[psB[pb]], writes=[kfB])
            else:
                vk = 0 if blk == 1 else 1
                for i in range(NSLOT):
                    pb = pscnt[0] % 7; pscnt[0] += 1
                    for k in range(DC):
                        cx.op(cx.pe, lambda: nc.tensor.matmul(ps[pb][:, 0:256], lhsT=xnO[:, k, i * 128:(i + 1) * 128],
                                                              rhs=wk[:, k, 256:512],
                                                              start=(k == 0), stop=(k == DC - 1)),
                              reads=[wkb, xnB[k]], writes=[psB[pb]], mark=(k == DC - 1))
                    cx.op(cx.act, lambda: nc.scalar.copy(out=vtm[:, vk, i, :], in_=ps[pb][:, 0:256]),
                          reads=[psB[pb]], writes=[vtB])
    wgl, wglb = ws.load(w_in, 4608, 48)
    for i in range(NSLOT):
        pb = pscnt[0] % 7; pscnt[0] += 1
        for k in range(DC):
            cx.op(cx.pe, lambda: nc.tensor.matmul(ps[pb][:, 0:48], lhsT=xnO[:, k, i * 128:(i + 1) * 128], rhs=wgl[:, k, 0:48],
                                                  start=(k == 0), stop=(k == DC - 1)),
                  reads=[wglb, xnB[k]], writes=[psB[pb]], mark=(k == DC - 1))
        cx.op(cx.act, lambda: nc.scalar.activation(out=gat[:, i, :], in_=ps[pb][:, 0:48], func=AF.Sigmoid),
              reads=[psB[pb]], writes=[gatB])
    cx.dma(cx.sp, [(qT_o, qT[:])], reads=[qB], is_out=True)
    cx.dma(cx.sp, [(kfm_o, kfm[:]), (vtm_o, vtm[:])], reads=[kfB, vtB], writes=[kvB], is_out=True)
    if T.get("after_kv"):
        T["after_kv"]()
    cx.dma(cx.sp, [(gates_o, gat[:])], reads=[gatB], is_out=True)

    cx.barrier()
    sB.close()
    s1.close()
    sC = _scope(nc)
    acc = _sb(sC, "acc", [128, 8, NSLOT, 128], F32); accB = [Buf() for _ in range(8)]
    cT = _sb(sC, "cT", [128, 8, TL], BF16); cTB = Buf()
    mean_sb = _sb(sC, "mean_sb", [128, TL], F32); meanB = Buf()
    sq = [_sb(sC, "sqc%d" % j, [128, TL], F32) for j in range(2)]; sqB = [Buf(), Buf()]
    rstd = _sb(sC, "rstdc", [128, TL], F32); rstdB = Buf()
    for m in range(8):
        cx.op(cx.dve, lambda: nc.vector.tensor_scalar(out=acc[:, m], in0=cfm[:, m, :, HALO:W],
                                                      scalar1=cw_sb[:, m, 30:31], scalar2=cp_sb[:, 0, m:m + 1],
                                                      op0=ALU.mult, op1=ALU.add),
              reads=[cfmB[m], gB], writes=[accB[m]])
        for k in range(1, 31):
            cx.op(cx.dve, lambda: nc.vector.scalar_tensor_tensor(out=acc[:, m], in0=cfm[:, m, :, HALO - k:W - k],
                                                                 scalar=cw_sb[:, m, 30 - k:31 - k], in1=acc[:, m],
                                                                 op0=ALU.mult, op1=ALU.add),
                  reads=[cfmB[m], accB[m]], writes=[accB[m]])
    for m in range(8):
        j = m % 2
        af = acc[:, m].rearrange("p i w -> p (i w)")
        cx.op(cx.act, lambda: nc.scalar.activation(out=sq[j][:, 0:TL], in_=af, func=AF.Square),
              reads=[accB[m]], writes=[sqB[j]])
        for tb in range(2):
            cx.op(cx.pe, lambda: nc.tensor.matmul(ps[tb][:], lhsT=ones1k[:], rhs=af[:, tb * 512:(tb + 1) * 512],
                                                  start=(m == 0), stop=(m == 7)),
                  reads=[accB[m], onesB], writes=[psB[tb]])
            cx.op(cx.pe, lambda: nc.tensor.matmul(ps[2 + tb][:], lhsT=ones1k[:], rhs=sq[j][:, tb * 512:(tb + 1) * 512],
                                                  start=(m == 0), stop=(m == 7)),
                  reads=[sqB[j], onesB], writes=[psB[2 + tb]])
    for tb in range(2):
        sl = slice(tb * 512, (tb + 1) * 512)
        cx.op(cx.act, lambda: nc.scalar.copy(out=mean_sb[:, sl], in_=ps[tb][:]), reads=[psB[tb]], writes=[meanB])
        cx.op(cx.dve, lambda: nc.vector.tensor_mul(out=rstd[:, sl], in0=mean_sb[:, sl], in1=mean_sb[:, sl]),
              reads=[meanB], writes=[rstdB])
        cx.op(cx.dve, lambda: nc.vector.tensor_sub(out=rstd[:, sl], in0=ps[2 + tb][:], in1=rstd[:, sl]),
              reads=[psB[2 + tb], rstdB], writes=[rstdB])
    cx.op(cx.act, lambda: nc.scalar.activation(out=rstd[:, 0:TL], in_=rstd[:, 0:TL], func=AF.Sqrt,
                                               bias=epsT[:, 0:1], scale=1.0), reads=[rstdB], writes=[rstdB])
    cx.op(cx.dve, lambda: nc.vector.reciprocal(out=rstd[:, 0:TL], in_=rstd[:, 0:TL]), reads=[rstdB], writes=[rstdB])
    for m in range(8):
        j = m % 2
        af = acc[:, m].rearrange("p i w -> p (i w)")
        cx.op(cx.dve, lambda: nc.vector.tensor_sub(out=sq[j][:, 0:TL], in0=af, in1=mean_sb[:]),
              reads=[accB[m], meanB], writes=[sqB[j]])
        cx.op(cx.dve, lambda: nc.vector.tensor_mul(out=sq[j][:, 0:TL], in0=sq[j][:, 0:TL], in1=rstd[:, 0:TL]),
              reads=[sqB[j], rstdB], writes=[sqB[j]])
        cx.op(cx.act, lambda: nc.scalar.activation(out=cT[:, m, :], in_=sq[j][:, 0:TL], func=AF.Silu,
                                                   bias=cp_sb[:, 2, m:m + 1], scale=cp_sb[:, 1, m:m + 1]),
              reads=[sqB[j], gB], writes=[cTB])
    cx.dma(cx.sp, [(cT_o, cT[:])], reads=[cTB], is_out=True)
    cx.barrier()
    sC.close()
    s0.close()


def mlp_stage(cx, st, xT, xTB, g_sb, w1_ap, w2_ap, ps, psB, onesD, epsT):
    nc = cx.nc
    xnT = _sb(st, "m_xnT", [128, DC, TL], BF16); xnB = [Buf() for _ in range(DC)]
    sq = [_sb(st, "m_sq%d" % j, [128, TL], F32) for j in range(2)]; sqB = [Buf(), Buf()]
    rstd = _sb(st, "m_rstd", [128, TL], F32); rstdB = Buf()
    hT = [_sb(st, "m_hT%d" % j, [128, 8, TL], BF16) for j in range(1)]; hB = [[Buf() for _ in range(8)] for _ in range(1)]
    tmp = [_sb(st, "m_tmp%d" % j, [128, 512], F32) for j in range(2)]; tmpB = [Buf(), Buf()]
    ws = WStream(cx, st, "m_w", DC, 512, 3)
    scratch = dict(sq=sq, sqb=sqB, ps=ps[0:2], psb=psB[0:2], rstd=rstd, rstdb=rstdB, eps=epsT)
    rms_norm_fm(cx, lambda k: xT[:, k, :], xTB, TL, g_sb, lambda k: xnT[:, k, :], xnB, onesD, scratch)
    cnt = 0
    NPS = len(ps)
    for fb in range(8):
        hj = 0
        for wt in range(2):
            w, wb = ws.load(w1_ap, fb * 1024 + wt * 512, 512)
            for m in range(4):
                fch = wt * 4 + m
                for tb in range(2):
                    pb = cnt % NPS; cnt += 1
                    for k in range(DC):
                        cx.op(cx.pe, lambda: nc.tensor.matmul(ps[pb][:], lhsT=w[:, k, m * 128:(m + 1) * 128],
                                                              rhs=xnT[:, k, tb * 512:(tb + 1) * 512],
                                                              start=(k == 0), stop=(k == DC - 1)),
                              reads=[wb, xnB[k]], writes=[psB[pb]], mark=(k == DC - 1))
                    tj = cnt % 2
                    cx.op(cx.act, lambda: nc.scalar.activation(out=tmp[tj][:], in_=ps[pb][:], func=AF.Square),
                          reads=[psB[pb]], writes=[tmpB[tj]])
                    cx.op(cx.dve, lambda: nc.vector.scalar_tensor_tensor(
                        out=hT[hj][:, fch, tb * 512:(tb + 1) * 512], in0=ps[pb][:], scalar=0.0, in1=tmp[tj][:],
                        op0=ALU.is_gt, op1=ALU.mult), reads=[psB[pb], tmpB[tj]], writes=[hB[hj][fch]])
        for wt in range(4):
            w, wb = ws.load(w2_ap, wt * 512, 512, kch=8, r0=fb * 1024)
            for m in range(4):
                och = wt * 4 + m
                for tb in range(2):
                    pb = cnt % NPS; cnt += 1
                    for k in range(8):
                        cx.op(cx.pe, lambda: nc.tensor.matmul(ps[pb][:], lhsT=w[:, k, m * 128:(m + 1) * 128],
                                                              rhs=hT[hj][:, k, tb * 512:(tb + 1) * 512],
                                                              start=(k == 0), stop=(k == 7)),
                              reads=[wb, hB[hj][k]], writes=[psB[pb]], mark=(k == 7))
                    cx.op(cx.dve, lambda: nc.vector.tensor_add(out=xT[:, och, tb * 512:(tb + 1) * 512], in0=ps[pb][:],
                                                               in1=xT[:, och, tb * 512:(tb + 1) * 512]),
                          reads=[psB[pb]], writes=[xTB[och]])


def emit_p2(nc, cx, T, ps, psB, ptr, ptrB, debug=False):
    qT_d, gates_d, cT_d, xT_d = T["qT"], T["gates"], T["cT"], T["xT"]
    KG, VG, gB_ = T["kfm_g"], T["vtm_g"], T["kvgB"]
    cw1_d = [T["cw1_k"], T["cw1_v"]]
    cw2_d = [T["cw2_k"], T["cw2_v"]]
    posT_d, cmask_d, dzmask_d, wmask_d = T["posT"], T["cmask"], T["dzmask"], T["wmask"]
    esel_d, bonus_d, ovl_d, identb_d = T["esel"], T["bonus"], T["ovl"], T["identb"]
    w_out_d, w1_d, w2_d, gm_d = T["w_out"], T["w_mlp_in0"], T["w_mlp_out0"], T["g_mlp0"]
    xT_o = T["xT2"]
    if debug:
        oT_o, kc_o, vc_o = T["oT_dbg"], T["kc_dbg"], T["vc_dbg"]

    s0 = _scope(nc)
    identb = _sb(s0, "identb", [128, 128], BF16); cB = Buf()
    onesD = _sb(s0, "onesD", [128, 128], F32)
    epsT = _sb(s0, "epsT", [128, 1], F32)
    gm_sb = _sb(s0, "gm", [128, DC], F32)
    cx.dma(cx.sp, [(identb[:], identb_d), (gm_sb[:], gm_d)], writes=[cB])
    cx.op(cx.dve, lambda: nc.vector.memset(onesD[:], 1.0 / D), writes=[cB])
    cx.op(cx.dve, lambda: nc.vector.memset(epsT[:], EPS), writes=[cB])
    oT = _sb(s0, "oT", [128, 8, TL], BF16); oTB = Buf()

    sAtt = _scope(nc)
    kcT = _sb(sAtt, "kcT", [128, 2, 512], BF16); kcB = Buf()
    vcx = _sb(sAtt, "vcx", [128, 4, 4, 193], BF16); vcB = Buf()
    cx.op(cx.dve, lambda: nc.vector.memset(kcT[:], 0.0), writes=[kcB])
    cx.op(cx.dve, lambda: nc.vector.memset(vcx[:], 0.0), writes=[vcB])

    sCmp = _scope(nc)
    raw = _sb(sCmp, "raw", [128, 2, S], BF16); rawB = Buf()
    w1s = _sb(sCmp, "w1s", [128, 32, 128], BF16); w1B = Buf()
    w2k = _sb(sCmp, "w2k", [128, 128], BF16)
    w2v = _sb(sCmp, "w2v", [128, 64], BF16); w2B = Buf()
    posT = _sb(sCmp, "posT", [64, 2, 32], BF16); posB = Buf()
    hTs = [_sb(sCmp, "hTs%d" % j, [128, 512], BF16) for j in range(2)]; hTB = [Buf(), Buf()]
    bias_sb = _sb(sCmp, "cbias", [128, 2], F32); biasB = Buf()
    ovl_sb = _sb(sCmp, "ovl", [128, 4, 128], BF16); ovlB = Buf()
    cx.dma(cx.pool, [(posT[:], posT_d)], writes=[posB])
    cx.dma(cx.pool, [(w2k[:, 0:64], cw2_d[0]), (w2k[:, 64:128], cw2_d[0]), (w2v[:], cw2_d[1])], writes=[w2B])
    cx.dma(cx.sp, [(ovl_sb[:], ovl_d)], writes=[ovlB])
    for j in range(2):
        cx.op(cx.dve, lambda: nc.vector.memset(hTs[j][:], 0.0), writes=[hTB[j]])
    hcnt = 0
    for kind in range(2):
        cx.dma(cx.sp, [(raw[:, gh, :].rearrange("p (i r t) -> p r i t", r=8, t=128)[:, r], KG[r, :, kind, :, gh, :])
                       for gh in range(2) for r in range(8)], reads=[gB_], writes=[rawB])
        w1src = cw1_d[kind].rearrange("(l d) j -> d l j", d=64)
        cx.dma(cx.pool, [(w1s[0:64], w1src), (w1s[64:128], w1src)], writes=[w1B])
        for l in range(32):
            cx.op(cx.pe, lambda: nc.tensor.matmul(ps[6][:, 0:1], lhsT=w1s[0:64, l, :], rhs=posT[:, kind, l:l + 1],
                                                  start=(l == 0), stop=(l == 31)),
                  reads=[w1B, posB], writes=[psB[6]], mark=(l == 31))
        cx.op(cx.act, lambda: nc.scalar.copy(out=bias_sb[:, kind:kind + 1], in_=ps[6][:, 0:1]),
              reads=[psB[6]], writes=[biasB])
        for g in range(4):
            pb0 = 64 * (g % 2); gh = g // 2
            hp = hcnt % 2; hcnt += 1
            for l in range(32):
                cx.op(cx.pe, lambda: nc.tensor.matmul(ps[hp][:, 0:511], lhsT=w1s[pb0:pb0 + 64, l, :],
                                                      rhs=raw[pb0:pb0 + 64, gh, l:l + 16 * 510 + 1:16],
                                                      start=(l == 0), stop=(l == 31)),
                      reads=[w1B, rawB], writes=[psB[hp]], mark=(l == 31))
            cx.op(cx.act, lambda: nc.scalar.activation(out=hTs[hp][:, 0:511], in_=ps[hp][:, 0:511], func=AF.Silu,
                                                       bias=bias_sb[:, kind:kind + 1], scale=1.0),
                  reads=[psB[hp], biasB], writes=[hTB[hp]])
            if kind == 0:
                cx.op(cx.pe, lambda: nc.tensor.matmul(ps[2 + hp][:, 0:511], lhsT=w2k[:], rhs=hTs[hp][:, 0:511],
                                                      start=True, stop=True),
                      reads=[w2B, hTB[hp]], writes=[psB[2 + hp]])
                cx.op(cx.dve, lambda: nc.vector.tensor_copy(out=kcT[pb0:pb0 + 64, gh, 0:511],
                                                            in_=ps[2 + hp][pb0:pb0 + 64, 0:511]),
                      reads=[psB[2 + hp]], writes=[kcB])
            else:
                for nt in range(4):
                    cx.op(cx.pe, lambda: nc.tensor.matmul(ps[2 + hp][:, nt * 64:(nt + 1) * 64],
                                                          lhsT=hTs[hp][:, nt * 128:(nt + 1) * 128], rhs=w2v[:],
                                                          start=True, stop=True),
                          reads=[w2B, hTB[hp]], writes=[psB[2 + hp]], mark=(nt == 3))
                cx.op(cx.dve, lambda: nc.vector.tensor_copy(out=vcx[:, :, g, 0:64],
                                                            in_=ps[2 + hp][:, 0:256].rearrange("p (a b) -> p a b", a=4)),
                      reads=[psB[2 + hp]], writes=[vcB])
    for g in range(4):
        cx.op(cx.dve, lambda: nc.vector.tensor_copy(out=vcx[:, :, g, 64:192], in_=ovl_sb[:]), reads=[ovlB], writes=[vcB])
    cx.op(cx.dve, lambda: nc.vector.memset(vcx[:, :, :, 192:193], 1.0), writes=[vcB])
    if debug:
        cx.dma(cx.sp, [(kc_o, kcT[:])], reads=[kcB], is_out=True)
        cx.dma(cx.sp, [(vc_o, vcx[:])], reads=[vcB], is_out=True)
    cx.barrier()
    sCmp.close()

    sB = _scope(nc)
    ks = _sb(sB, "ks", [128, 64, 2, 128], BF16); ksB = Buf()
    vsx = _sb(sB, "vsx", [128, 64, 4, 65], BF16); vsB = Buf()
    kw = [_sb(sB, "kw%d" % j, [128, 12, 2, 128], BF16) for j in range(2)]; kwB = [Buf(), Buf()]
    vwx = [_sb(sB, "vwx%d" % j, [128, 12, 4, 65], BF16) for j in range(2)]; vwB = [Buf(), Buf()]
    esel = _sb(sB, "esel", [128, 64, 128], BF16)
    cmask = _sb(sB, "cmask", [128, NSLOT, 4, 128], BF16)
    dzmask = _sb(sB, "dzmask", [128, 8, 128], BF16)
    wmask = _sb(sB, "wmask", [128, 12, 128], BF16); mB = Buf()
    bonus = _sb(sB, "bonus", [128, NSLOT, 128], F32)
    qT = _sb(sB, "qT", [128, NSLOT, 8, 128], BF16)
    gat = _sb(sB, "gat", [128, NSLOT, 48], F32); qB = Buf()
    pT = [_sb(sB, "pT%d" % j, [128, 512], BF16) for j in range(3)]; pTB = [Buf() for _ in range(3)]
    imp = [_sb(sB, "imp%d" % j, [128, 128], F32) for j in range(2)]; impB = Buf()
    m8 = _sb(sB, "m8", [128, 16], F32)
    negq = _sb(sB, "negq", [128, 128], BF16); negqB = Buf()
    negT = _sb(sB, "negT", [128, 128], BF16); negTB = Buf()
    rz = _sb(sB, "rz", [128, 12], F32); rzB = Buf()
    oacc = _sb(sB, "oacc", [128, 4, 64], F32); oaccB = Buf()
    otok = _sb(sB, "otok", [128, 1024], BF16); otokB = Buf()

    cx.op(cx.dve, lambda: nc.vector.memset(vsx[:, :, :, 64:65], 1.0), writes=[vsB])
    for j in range(2):
        cx.op(cx.dve, lambda: nc.vector.memset(vwx[j][:, :, :, 64:65], 1.0), writes=[vwB[j]])
    cx.dma(cx.sp, [(esel[:], esel_d), (cmask[:], cmask_d), (dzmask[:], dzmask_d), (wmask[:], wmask_d),
                   (bonus[:], bonus_d)], writes=[mB])
    cx.dma(cx.sp, [(qT[:], qT_d), (gat[:], gates_d)], writes=[qB])
    for kb in range(8):
        sl = slice(kb * 8, kb * 8 + 8)
        cx.dma(cx.sp, [(ks[:, sl].rearrange("p r g t -> p r (g t)"), KG[:, :, 2, kb].rearrange("r p g t -> p r (g t)"))],
               reads=[gB_], writes=[ksB])
        cx.dma(cx.sp, [(vsx[:, kb * 8 + r, :, 0:64], VG[r, :, 0, kb, :].rearrange("p (g e) -> p g e", g=4))
                       for r in range(8)], reads=[gB_], writes=[vsB])

    scnt = 0
    pcnt = 0
    UC = [ps[2], ps[3]]; UCB = [psB[2], psB[3]]
    US, USB = ps[4], psB[4]
    UW, UWB = ps[5], psB[5]

    def score_tile(lhsT_k, rhs_q, masks, kreads):
        nonlocal scnt, pcnt
        sb_ = scnt % 2; scnt += 1
        pj = pcnt % 3; pcnt += 1
        cx.op(cx.pe, lambda: nc.tensor.matmul(ps[sb_][:], lhsT=lhsT_k, rhs=rhs_q, start=True, stop=False),
              reads=kreads + [qB], writes=[psB[sb_]], mark=False)
        for mi, (ml, mr, mrd) in enumerate(masks):
            last = (mi == len(masks) - 1)
            cx.op(cx.pe, lambda: nc.tensor.matmul(ps[sb_][:].rearrange("p (h n) -> p h n", h=4), lhsT=ml,
                                                  rhs=mr.unsqueeze(1).to_broadcast([128, 4, 128]),
                                                  start=False, stop=last),
                  reads=mrd, writes=[psB[sb_]], mark=last)
        cx.op(cx.act, lambda: nc.scalar.activation(out=pT[pj][:], in_=ps[sb_][:], func=AF.Exp, scale=SCALE),
              reads=[psB[sb_]], writes=[pTB[pj]])
        return pT[pj], pTB[pj]

    for i in range(NSLOT):
        wj = i % 2
        lo = max(0, 8 * i - 4); r0 = lo - (8 * i - 4)
        nt_w = 8 * i + 8 - lo
        cx.dma(cx.sp, [(kw[wj][:, r0 + q_].rearrange("p g t -> p (g t)"),
                        KG[(lo + q_) % 8, :, 3, (lo + q_) // 8].rearrange("p g t -> p (g t)")) for q_ in range(nt_w)],
               reads=[gB_], writes=[kwB[wj]])
        cx.dma(cx.sp, [(vwx[wj][:, r0 + q_, :, 0:64],
                        VG[(lo + q_) % 8, :, 1, (lo + q_) // 8, :].rearrange("p (g e) -> p g e", g=4)) for q_ in range(nt_w)],
               reads=[gB_], writes=[vwB[wj]])
        for g in range(4):
            pb0 = 64 * (g % 2); gh = g // 2
            QT = qT[pb0:pb0 + 64, i, gh * 4:gh * 4 + 4, :].rearrange("p a b -> p (a b)")
            NT = i // 2 + 1
            for nt in range(NT):
                p_, pB_ = score_tile(kcT[pb0:pb0 + 64, gh, nt * 128:(nt + 1) * 128], QT,
                                     [(identb[:], cmask[:, i, nt, :], [mB, cB])], [kcB])
                for r in range(4):
                    cx.op(cx.pe, lambda: nc.tensor.matmul(UC[r // 2][:, (r % 2) * 193:(r % 2) * 193 + 193],
                                                          lhsT=p_[:, r * 128:(r + 1) * 128], rhs=vcx[:, nt, g, :],
                                                          start=(nt == 0 and r % 2 == 0), stop=(nt == NT - 1),
                                                          skip_group_check=True),
                          reads=[pB_, vcB], writes=[UCB[r // 2]], mark=(r % 2 == 1))
            for r in range(4):
                cx.op(cx.dve, lambda: nc.vector.tensor_scalar(out=rz[:, r:r + 1],
                                                              in0=UC[r // 2][:, (r % 2) * 193 + 192:(r % 2) * 193 + 193],
                                                              scalar1=1e-30, scalar2=None, op0=ALU.max),
                      reads=[UCB[r // 2]], writes=[rzB])
            cx.op(cx.dve, lambda: nc.vector.reciprocal(out=rz[:, 0:4], in_=rz[:, 0:4]), reads=[rzB], writes=[rzB])
            for r in range(0):
                cx.op(cx.dve, lambda: nc.vector.reciprocal(out=rz[:, r:r + 1],
                                                           in_=UC[r // 2][:, (r % 2) * 193 + 192:(r % 2) * 193 + 193]),
                      reads=[UCB[r // 2]], writes=[rzB])
            im = imp[0]
            cx.op(cx.dve, lambda: nc.vector.tensor_scalar(out=im[:], in0=UC[0][:, 64:192], scalar1=rz[:, 0:1], scalar2=None,
                                                          op0=ALU.mult), reads=[UCB[0], rzB], writes=[impB])
            for r in range(1, 4):
                cx.op(cx.dve, lambda: nc.vector.scalar_tensor_tensor(
                    out=im[:], in0=UC[r // 2][:, (r % 2) * 193 + 64:(r % 2) * 193 + 192], scalar=rz[:, r:r + 1],
                    in1=im[:], op0=ALU.mult, op1=ALU.add), reads=[UCB[r // 2], rzB, impB], writes=[impB])
            cx.op(cx.dve, lambda: nc.vector.tensor_add(out=im[:], in0=im[:], in1=bonus[:, i, :]),
                  reads=[impB, mB], writes=[impB])
            cx.op(cx.dve, lambda: nc.vector.max(out=m8[:, 0:8], in_=im[:]), reads=[impB], writes=[rzB])
            cx.op(cx.dve, lambda: nc.vector.match_replace(out=imp[1][:], in_to_replace=m8[:, 0:8], in_values=im[:],
                                                          imm_value=-1e30), reads=[impB, rzB], writes=[impB])
            cx.op(cx.dve, lambda: nc.vector.max(out=m8[:, 8:16], in_=imp[1][:]), reads=[impB], writes=[rzB])
            cx.op(cx.dve, lambda: nc.vector.tensor_scalar(out=negq[:], in0=im[:], scalar1=m8[:, 15:16], scalar2=NEG,
                                                          op0=ALU.is_lt, op1=ALU.mult),
                  reads=[impB, rzB], writes=[negqB])
            cx.op(cx.pe, lambda: nc.tensor.transpose(out=ptr[:, 0:128], in_=negq[:], identity=identb[:]),
                  reads=[negqB, cB], writes=[ptrB])
            cx.op(cx.act, lambda: nc.scalar.copy(out=negT[:], in_=ptr[:, 0:128]), reads=[ptrB], writes=[negTB])
            gsl = lambda b: gat[:, i, g * 12 + b:g * 12 + 12:3]
            cx.op(cx.dve, lambda: nc.vector.tensor_mul(out=rz[:, 4:8], in0=rz[:, 0:4], in1=gsl(0)),
                  reads=[rzB, qB], writes=[rzB])
            for r in range(4):
                cx.op(cx.dve, lambda: nc.vector.tensor_scalar(out=oacc[:, r, :],
                                                              in0=UC[r // 2][:, (r % 2) * 193:(r % 2) * 193 + 64],
                                                              scalar1=rz[:, 4 + r:5 + r], scalar2=None, op0=ALU.mult),
                      reads=[UCB[r // 2], rzB], writes=[oaccB])
            nkt = 8 * i + 8
            for kt in range(nkt):
                masks = [(esel[:, kt, :], negT[:], [mB, negTB])]
                if kt >= 8 * i:
                    masks.append((identb[:], dzmask[:, kt - 8 * i, :], [mB, cB]))
                p_, pB_ = score_tile(ks[pb0:pb0 + 64, kt, gh, :], QT, masks, [ksB])
                for r in range(4):
                    cx.op(cx.pe, lambda: nc.tensor.matmul(US[:, r * 65:(r + 1) * 65], lhsT=p_[:, r * 128:(r + 1) * 128],
                                                          rhs=vsx[:, kt, g, :], start=(kt == 0 and r == 0),
                                                          stop=(kt == nkt - 1), skip_group_check=True),
                          reads=[pB_, vsB], writes=[USB], mark=(r == 3))
            for rho in range(r0, 12):
                p_, pB_ = score_tile(kw[wj][pb0:pb0 + 64, rho, gh, :], QT,
                                     [(identb[:], wmask[:, rho, :], [mB, cB])], [kwB[wj]])
                for r in range(4):
                    cx.op(cx.pe, lambda: nc.tensor.matmul(UW[:, r * 65:(r + 1) * 65], lhsT=p_[:, r * 128:(r + 1) * 128],
                                                          rhs=vwx[wj][:, rho, g, :], start=(rho == r0 and r == 0),
                                                          stop=(rho == 11), skip_group_check=True),
                          reads=[pB_, vwB[wj]], writes=[UWB], mark=(r == 3))
            for bi, (U, UB) in enumerate(((US, USB), (UW, UWB))):
                cx.op(cx.dve, lambda: nc.vector.reciprocal(out=rz[:, 8:12], in_=U[:, 64:260:65]),
                      reads=[UB], writes=[rzB])
                cx.op(cx.dve, lambda: nc.vector.tensor_mul(out=rz[:, 8:12], in0=rz[:, 8:12], in1=gsl(1 + bi)),
                      reads=[rzB, qB], writes=[rzB])
                for r in range(4):
                    dst = oacc[:, r, :] if bi == 0 else otok[:, g * 256 + r * 64:g * 256 + (r + 1) * 64]
                    cx.op(cx.dve, lambda: nc.vector.scalar_tensor_tensor(out=dst, in0=U[:, r * 65:r * 65 + 64],
                                                                         scalar=rz[:, 8 + r:9 + r], in1=oacc[:, r, :],
                                                                         op0=ALU.mult, op1=ALU.add),
                          reads=[UB, rzB, oaccB], writes=[oaccB, otokB])
        for m in range(8):
            cx.op(cx.pe, lambda: nc.tensor.transpose(out=ptr[:, m * 128:(m + 1) * 128], in_=otok[:, m * 128:(m + 1) * 128],
                                                     identity=identb[:]),
                  reads=[otokB, cB], writes=[ptrB], mark=(m == 7))
        cx.op(cx.act, lambda: nc.scalar.copy(out=oT[:, :, i * 128:(i + 1) * 128],
                                             in_=ptr[:].rearrange("p (a b) -> p a b", a=8)),
              reads=[ptrB], writes=[oTB])
    if debug:
        cx.dma(cx.sp, [(oT_o, oT[:])], reads=[oTB], is_out=True)
    cx.barrier()
    sB.close()
    sAtt.close()

    sX = _scope(nc)
    xT = _sb(sX, "xT", [128, DC, TL], F32); xTB = [Buf() for _ in range(DC)]
    cx.dma(cx.sp, [(xT[:, 0:8], xT_d[:, 0:8])], writes=xTB[0:8])
    cx.dma(cx.sp, [(xT[:, 8:16], xT_d[:, 8:16])], writes=xTB[8:16])
    sC = _scope(nc)
    cT = _sb(sC, "cT", [128, 8, TL], BF16); cTB = Buf()
    cx.dma(cx.sp, [(cT[:], cT_d)], writes=[cTB])
    wso = WStream(cx, sC, "wo", DC, 512, 2)
    cnt = 0
    for wt in range(4):
        w, wb = wso.load(w_out_d, wt * 512, 512)
        for m in range(4):
            och = wt * 4 + m
            for tb in range(2):
                pb = cnt % 7; cnt += 1
                for k in range(DC):
                    rhs = cT[:, k, tb * 512:(tb + 1) * 512] if k < 8 else oT[:, k - 8, tb * 512:(tb + 1) * 512]
                    cx.op(cx.pe, lambda: nc.tensor.matmul(ps[pb][:], lhsT=w[:, k, m * 128:(m + 1) * 128], rhs=rhs,
                                                          start=(k == 0), stop=(k == DC - 1)),
                          reads=[wb, cTB, oTB], writes=[psB[pb]], mark=(k == DC - 1))
                cx.op(cx.dve, lambda: nc.vector.tensor_add(out=xT[:, och, tb * 512:(tb + 1) * 512], in0=ps[pb][:],
                                                           in1=xT[:, och, tb * 512:(tb + 1) * 512]),
                      reads=[psB[pb]], writes=[xTB[och]])
    cx.barrier()
    sC.close()

    sM = _scope(nc)
    mlp_stage(cx, sM, xT, xTB, gm_sb, w1_d, w2_d, ps, psB, onesD, epsT)
    cx.dma(cx.sp, [(xT_o, xT[:])], reads=xTB, is_out=True)
    if T.get("halo_out") is not None:
        cx.dma(cx.sp, [(T["halo_out"][:, k], xT[:, k, :].rearrange("p (i t) -> p i t", i=NSLOT)[:, :, 128 - PH:128])
                       for k in range(DC)], reads=xTB, writes=[T["haloB"]], is_out=True)
        T["after_halo"]()
    cx.barrier()
    sM.close(); sX.close(); s0.close()


PH = 16
PW = PH + 128


def emit_p3(nc, cx, T, ps, psB):
    xT_d, icnt_d, pw_d, gv_d, identf_d = T["xT2"], T["invcnt"], T["pool_w"], T["gvec"], T["ident"]
    w1_d, w2_d, out_o = T["w_mlp_in1"], T["w_mlp_out1"], T["out"]

    s0 = _scope(nc)
    identf = _sb(s0, "identf", [128, 128], F32); cB = Buf()
    onesD = _sb(s0, "onesD", [128, 128], F32)
    epsT = _sb(s0, "epsT", [128, 1], F32)
    gv = _sb(s0, "gv", [128, 4, DC], F32)
    cx.dma(cx.sp, [(identf[:], identf_d), (gv[:], gv_d)], writes=[cB])
    cx.op(cx.dve, lambda: nc.vector.memset(onesD[:], 1.0 / D), writes=[cB])
    cx.op(cx.dve, lambda: nc.vector.memset(epsT[:], EPS), writes=[cB])
    xT = _sb(s0, "xT", [128, DC, TL], F32); xTB = [Buf() for _ in range(DC)]
    cx.dma(cx.sp, [(xT[:, 0:8], xT_d[:, 0:8])], writes=xTB[0:8])
    cx.dma(cx.sp, [(xT[:, 8:16], xT_d[:, 8:16])], writes=xTB[8:16])

    sP = _scope(nc)
    xh = _sb(sP, "xh", [128, DC, NSLOT * PH], F32); xhB = [Buf() for _ in range(DC)]
    T["load_halo"](cx, sP, xh, xhB)
    sq = [_sb(sP, "sq%d" % j, [128, TL], F32) for j in range(2)]; sqB = [Buf(), Buf()]
    rstd = _sb(sP, "rstd", [128, TL], F32); rstdB = Buf()
    rstdh = _sb(sP, "rstdh", [128, NSLOT * PH], F32); rstdhB = Buf()
    icnt = _sb(sP, "icnt", [128, 4, TL], F32); icB = Buf()
    cx.dma(cx.sp, [(icnt[:], icnt_d)], writes=[icB])
    xn = _sb(sP, "xn", [128, 4, NSLOT, PW], F32); xnB = Buf()
    sa = _sb(sP, "sa", [128, 4, NSLOT, PW], F32); saB = Buf()
    sb2 = _sb(sP, "sb2", [128, 4, NSLOT, PW], F32); sbB = Buf()
    pT = _sb(sP, "pT", [128, 4, TL], BF16); pTB = Buf()
    wsp = WStream(cx, sP, "pw", 4, 512, 2)
    rms_stats_fm(cx, lambda k: xT[:, k, :], xTB, TL, onesD,
                 dict(sq=sq, sqb=sqB, ps=ps[0:2], psb=psB[0:2], rstd=rstd, rstdb=rstdB, eps=epsT))
    rms_stats_fm(cx, lambda k: xh[:, k, :], xhB, NSLOT * PH, onesD,
                 dict(sq=sq, sqb=sqB, ps=ps[2:3], psb=psB[2:3], rstd=rstdh, rstdb=rstdhB, eps=epsT))
    cnt = 0
    v8 = lambda ap: ap.rearrange("p (i t) -> p i t", i=NSLOT)
    for gi in range(4):
        w = 2 ** (gi + 1)
        for kk in range(4):
            k = gi * 4 + kk
            rms_apply_fm(cx, k, v8(xT[:, k, :]), xTB[k], gv[:, 0, :], xn[:, kk, :, PH:PW], xnB, v8(rstd[:, :]), rstdB)
            rms_apply_fm(cx, k, v8(xh[:, k, :]), xhB[k], gv[:, 0, :], xn[:, kk, :, 0:PH], xnB, v8(rstdh[:, :]), rstdhB)
        fl = lambda t, lo, hi: t[:].rearrange("p a i w -> p (a i) w")[:, :, lo:hi]
        src, srcB = xn, xnB
        bufs = [(sa, saB), (sb2, sbB)]
        sh = 1
        step = 0
        while sh < w:
            dst, dstB = bufs[step % 2]
            cx.op(cx.dve, lambda: nc.vector.tensor_add(out=fl(dst, sh, PW), in0=fl(src, sh, PW), in1=fl(src, 0, PW - sh)),
                  reads=[srcB], writes=[dstB])
            src, srcB = dst, dstB
            sh *= 2; step += 1
        oth, othB = bufs[step % 2]
        for kk in range(4):
            cx.op(cx.dve, lambda: nc.vector.tensor_mul(out=oth[:, kk, :, PH:PW], in0=src[:, kk, :, PH:PW],
                                                       in1=v8(icnt[:, gi, :])), reads=[srcB, icB], writes=[othB])
            cx.op(cx.dve, lambda: nc.vector.tensor_sub(out=v8(pT[:, kk, :]), in0=oth[:, kk, :, PH:PW],
                                                       in1=xn[:, kk, :, PH:PW]), reads=[othB, xnB], writes=[pTB])
        wt, wb = wsp.load(pw_d[gi], 0, 512, kch=4)
        for m in range(4):
            och = gi * 4 + m
            for tb in range(2):
                pb = 3 + cnt % 4; cnt += 1
                for kk in range(4):
                    cx.op(cx.pe, lambda: nc.tensor.matmul(ps[pb][:], lhsT=wt[:, kk, m * 128:(m + 1) * 128],
                                                          rhs=pT[:, kk, tb * 512:(tb + 1) * 512],
                                                          start=(kk == 0), stop=(kk == 3)),
                          reads=[wb, pTB], writes=[psB[pb]], mark=(kk == 3))
                cx.op(cx.dve, lambda: nc.vector.scalar_tensor_tensor(
                    out=xT[:, och, tb * 512:(tb + 1) * 512], in0=ps[pb][:], scalar=gv[:, 1, och:och + 1],
                    in1=xT[:, och, tb * 512:(tb + 1) * 512], op0=ALU.mult, op1=ALU.add),
                    reads=[psB[pb], cB], writes=[xTB[och]])
    cx.barrier()
    sP.close()

    sM = _scope(nc)
    gm = gv[:, 2, :]
    mlp_stage(cx, sM, xT, xTB, gm, w1_d, w2_d, ps, psB, onesD, epsT)
    cx.barrier()
    sM.close()

    sF = _scope(nc)
    sq = [_sb(sF, "fsq%d" % j, [128, TL], F32) for j in range(2)]; sqB = [Buf(), Buf()]
    rstd = _sb(sF, "frstd", [128, TL], F32); rstdB = Buf()
    otk = [_sb(sF, "otk%d" % j, [128, D], F32) for j in range(2)]; otkB = [Buf(), Buf()]
    rms_norm_fm(cx, lambda k: xT[:, k, :], xTB, TL, gv[:, 3, :], lambda k: xT[:, k, :], xTB, onesD,
                dict(sq=sq, sqb=sqB, ps=ps[0:2], psb=psB[0:2], rstd=rstd, rstdb=rstdB, eps=epsT))
    cnt = 0
    for i in range(NSLOT):
        j = i % 2
        for k4 in range(4):
            pb = 2 + cnt % 5; cnt += 1
            for kk in range(4):
                k = k4 * 4 + kk
                cx.op(cx.pe, lambda: nc.tensor.transpose(out=ps[pb][:, kk * 128:(kk + 1) * 128],
                                                         in_=xT[:, k, i * 128:(i + 1) * 128], identity=identf[:]),
                      reads=[xTB[k], cB], writes=[psB[pb]], mark=(kk == 3))
            eng = cx.act if k4 % 2 else cx.dve
            if k4 % 2:
                cx.op(cx.act, lambda: nc.scalar.copy(out=otk[j][:, k4 * 512:(k4 + 1) * 512], in_=ps[pb][:]),
                      reads=[psB[pb]], writes=[otkB[j]])
            else:
                cx.op(cx.dve, lambda: nc.vector.tensor_copy(out=otk[j][:, k4 * 512:(k4 + 1) * 512], in_=ps[pb][:]),
                      reads=[psB[pb]], writes=[otkB[j]])
        cx.dma(cx.sp, [(out_o[i], otk[j][:])], reads=[otkB[j]], is_out=True)
    cx.barrier()
    sF.close(); s0.close()

def _tile_of(c, i):
    return 8 * i + c


def host_p1_inputs(x, mix_norm, w_in, conv_w, conv_b, conv_ln_g, conv_ln_b):
    x2 = x.reshape(S, D)
    xpad = np.concatenate([np.zeros((HALO, D), np.float32), x2], 0)
    fm = lambda v, n: np.ascontiguousarray(v.reshape(n, 128).T)
    common = {
        "w_in": np.ascontiguousarray(w_in[0]),
        "g0": fm(mix_norm[0], DC),
        "convw": np.ascontiguousarray(conv_w[0].reshape(31, 8, 128).transpose(2, 1, 0)),
        "cpar": np.ascontiguousarray(np.stack([fm(conv_b[0], 8), fm(conv_ln_g[0], 8), fm(conv_ln_b[0], 8)], 1)),
        "ident": np.eye(128, dtype=np.float32),
    }
    maps = []
    for c in range(NCORES):
        xh = np.stack([xpad[128 * _tile_of(c, i):128 * _tile_of(c, i) + HALO + 128] for i in range(NSLOT)], 0)
        m = dict(common)
        m["xh"] = np.ascontiguousarray(xh)
        maps.append(m)
    return maps


BF = ml_dtypes.bfloat16


def host_masks(c):
    kl = np.arange(128)[:, None]
    ql = np.arange(128)[None, :]
    cmask = np.zeros((128, NSLOT, 4, 128), np.float32)
    bonus = np.zeros((128, NSLOT, 128), np.float32)
    for i in range(NSLOT):
        t = 128 * (8 * i + c) + np.arange(128)
        for nt in range(4):
            n = 128 * nt + np.arange(128)
            vis = (16 * n[:, None] + 31 <= t[None, :]) & (n[:, None] <= 510)
            cmask[:, i, nt, :] = np.where(vis, 0.0, NEG)
        cur = t // 64
        j = np.arange(128)[None, :]
        forced = (j == 0) | (j == cur[:, None]) | (j == cur[:, None] - 1)
        bonus[:, i, :] = np.where(forced, 1000.0, 0.0)
    dz = np.zeros((128, 8, 128), np.float32)
    for z in range(8):
        dz[:, z, :] = np.where(128 * z + kl <= 128 * c + ql, 0.0, NEG)
    wm = np.zeros((128, 12, 128), np.float32)
    for rho in range(12):
        dlt = 128 * (c + 4 - rho)
        d = kl - ql
        wm[:, rho, :] = np.where((d <= dlt) & (d > dlt - 512), 0.0, NEG)
    return {"cmask": cmask.astype(BF), "dzmask": dz.astype(BF), "wmask": wm.astype(BF), "bonus": bonus}


def host_shared_consts():
    esel = np.zeros((128, 64, 128), np.float32)
    for kt in range(64):
        for b in range(2):
            esel[2 * kt + b, kt, 64 * b:64 * b + 64] = 1.0
    ovl = np.zeros((128, 4, 128), np.float32)
    for nt in range(4):
        n = 128 * nt + np.arange(128)
        j = np.arange(128)
        ov = (16 * n[:, None] <= 64 * j[None, :] + 63) & (16 * n[:, None] + 31 >= 64 * j[None, :]) & (n[:, None] <= 510)
        ovl[:, nt, :] = ov
    return {"esel": esel.astype(BF), "ovl": ovl.astype(BF), "identb": np.eye(128, dtype=np.float32).astype(BF)}


def _fm(v, n):
    return np.ascontiguousarray(np.asarray(v).reshape(n, 128).T)


def host_inputs(inp):
    m1 = host_p1_inputs(inp["x"], inp["mix_norm"], inp["w_in"], inp["conv_w"], inp["conv_b"],
                        inp["conv_ln_g"], inp["conv_ln_b"])
    shared = host_shared_consts()
    gvec = np.ascontiguousarray(np.stack([_fm(inp["mix_norm"][1], DC), _fm(inp["pool_scale"][0], DC),
                                          _fm(inp["mlp_norm"][1], DC), _fm(inp["final_norm"], DC)], 1))
    shared.update({
        "cw1_k": np.ascontiguousarray(inp["cmp_k_w1"][0]), "cw1_v": np.ascontiguousarray(inp["cmp_v_w1"][0]),
        "cw2_k": np.ascontiguousarray(inp["cmp_k_w2"][0]), "cw2_v": np.ascontiguousarray(inp["cmp_v_w2"][0]),
        "posT": np.ascontiguousarray(np.stack([inp["cmp_pos_k"][0].T, inp["cmp_pos_v"][0].T], 1)),
        "w_out": np.ascontiguousarray(inp["w_out"][0]),
        "w_mlp_in0": np.ascontiguousarray(inp["w_mlp_in"][0]), "w_mlp_out0": np.ascontiguousarray(inp["w_mlp_out"][0]),
        "w_mlp_in1": np.ascontiguousarray(inp["w_mlp_in"][1]), "w_mlp_out1": np.ascontiguousarray(inp["w_mlp_out"][1]),
        "g_mlp0": _fm(inp["mlp_norm"][0], DC),
        "pool_w": np.ascontiguousarray(inp["pool_w"][0]), "gvec": gvec,
    })
    maps = []
    for c in range(NCORES):
        m = dict(m1[c])
        m.update(shared)
        m.update(host_masks(c))
        icnt = np.zeros((4, TL), np.float32)
        for i in range(NSLOT):
            pos1 = 128 * (8 * i + c) + np.arange(128) + 1
            for gi in range(4):
                icnt[gi, i * 128:(i + 1) * 128] = 1.0 / np.minimum(pos1, 2 ** (gi + 1))
        m["invcnt"] = np.ascontiguousarray(np.broadcast_to(icnt[None], (128, 4, TL)))
        sel = np.zeros((128, 9), np.float32)
        if c > 0:
            sel[:, c - 1] = 1.0
        else:
            sel[:, 8] = 1.0
        m["halosel"] = sel
        maps.append(m)
    return maps


def build_fused(stage=3):
    nc = bass.Bass("TRN2", target_bir_lowering=False)
    cx = Ctx(nc)
    din = lambda name, shape, dt: nc.dram_tensor(name, shape, dt, kind="ExternalInput").ap()
    itn = lambda name, shape, dt: nc.dram_tensor(name, shape, dt)
    T = {}
    for name, shape, dt in [
        ("xh", [NSLOT, HALO + 128, D], F32), ("w_in", [D, IN_COLS], F32), ("g0", [128, DC], F32),
        ("convw", [128, 8, 31], F32), ("cpar", [128, 3, 8], F32), ("ident", [128, 128], F32),
        ("cw1_k", [2048, 128], F32), ("cw1_v", [2048, 128], F32), ("cw2_k", [128, 64], F32), ("cw2_v", [128, 64], F32),
        ("posT", [64, 2, 32], F32), ("cmask", [128, NSLOT, 4, 128], BF16), ("dzmask", [128, 8, 128], BF16),
        ("wmask", [128, 12, 128], BF16), ("esel", [128, 64, 128], BF16), ("bonus", [128, NSLOT, 128], F32),
        ("ovl", [128, 4, 128], BF16), ("identb", [128, 128], BF16), ("w_out", [D, D], F32),
        ("w_mlp_in0", [D, DFF], F32), ("w_mlp_out0", [DFF, D], F32), ("g_mlp0", [128, DC], F32),
        ("w_mlp_in1", [D, DFF], F32), ("w_mlp_out1", [DFF, D], F32),
        ("invcnt", [128, 4, TL], F32), ("pool_w", [4, 512, 512], F32), ("gvec", [128, 4, DC], F32),
        ("halosel", [128, 9], F32),
    ]:
        T[name] = din(name, shape, dt)
    T["out"] = nc.dram_tensor("out", [NSLOT, 128, D], F32, kind="ExternalOutput").ap()
    T["xT"] = itn("h_xT", [128, DC, TL], F32).ap()
    T["xT2"] = itn("h_xT2", [128, DC, TL], F32).ap()
    T["qT"] = itn("h_qT", [128, NSLOT, 8, 128], BF16).ap()
    T["cT"] = itn("h_cT", [128, 8, TL], BF16).ap()
    T["gates"] = itn("h_gates", [128, NSLOT, 48], F32).ap()
    KW = 4 * NSLOT * 2 * 128
    VW = 2 * NSLOT * 256
    kvloc = itn("kvloc", [128, (KW + VW) // 2], F32)
    kvgat = itn("kvgat", [NCORES * 128, (KW + VW) // 2], F32)
    lb = kvloc.bitcast(BF16).ap()
    gb = kvgat.bitcast(BF16).ap().rearrange("(r p) c -> r p c", r=NCORES)
    T["kfm"] = lb[:, 0:KW].rearrange("p (k i g t) -> p k i g t", k=4, i=NSLOT, g=2)
    T["vtm"] = lb[:, KW:KW + VW].rearrange("p (k i e) -> p k i e", k=2, i=NSLOT)
    T["kfm_g"] = gb[:, :, 0:KW].rearrange("r p (k i g t) -> r p k i g t", k=4, i=NSLOT, g=2)
    T["vtm_g"] = gb[:, :, KW:KW + VW].rearrange("r p (k i e) -> r p k i e", k=2, i=NSLOT)
    T["kvB"] = Buf(); T["kvgB"] = Buf()
    hloc = itn("hloc", [128, DC * NSLOT * PH], F32)
    hgat = itn("hgat", [NCORES * 128, DC * NSLOT * PH], F32)
    T["halo_out"] = hloc.ap().rearrange("p (k i t) -> p k i t", k=DC, i=NSLOT)
    H = hgat.ap().rearrange("(r p) (k i t) -> r p k i t", r=NCORES, k=DC, i=NSLOT)
    T["haloB"] = Buf(); hgB = Buf()
    rg = [list(range(NCORES))]

    def after_kv():
        cx.op(cx.pool, lambda: nc.gpsimd.collective_compute("AllGather", ALU.bypass, replica_groups=rg,
                                                            ins=[kvloc.ap().opt()], outs=[kvgat.ap().opt()]),
              reads=[T["kvB"]], writes=[T["kvgB"]])

    def after_halo():
        cx.op(cx.pool, lambda: nc.gpsimd.collective_compute("AllGather", ALU.bypass, replica_groups=rg,
                                                            ins=[hloc.ap().opt()], outs=[hgat.ap().opt()]),
              reads=[T["haloB"]], writes=[hgB])

    def load_halo(cx_, st, xh, xhB):
        sel = _sb(st, "hsel", [128, 9], F32); selB = Buf()
        cand = _sb(st, "hcand", [128, 9, 4, NSLOT * PH], F32); candB = Buf()
        cx.dma(cx.sp, [(sel[:], T["halosel"])], writes=[selB])
        for kg in range(4):
            ks_ = slice(kg * 4, kg * 4 + 4)
            cx.op(cx.dve, lambda: nc.vector.memset(cand[:, 8, :, 0:PH], 0.0), writes=[candB])
            prs = [(cand[:, j], H[j, :, ks_].rearrange("p k i t -> p k (i t)")) for j in range(8)]
            prs.append((cand[:, 8, :, PH:NSLOT * PH], H[7, :, ks_, 0:NSLOT - 1, :].rearrange("p k i t -> p k (i t)")))
            cx.dma(cx.sp, prs, reads=[hgB], writes=[candB])
            cx.op(cx.dve, lambda: nc.vector.tensor_scalar(out=xh[:, ks_, :], in0=cand[:, 0], scalar1=sel[:, 0:1],
                                                          scalar2=None, op0=ALU.mult),
                  reads=[candB, selB], writes=xhB[ks_])
            for j in range(1, 9):
                cx.op(cx.dve, lambda: nc.vector.scalar_tensor_tensor(out=xh[:, ks_, :], in0=cand[:, j],
                                                                     scalar=sel[:, j:j + 1], in1=xh[:, ks_, :],
                                                                     op0=ALU.mult, op1=ALU.add),
                      reads=[candB, selB], writes=xhB[ks_])

    T["after_kv"] = after_kv
    T["after_halo"] = after_halo
    T["load_halo"] = load_halo
    ps = [_ps(nc, "ps%d" % j, [128, 512], F32) for j in range(7)]; psB = [Buf() for _ in range(7)]
    ptr = _ps(nc, "ptr", [128, 1024], BF16); ptrB = Buf()
    emit_p1(nc, cx, T, ps, psB)
    if stage == 1:
        dbg = nc.dram_tensor("dbg", [NCORES * 128, (KW + VW) // 2], F32, kind="ExternalOutput").ap()
        cx.dma(cx.sp, [(dbg, kvgat.ap())], reads=[T["kvgB"]], is_out=True)
        cx.finish()
        return nc
    emit_p2(nc, cx, T, ps, psB, ptr, ptrB)
    if stage == 2:
        dbg = nc.dram_tensor("dbg", [NCORES * 128, DC * NSLOT * PH], F32, kind="ExternalOutput").ap()
        cx.dma(cx.sp, [(dbg, hgat.ap())], reads=[hgB], is_out=True)
        dbg2 = nc.dram_tensor("dbg2", [128, DC, TL], F32, kind="ExternalOutput").ap()
        cx.dma(cx.sp, [(dbg2, T["xT2"])], is_out=True)
        cx.finish()
        return nc
    emit_p3(nc, cx, T, ps, psB)
    cx.finish()
    return nc


_PROGS = {}


def kernel(x, mix_norm, mlp_norm, w_mlp_in, w_mlp_out, w_in, w_out, conv_w, conv_b,
           conv_ln_g, conv_ln_b, cmp_pos_k, cmp_pos_v, cmp_k_w1, cmp_k_w2,
           cmp_v_w1, cmp_v_w2, pool_w, pool_scale, final_norm):
    inp = dict(x=x, mix_norm=mix_norm, mlp_norm=mlp_norm, w_mlp_in=w_mlp_in, w_mlp_out=w_mlp_out, w_in=w_in,
               w_out=w_out, conv_w=conv_w, conv_b=conv_b, conv_ln_g=conv_ln_g, conv_ln_b=conv_ln_b,
               cmp_pos_k=cmp_pos_k, cmp_pos_v=cmp_pos_v, cmp_k_w1=cmp_k_w1, cmp_k_w2=cmp_k_w2,
               cmp_v_w1=cmp_v_w1, cmp_v_w2=cmp_v_w2, pool_w=pool_w, pool_scale=pool_scale, final_norm=final_norm)
    inp = {k: np.asarray(v, dtype=np.float32) for k, v in inp.items()}
    if "fused" not in _PROGS:
        _PROGS["fused"] = build_fused()
    maps = host_inputs(inp)
    res = run_bass_kernel_spmd(_PROGS["fused"], maps, core_ids=list(range(NCORES))).results
    out = np.zeros((S, D), np.float32)
    for c in range(NCORES):
        o = np.asarray(res[c]["out"])
        for i in range(NSLOT):
            kt = 8 * i + c
            out[128 * kt:128 * kt + 128] = o[i]
    return out.reshape(1, S, D)
```

```python
import numpy as np
from contextlib import ExitStack
import ml_dtypes
import concourse.bass as bass
import concourse.mybir as mybir
from concourse.bass_utils import run_bass_kernel_spmd

F32 = mybir.dt.float32
BF16 = mybir.dt.bfloat16
ALU = mybir.AluOpType
AF = mybir.ActivationFunctionType
AX = mybir.AxisListType

NCORES = 8
D = 2048
S = 8192
NSLOT = 8
TL = 1024
DC = 16
HALO = 32
IN_COLS = 4656
DFF = 8192
EPS = 1e-6
NEG = -30000.0
SCALE = 0.125


class Buf:
    __slots__ = ("name", "w", "r")

    def __init__(self, name=""):
        self.name = name
        self.w = None
        self.r = []


class EngS:
    def __init__(self, nc, eng, name, ndma=0):
        self.nc = nc
        self.e = eng
        self.name = name
        self.sem = nc.alloc_semaphore("sem_" + name)
        self.count = 0
        self.waited = {}
        self.dsems = [[nc.alloc_semaphore("dsem_%s_%d" % (name, i)), 0] for i in range(ndma)]
        self.dnext = 0

    def wait(self, tk):
        if tk is None:
            return
        sem, val = tk
        key = id(sem)
        if self.waited.get(key, 0) >= val:
            return
        self.waited[key] = val
        self.e.wait_ge(sem, val)


class Ctx:
    def __init__(self, nc):
        self.nc = nc
        self.pe = EngS(nc, nc.tensor, "pe")
        self.act = EngS(nc, nc.scalar, "act", ndma=4)
        self.dve = EngS(nc, nc.vector, "dve")
        self.pool = EngS(nc, nc.gpsimd, "pool", ndma=8)
        self.sp = EngS(nc, nc.sync, "sp", ndma=12)
        self.out_tickets = []

    def _deps(self, E, reads, writes, same_engine_ok=False):
        for b in reads:
            if b.w is not None:
                if not (same_engine_ok and b.w[0] is E.sem):
                    E.wait(b.w)
        for b in writes:
            if b.w is not None:
                if not (same_engine_ok and b.w[0] is E.sem):
                    E.wait(b.w)
            for t in b.r:
                if not (same_engine_ok and t[0] is E.sem):
                    E.wait(t)

    def _commit(self, tk, reads, writes):
        for b in reads:
            b.r.append(tk)
            if len(b.r) > 24:
                last = {}
                for t in b.r:
                    k = id(t[0])
                    if k not in last or last[k][1] < t[1]:
                        last[k] = t
                b.r = list(last.values())
        for b in writes:
            b.w = tk
            b.r = []

    def op(self, E, fn, reads=(), writes=(), mark=True):
        self._deps(E, reads, writes, same_engine_ok=(E is self.pe))
        ins = fn()
        if mark:
            E.count += 1
            ins.then_inc(E.sem, 1)
            tk = (E.sem, E.count)
        else:
            tk = (E.sem, E.count + 1)
        self._commit(tk, reads, writes)
        return tk

    def dma(self, Q, pairs, reads=(), writes=(), is_out=False):
        self._deps(Q, reads, writes)
        slot = Q.dsems[Q.dnext]
        Q.dnext = (Q.dnext + 1) % len(Q.dsems)
        sem = slot[0]
        Q.wait((sem, slot[1]))
        for (o, i) in pairs:
            Q.e.dma_start(out=o, in_=i).then_inc(sem, 16)
            slot[1] += 16
        tk = (sem, slot[1])
        self._commit(tk, reads, writes)
        if is_out:
            self.out_tickets.append(tk)
        return tk

    def barrier(self):
        engs = (self.pe, self.act, self.dve, self.pool, self.sp)
        for E in engs:
            for Fo in engs:
                if Fo is not E and Fo.count:
                    E.wait((Fo.sem, Fo.count))
                for sv in Fo.dsems:
                    if sv[1]:
                        E.wait((sv[0], sv[1]))

    def finish(self):
        for tk in self.out_tickets:
            self.sp.wait(tk)
        for E in (self.pe, self.act, self.dve, self.pool):
            if E.count:
                self.sp.wait((E.sem, E.count))
            for s, v in E.dsems:
                if v:
                    self.sp.wait((s, v))
        for s, v in self.sp.dsems:
            if v:
                self.sp.wait((s, v))


_SBN = [0]


def _sb(st, name, shape, dt):
    _SBN[0] += 1
    return st.enter_context(st.nc.sbuf_tensor("s%d_%s" % (_SBN[0], name), shape, dt))


def _scope(nc):
    st = ExitStack()
    st.nc = nc
    return st


def _ps(nc, name, shape, dt):
    return nc.alloc_psum_tensor(name, shape, dt)


class WStream:
    def __init__(self, cx, st, name, kch, cols, nbuf):
        nc = st
        self.cx = cx
        self.kch = kch
        self.cols = cols
        self.t = [_sb(nc, "%s_%d" % (name, j), [128, kch, cols], BF16) for j in range(nbuf)]
        self.b = [Buf() for _ in range(nbuf)]
        self.n = 0

    def load(self, w_ap, c0, ncols, kch=None, r0=0):
        kch = kch or self.kch
        j = self.n % len(self.t)
        self.n += 1
        src = w_ap[r0:r0 + kch * 128, c0:c0 + ncols].rearrange("(k p) c -> p k c", p=128)
        self.cx.dma(self.cx.pool, [(self.t[j][:, 0:kch, 0:ncols], src)], writes=[self.b[j]])
        return self.t[j], self.b[j]

    def load_q(self, w_ap, c0):
        j = self.n % len(self.t)
        self.n += 1
        src = w_ap[:, c0:c0 + 512].rearrange("(k p) (two r e) -> p k r two e", p=128, two=2, r=4)
        dst = self.t[j][:, :, 0:512].rearrange("p k (r two e) -> p k r two e", two=2, r=4)
        self.cx.dma(self.cx.pool, [(dst[:, :, r, two], src[:, :, r, two]) for r in range(4) for two in range(2)], writes=[self.b[j]])
        return self.t[j], self.b[j]


def rms_stats_fm(cx, xT_fn, xbufs, ncols, onesD, scratch):
    nc = cx.nc
    sq, sqb = scratch["sq"], scratch["sqb"]
    ps, psb = scratch["ps"], scratch["psb"]
    rstd, rstdb = scratch["rstd"], scratch["rstdb"]
    nblk = (ncols + 511) // 512
    for k in range(DC):
        j = k % 2
        cx.op(cx.act, lambda: nc.scalar.activation(out=sq[j][:, 0:ncols], in_=xT_fn(k), func=AF.Square),
              reads=[xbufs[k]], writes=[sqb[j]])
        for b in range(nblk):
            c0, c1 = b * 512, min(ncols, b * 512 + 512)
            cx.op(cx.pe, lambda: nc.tensor.matmul(ps[b][:, 0:c1 - c0], lhsT=onesD[:], rhs=sq[j][:, c0:c1],
                                                  start=(k == 0), stop=(k == DC - 1)),
                  reads=[sqb[j]], writes=[psb[b]])
    for b in range(nblk):
        c0, c1 = b * 512, min(ncols, b * 512 + 512)
        cx.op(cx.act, lambda: nc.scalar.activation(out=rstd[:, c0:c1], in_=ps[b][:, 0:c1 - c0], func=AF.Sqrt,
                                                   bias=scratch["eps"][:, 0:1], scale=1.0),
              reads=[psb[b]], writes=[rstdb])
    cx.op(cx.dve, lambda: nc.vector.reciprocal(out=rstd[:, 0:ncols], in_=rstd[:, 0:ncols]),
          reads=[rstdb], writes=[rstdb])


def rms_apply_fm(cx, k, xT_ap, xbuf, g_sb, out_ap, obuf, rstd_ap, rstdb):
    nc = cx.nc
    cx.op(cx.dve, lambda: nc.vector.scalar_tensor_tensor(out=out_ap, in0=xT_ap, scalar=g_sb[:, k:k + 1],
                                                         in1=rstd_ap, op0=ALU.mult, op1=ALU.mult),
          reads=[xbuf, rstdb], writes=[obuf])


def rms_norm_fm(cx, xT_fn, xbufs, ncols, g_sb, out_fn, obufs, onesD, scratch):
    rms_stats_fm(cx, xT_fn, xbufs, ncols, onesD, scratch)
    for k in range(DC):
        rms_apply_fm(cx, k, xT_fn(k), xbufs[k], g_sb, out_fn(k), obufs[k], scratch["rstd"][:, 0:ncols], scratch["rstdb"])


def emit_p1(nc, cx, T, ps, psB):
    W = HALO + 128
    NCOL = NSLOT * W
    xh, w_in, g0, convw, cpar, ident_d = T["xh"], T["w_in"], T["g0"], T["convw"], T["cpar"], T["ident"]
    xT_o, qT_o, cT_o, gates_o, kfm_o, vtm_o = T["xT"], T["qT"], T["cT"], T["gates"], T["kfm"], T["vtm"]
    kvB = T["kvB"]

    s0 = _scope(nc)
    ident = _sb(s0, "ident", [128, 128], F32); identB = Buf()
    onesD = _sb(s0, "onesD", [128, 128], F32); onesB = Buf()
    ones1k = _sb(s0, "ones1k", [128, 128], F32)
    epsT = _sb(s0, "epsT", [128, 1], F32)
    g_sb = _sb(s0, "g_sb", [128, DC], F32); gB = Buf()
    cw_sb = _sb(s0, "cw_sb", [128, 8, 31], F32)
    cp_sb = _sb(s0, "cp_sb", [128, 3, 8], F32)
    cfm = _sb(s0, "cfm", [128, 8, NSLOT, W], F32); cfmB = [Buf() for _ in range(8)]
    s1 = _scope(nc)
    xnO = _sb(s1, "xnO", [128, DC, TL], BF16); xnB = [Buf() for _ in range(DC)]
    xnH = _sb(s1, "xnH", [128, DC, NSLOT * HALO], BF16); xnHB = [Buf() for _ in range(DC)]
    sA = _scope(nc)
    xtok = [_sb(sA, "xtok%d" % j, [128, D], F32) for j in range(1)] * 2; xtokB = [Buf()] * 2
    xhal = [_sb(sA, "xhal%d" % j, [HALO, D], F32) for j in range(1)] * 2; xhalB = [Buf()] * 2
    xTo = _sb(sA, "xTo", [128, DC, TL], F32); xToB = [Buf() for _ in range(DC)]
    xTl = _sb(sA, "xTl", [128, DC, NSLOT * HALO], F32); xTlB = [Buf() for _ in range(DC)]
    sq = [_sb(sA, "sq%d" % j, [128, NCOL], F32) for j in range(2)]; sqB = [Buf(), Buf()]
    rstd = _sb(sA, "rstd", [128, NCOL], F32); rstdB = Buf()

    cx.dma(cx.sp, [(ident[:], ident_d)], writes=[identB])
    cx.dma(cx.sp, [(g_sb[:], g0), (cw_sb[:], convw), (cp_sb[:], cpar)], writes=[gB])
    cx.op(cx.dve, lambda: nc.vector.memset(onesD[:], 1.0 / D), writes=[onesB])
    cx.op(cx.dve, lambda: nc.vector.memset(ones1k[:], 1.0 / 1024), writes=[onesB])
    cx.op(cx.dve, lambda: nc.vector.memset(epsT[:], EPS), writes=[onesB])

    for i in range(NSLOT):
        j = i % 2
        cx.dma(cx.sp, [(xtok[j][:], xh[i, HALO:W, :])], writes=[xtokB[j]])
        cx.dma(cx.sp, [(xhal[j][:], xh[i, 0:HALO, :])], writes=[xhalB[j]])
        for k4 in range(4):
            pb = (i * 4 + k4) % 4
            for kk in range(4):
                k = k4 * 4 + kk
                cx.op(cx.pe, lambda: nc.tensor.transpose(out=ps[pb][:, kk * 128:(kk + 1) * 128],
                                                         in_=xtok[j][:, k * 128:(k + 1) * 128], identity=ident[:]),
                      reads=[xtokB[j], identB], writes=[psB[pb]], mark=(kk == 3))
            cx.op(cx.dve, lambda: nc.vector.tensor_copy(
                out=xTo[:, k4 * 4:k4 * 4 + 4, i * 128:(i + 1) * 128],
                in_=ps[pb][:].rearrange("p (a b) -> p a b", a=4)),
                reads=[psB[pb]], writes=xToB[k4 * 4:k4 * 4 + 4])
        pb = 4 + (i % 2)
        for k in range(DC):
            cx.op(cx.pe, lambda: nc.tensor.transpose(out=ps[pb][:, k * HALO:(k + 1) * HALO],
                                                     in_=xhal[j][:, k * 128:(k + 1) * 128],
                                                     identity=ident[0:HALO, 0:HALO]),
                  reads=[xhalB[j], identB], writes=[psB[pb]], mark=(k == DC - 1))
        cx.op(cx.act, lambda: nc.scalar.copy(out=xTl[:, :, i * HALO:(i + 1) * HALO],
                                             in_=ps[pb][:].rearrange("p (a b) -> p a b", a=DC)),
              reads=[psB[pb]], writes=xTlB)

    cx.dma(cx.sp, [(xT_o, xTo[:])], reads=xToB, is_out=True)

    scratch = dict(sq=sq, sqb=sqB, ps=ps[0:3], psb=psB[0:3], rstd=rstd, rstdb=rstdB, eps=epsT)
    rms_norm_fm(cx, lambda k: xTo[:, k, :], xToB, TL, g_sb, lambda k: xnO[:, k, :], xnB, onesD, scratch)
    rms_norm_fm(cx, lambda k: xTl[:, k, :], xTlB, NSLOT * HALO, g_sb, lambda k: xnH[:, k, :], xnHB, onesD, scratch)

    cx.barrier()
    sA.close()
    sB = _scope(nc)
    ws = WStream(cx, sB, "w1", DC, 512, 3)
    a_sb = _sb(sB, "a_sb", [128, NCOL], F32); aB = Buf()
    sg_sb = _sb(sB, "sg_sb", [128, NCOL], F32); sgB = Buf()
    qT = _sb(sB, "qT", [128, NSLOT, 8, 128], BF16); qB = Buf()
    kfm = _sb(sB, "kfm", [128, 4, NSLOT, 2, 128], BF16); kfB = Buf()
    vtm = _sb(sB, "vtm", [128, 2, NSLOT, 256], BF16); vtB = Buf()
    gat = _sb(sB, "gat", [128, NSLOT, 48], F32); gatB = Buf()
    blocks = [(0, 512), (512, 512), (1024, 256)]
    pscnt = [0]

    def formB_all(wt, wb, c0, evac):
        for bi, (b0, bn) in enumerate(blocks):
            pb = pscnt[0] % 7; pscnt[0] += 1
            for k in range(DC):
                rhs = xnO[:, k, b0:b0 + bn] if bi < 2 else xnH[:, k, :]
                cx.op(cx.pe, lambda: nc.tensor.matmul(ps[pb][:, 0:bn], lhsT=wt[:, k, c0:c0 + 128], rhs=rhs,
                                                      start=(k == 0), stop=(k == DC - 1)),
                      reads=[wb, xnB[k], xnHB[k]], writes=[psB[pb]], mark=(k == DC - 1))
            evac(ps[pb][:, 0:bn], psB[pb], b0, b0 + bn)

    kind_of = {(0, 0): 0, (0, 1): 1, (1, 0): 2, (2, 0): 3}
    for blk in range(3):
        wk, wkb = ws.load(w_in, 3072 + blk * 512, 512)
        for hf in range(2):
            if (blk, hf) in kind_of:
                kind = kind_of[(blk, hf)]
                for gh in range(2):
                    c0 = hf * 256 + gh * 128
                    for tb in range(2):
                        pb = pscnt[0] % 7; pscnt[0] += 1
                        for k in range(DC):
                            cx.op(cx.pe, lambda: nc.tensor.matmul(ps[pb][:], lhsT=wk[:, k, c0:c0 + 128],
                                                                  rhs=xnO[:, k, tb * 512:(tb + 1) * 512],
                                                                  start=(k == 0), stop=(k == DC - 1)),
                                  reads=[wkb, xnB[k]], writes=[psB[pb]], mark=(k == DC - 1))
                        cx.op(cx.act, lambda: nc.scalar.copy(out=kfm[:, kind, tb * 4:tb * 4 + 4, gh, :],
                                                             in_=ps[pb][:].rearrange("p (a b) -> p a b", a=4)),
                              reads=[psB[pb]], writes=[kfB])
            else:
                vk = 0 if blk == 1 else 1
                for i in range(NSLOT):
                    pb = pscnt[0] % 7; pscnt[0] += 1
                    for k in range(DC):
                        cx.op(cx.pe, lambda: nc.tensor.matmul(ps[pb][:, 0:256], lhsT=xnO[:, k, i * 128:(i + 1) * 128],
                                                              rhs=wk[:, k, 256:512],
                                                              start=(k == 0), stop=(k == DC - 1)),
                              reads=[wkb, xnB[k]], writes=[psB[pb]], mark=(k == DC - 1))
                    cx.op(cx.act, lambda: nc.scalar.copy(out=vtm[:, vk, i, :], in_=ps[pb][:, 0:256]),
                          reads=[psB[pb]], writes=[vtB])
    wgl, wglb = ws.load(w_in, 4608, 48)
    for i in range(NSLOT):
        pb = pscnt[0] % 7; pscnt[0] += 1
        for k in range(DC):
            cx.op(cx.pe, lambda: nc.tensor.matmul(ps[pb][:, 0:48], lhsT=xnO[:, k, i * 128:(i + 1) * 128], rhs=wgl[:, k, 0:48],
                                                  start=(k == 0), stop=(k == DC - 1)),
                  reads=[wglb, xnB[k]], writes=[psB[pb]], mark=(k == DC - 1))
        cx.op(cx.act, lambda: nc.scalar.activation(out=gat[:, i, :], in_=ps[pb][:, 0:48], func=AF.Sigmoid),
              reads=[psB[pb]], writes=[gatB])
    cx.dma(cx.sp, [(kfm_o, kfm[:]), (vtm_o, vtm[:])], reads=[kfB, vtB], writes=[kvB], is_out=True)
    if T.get("after_kv"):
        T["after_kv"]()
    cx.dma(cx.sp, [(gates_o, gat[:])], reads=[gatB], is_out=True)
    for half in range(2):
        wa, wab = ws.load(w_in, half * 512, 512)
        wg, wgb = ws.load(w_in, 1024 + half * 512, 512)
        for mm in range(4):
            m = half * 4 + mm
            formB_all(wa, wab, mm * 128,
                      lambda p, pbuf, lo, hi: cx.op(cx.act, lambda: nc.scalar.copy(out=a_sb[:, lo:hi], in_=p),
                                                    reads=[pbuf], writes=[aB]))
            formB_all(wg, wgb, mm * 128,
                      lambda p, pbuf, lo, hi: cx.op(cx.act, lambda: nc.scalar.activation(out=sg_sb[:, lo:hi], in_=p,
                                                                                        func=AF.Sigmoid),
                                                    reads=[pbuf], writes=[sgB]))
            cx.op(cx.dve, lambda: nc.vector.tensor_mul(out=cfm[:, m, :, HALO:W],
                                                       in0=a_sb[:, 0:TL].rearrange("p (i t) -> p i t", i=NSLOT),
                                                       in1=sg_sb[:, 0:TL].rearrange("p (i t) -> p i t", i=NSLOT)),
                  reads=[aB, sgB], writes=[cfmB[m]])
            cx.op(cx.dve, lambda: nc.vector.tensor_mul(out=cfm[:, m, :, 0:HALO],
                                                       in0=a_sb[:, TL:NCOL].rearrange("p (i t) -> p i t", i=NSLOT),
                                                       in1=sg_sb[:, TL:NCOL].rearrange("p (i t) -> p i t", i=NSLOT)),
                  reads=[aB, sgB], writes=[cfmB[m]])

    for gh in range(2):
        wq, wqb = ws.load_q(w_in, 2048 + gh * 512)
        for r in range(4):
            lhs_fn = lambda k: wq[:, k, r * 128:(r + 1) * 128]
            for tb in range(2):
                pb = pscnt[0] % 7; pscnt[0] += 1
                for k in range(DC):
                    cx.op(cx.pe, lambda: nc.tensor.matmul(ps[pb][:], lhsT=lhs_fn(k),
                                                          rhs=xnO[:, k, tb * 512:(tb + 1) * 512],
                                                          start=(k == 0), stop=(k == DC - 1)),
                          reads=[wqb, xnB[k]], writes=[psB[pb]], mark=(k == DC - 1))
                cx.op(cx.act, lambda: nc.scalar.copy(out=qT[:, tb * 4:tb * 4 + 4, gh * 4 + r, :],
                                                     in_=ps[pb][:].rearrange("p (a b) -> p a b", a=4)),
                      reads=[psB[pb]], writes=[qB])

    cx.dma(cx.sp, [(qT_o, qT[:])], reads=[qB], is_out=True)
    cx.barrier()
    sB.close()
    s1.close()
    sC = _scope(nc)
    acc = _sb(sC, "acc", [128, 8, NSLOT, 128], F32); accB = [Buf() for _ in range(8)]
    cT = _sb(sC, "cT", [128, 8, TL], BF16); cTB = Buf()
    mean_sb = _sb(sC, "mean_sb", [128, TL], F32); meanB = Buf()
    sq = [_sb(sC, "sqc%d" % j, [128, TL], F32) for j in range(2)]; sqB = [Buf(), Buf()]
    rstd = _sb(sC, "rstdc", [128, TL], F32); rstdB = Buf()
    for m in range(8):
        cx.op(cx.dve, lambda: nc.vector.tensor_scalar(out=acc[:, m], in0=cfm[:, m, :, HALO:W],
                                                      scalar1=cw_sb[:, m, 30:31], scalar2=cp_sb[:, 0, m:m + 1],
                                                      op0=ALU.mult, op1=ALU.add),
              reads=[cfmB[m], gB], writes=[accB[m]])
        for k in range(1, 31):
            cx.op(cx.dve, lambda: nc.vector.scalar_tensor_tensor(out=acc[:, m], in0=cfm[:, m, :, HALO - k:W - k],
                                                                 scalar=cw_sb[:, m, 30 - k:31 - k], in1=acc[:, m],
                                                                 op0=ALU.mult, op1=ALU.add),
                  reads=[cfmB[m], accB[m]], writes=[accB[m]])
    for m in range(8):
        j = m % 2
        af = acc[:, m].rearrange("p i w -> p (i w)")
        cx.op(cx.act, lambda: nc.scalar.activation(out=sq[j][:, 0:TL], in_=af, func=AF.Square),
              reads=[accB[m]], writes=[sqB[j]])
        for tb in range(2):
            cx.op(cx.pe, lambda: nc.tensor.matmul(ps[tb][:], lhsT=ones1k[:], rhs=af[:, tb * 512:(tb + 1) * 512],
                                                  start=(m == 0), stop=(m == 7)),
                  reads=[accB[m], onesB], writes=[psB[tb]])
            cx.op(cx.pe, lambda: nc.tensor.matmul(ps[2 + tb][:], lhsT=ones1k[:], rhs=sq[j][:, tb * 512:(tb + 1) * 512],
                                                  start=(m == 0), stop=(m == 7)),
                  reads=[sqB[j], onesB], writes=[psB[2 + tb]])
    for tb in range(2):
        sl = slice(tb * 512, (tb + 1) * 512)
        cx.op(cx.act, lambda: nc.scalar.copy(out=mean_sb[:, sl], in_=ps[tb][:]), reads=[psB[tb]], writes=[meanB])
        cx.op(cx.dve, lambda: nc.vector.tensor_mul(out=rstd[:, sl], in0=mean_sb[:, sl], in1=mean_sb[:, sl]),
              reads=[meanB], writes=[rstdB])
        cx.op(cx.dve, lambda: nc.vector.tensor_sub(out=rstd[:, sl], in0=ps[2 + tb][:], in1=rstd[:, sl]),
              reads=[psB[2 + tb], rstdB], writes=[rstdB])
    cx.op(cx.act, lambda: nc.scalar.activation(out=rstd[:, 0:TL], in_=rstd[:, 0:TL], func=AF.Sqrt,
                                               bias=epsT[:, 0:1], scale=1.0), reads=[rstdB], writes=[rstdB])
    cx.op(cx.dve, lambda: nc.vector.reciprocal(out=rstd[:, 0:TL], in_=rstd[:, 0:TL]), reads=[rstdB], writes=[rstdB])
    for m in range(8):
        j = m % 2
        af = acc[:, m].rearrange("p i w -> p (i w)")
        cx.op(cx.dve, lambda: nc.vector.tensor_sub(out=sq[j][:, 0:TL], in0=af, in1=mean_sb[:]),
              reads=[accB[m], meanB], writes=[sqB[j]])
        cx.op(cx.dve, lambda: nc.vector.tensor_mul(out=sq[j][:, 0:TL], in0=sq[j][:, 0:TL], in1=rstd[:, 0:TL]),
              reads=[sqB[j], rstdB], writes=[sqB[j]])
        cx.op(cx.act, lambda: nc.scalar.activation(out=cT[:, m, :], in_=sq[j][:, 0:TL], func=AF.Silu,
                                                   bias=cp_sb[:, 2, m:m + 1], scale=cp_sb[:, 1, m:m + 1]),
              reads=[sqB[j], gB], writes=[cTB])
    cx.dma(cx.sp, [(cT_o, cT[:])], reads=[cTB], is_out=True)
    cx.barrier()
    sC.close()
    s0.close()


def mlp_stage(cx, st, xT, xTB, g_sb, w1_ap, w2_ap, ps, psB, onesD, epsT):
    nc = cx.nc
    xnT = _sb(st, "m_xnT", [128, DC, TL], BF16); xnB = [Buf() for _ in range(DC)]
    sq = [_sb(st, "m_sq%d" % j, [128, TL], F32) for j in range(2)]; sqB = [Buf(), Buf()]
    rstd = _sb(st, "m_rstd", [128, TL], F32); rstdB = Buf()
    hT = [_sb(st, "m_hT%d" % j, [128, 8, TL], BF16) for j in range(1)]; hB = [[Buf() for _ in range(8)] for _ in range(1)]
    tmp = [_sb(st, "m_tmp%d" % j, [128, 512], F32) for j in range(2)]; tmpB = [Buf(), Buf()]
    ws = WStream(cx, st, "m_w", DC, 512, 3)
    scratch = dict(sq=sq, sqb=sqB, ps=ps[0:2], psb=psB[0:2], rstd=rstd, rstdb=rstdB, eps=epsT)
    rms_norm_fm(cx, lambda k: xT[:, k, :], xTB, TL, g_sb, lambda k: xnT[:, k, :], xnB, onesD, scratch)
    cnt = 0
    NPS = len(ps)
    for fb in range(8):
        hj = 0
        for wt in range(2):
            w, wb = ws.load(w1_ap, fb * 1024 + wt * 512, 512)
            for m in range(4):
                fch = wt * 4 + m
                for tb in range(2):
                    pb = cnt % NPS; cnt += 1
                    for k in range(DC):
                        cx.op(cx.pe, lambda: nc.tensor.matmul(ps[pb][:], lhsT=w[:, k, m * 128:(m + 1) * 128],
                                                              rhs=xnT[:, k, tb * 512:(tb + 1) * 512],
                                                              start=(k == 0), stop=(k == DC - 1)),
                              reads=[wb, xnB[k]], writes=[psB[pb]], mark=(k == DC - 1))
                    tj = cnt % 2
                    cx.op(cx.act, lambda: nc.scalar.activation(out=tmp[tj][:], in_=ps[pb][:], func=AF.Square),
                          reads=[psB[pb]], writes=[tmpB[tj]])
                    cx.op(cx.dve, lambda: nc.vector.scalar_tensor_tensor(
                        out=hT[hj][:, fch, tb * 512:(tb + 1) * 512], in0=ps[pb][:], scalar=0.0, in1=tmp[tj][:],
                        op0=ALU.is_gt, op1=ALU.mult), reads=[psB[pb], tmpB[tj]], writes=[hB[hj][fch]])
        for wt in range(4):
            w, wb = ws.load(w2_ap, wt * 512, 512, kch=8, r0=fb * 1024)
            for m in range(4):
                och = wt * 4 + m
                for tb in range(2):
                    pb = cnt % NPS; cnt += 1
                    for k in range(8):
                        cx.op(cx.pe, lambda: nc.tensor.matmul(ps[pb][:], lhsT=w[:, k, m * 128:(m + 1) * 128],
                                                              rhs=hT[hj][:, k, tb * 512:(tb + 1) * 512],
                                                              start=(k == 0), stop=(k == 7)),
                              reads=[wb, hB[hj][k]], writes=[psB[pb]], mark=(k == 7))
                    cx.op(cx.dve, lambda: nc.vector.tensor_add(out=xT[:, och, tb * 512:(tb + 1) * 512], in0=ps[pb][:],
                                                               in1=xT[:, och, tb * 512:(tb + 1) * 512]),
                          reads=[psB[pb]], writes=[xTB[och]])


def emit_p2(nc, cx, T, ps, psB, ptr, ptrB, debug=False):
    qT_d, gates_d, cT_d, xT_d = T["qT"], T["gates"], T["cT"], T["xT"]
    KG, VG, gB_ = T["kfm_g"], T["vtm_g"], T["kvgB"]
    cw1_d = [T["cw1_k"], T["cw1_v"]]
    cw2_d = [T["cw2_k"], T["cw2_v"]]
    posT_d, cmask_d, dzmask_d, wmask_d = T["posT"], T["cmask"], T["dzmask"], T["wmask"]
    esel_d, bonus_d, ovl_d, identb_d = T["esel"], T["bonus"], T["ovl"], T["identb"]
    w_out_d, w1_d, w2_d, gm_d = T["w_out"], T["w_mlp_in0"], T["w_mlp_out0"], T["g_mlp0"]
    xT_o = T["xT2"]
    if debug:
        oT_o, kc_o, vc_o = T["oT_dbg"], T["kc_dbg"], T["vc_dbg"]

    s0 = _scope(nc)
    identb = _sb(s0, "identb", [128, 128], BF16); cB = Buf()
    onesD = _sb(s0, "onesD", [128, 128], F32)
    epsT = _sb(s0, "epsT", [128, 1], F32)
    gm_sb = _sb(s0, "gm", [128, DC], F32)
    cx.dma(cx.sp, [(identb[:], identb_d), (gm_sb[:], gm_d)], writes=[cB])
    cx.op(cx.dve, lambda: nc.vector.memset(onesD[:], 1.0 / D), writes=[cB])
    cx.op(cx.dve, lambda: nc.vector.memset(epsT[:], EPS), writes=[cB])
    oT = _sb(s0, "oT", [128, 8, TL], BF16); oTB = Buf()

    sAtt = _scope(nc)
    kcT = _sb(sAtt, "kcT", [128, 2, 512], BF16); kcB = Buf()
    vcx = _sb(sAtt, "vcx", [128, 4, 4, 193], BF16); vcB = Buf()
    cx.op(cx.dve, lambda: nc.vector.memset(kcT[:], 0.0), writes=[kcB])
    cx.op(cx.dve, lambda: nc.vector.memset(vcx[:], 0.0), writes=[vcB])

    sCmp = _scope(nc)
    raw = _sb(sCmp, "raw", [128, 2, S], BF16); rawB = Buf()
    w1s = _sb(sCmp, "w1s", [128, 32, 128], BF16); w1B = Buf()
    w2k = _sb(sCmp, "w2k", [128, 128], BF16)
    w2v = _sb(sCmp, "w2v", [128, 64], BF16); w2B = Buf()
    posT = _sb(sCmp, "posT", [64, 2, 32], BF16); posB = Buf()
    hTs = [_sb(sCmp, "hTs%d" % j, [128, 512], BF16) for j in range(2)]; hTB = [Buf(), Buf()]
    bias_sb = _sb(sCmp, "cbias", [128, 2], F32); biasB = Buf()
    ovl_sb = _sb(sCmp, "ovl", [128, 4, 128], BF16); ovlB = Buf()
    cx.dma(cx.pool, [(posT[:], posT_d)], writes=[posB])
    cx.dma(cx.pool, [(w2k[:, 0:64], cw2_d[0]), (w2k[:, 64:128], cw2_d[0]), (w2v[:], cw2_d[1])], writes=[w2B])
    cx.dma(cx.sp, [(ovl_sb[:], ovl_d)], writes=[ovlB])
    for j in range(2):
        cx.op(cx.dve, lambda: nc.vector.memset(hTs[j][:], 0.0), writes=[hTB[j]])
    hcnt = 0
    for kind in range(2):
        cx.dma(cx.sp, [(raw[:, gh, :].rearrange("p (i r t) -> p r i t", r=8, t=128)[:, r], KG[r, :, kind, :, gh, :])
                       for gh in range(2) for r in range(8)], reads=[gB_], writes=[rawB])
        w1src = cw1_d[kind].rearrange("(l d) j -> d l j", d=64)
        cx.dma(cx.pool, [(w1s[0:64], w1src), (w1s[64:128], w1src)], writes=[w1B])
        for l in range(32):
            cx.op(cx.pe, lambda: nc.tensor.matmul(ps[6][:, 0:1], lhsT=w1s[0:64, l, :], rhs=posT[:, kind, l:l + 1],
                                                  start=(l == 0), stop=(l == 31)),
                  reads=[w1B, posB], writes=[psB[6]], mark=(l == 31))
        cx.op(cx.act, lambda: nc.scalar.copy(out=bias_sb[:, kind:kind + 1], in_=ps[6][:, 0:1]),
              reads=[psB[6]], writes=[biasB])
        for g in range(4):
            pb0 = 64 * (g % 2); gh = g // 2
            hp = hcnt % 2; hcnt += 1
            for l in range(32):
                cx.op(cx.pe, lambda: nc.tensor.matmul(ps[hp][:, 0:511], lhsT=w1s[pb0:pb0 + 64, l, :],
                                                      rhs=raw[pb0:pb0 + 64, gh, l:l + 16 * 510 + 1:16],
                                                      start=(l == 0), stop=(l == 31)),
                      reads=[w1B, rawB], writes=[psB[hp]], mark=(l == 31))
            cx.op(cx.act, lambda: nc.scalar.activation(out=hTs[hp][:, 0:511], in_=ps[hp][:, 0:511], func=AF.Silu,
                                                       bias=bias_sb[:, kind:kind + 1], scale=1.0),
                  reads=[psB[hp], biasB], writes=[hTB[hp]])
            if kind == 0:
                cx.op(cx.pe, lambda: nc.tensor.matmul(ps[2 + hp][:, 0:511], lhsT=w2k[:], rhs=hTs[hp][:, 0:511],
                                                      start=True, stop=True),
                      reads=[w2B, hTB[hp]], writes=[psB[2 + hp]])
                cx.op(cx.dve, lambda: nc.vector.tensor_copy(out=kcT[pb0:pb0 + 64, gh, 0:511],
                                                            in_=ps[2 + hp][pb0:pb0 + 64, 0:511]),
                      reads=[psB[2 + hp]], writes=[kcB])
            else:
                for nt in range(4):
                    cx.op(cx.pe, lambda: nc.tensor.matmul(ps[2 + hp][:, nt * 64:(nt + 1) * 64],
                                                          lhsT=hTs[hp][:, nt * 128:(nt + 1) * 128], rhs=w2v[:],
                                                          start=True, stop=True),
                          reads=[w2B, hTB[hp]], writes=[psB[2 + hp]], mark=(nt == 3))
                cx.op(cx.dve, lambda: nc.vector.tensor_copy(out=vcx[:, :, g, 0:64],
                                                            in_=ps[2 + hp][:, 0:256].rearrange("p (a b) -> p a b", a=4)),
                      reads=[psB[2 + hp]], writes=[vcB])
    for g in range(4):
        cx.op(cx.dve, lambda: nc.vector.tensor_copy(out=vcx[:, :, g, 64:192], in_=ovl_sb[:]), reads=[ovlB], writes=[vcB])
    cx.op(cx.dve, lambda: nc.vector.memset(vcx[:, :, :, 192:193], 1.0), writes=[vcB])
    if debug:
        cx.dma(cx.sp, [(kc_o, kcT[:])], reads=[kcB], is_out=True)
        cx.dma(cx.sp, [(vc_o, vcx[:])], reads=[vcB], is_out=True)
    cx.barrier()
    sCmp.close()

    sB = _scope(nc)
    ks = _sb(sB, "ks", [128, 64, 2, 128], BF16); ksB = Buf()
    vsx = _sb(sB, "vsx", [128, 64, 4, 65], BF16); vsB = Buf()
    kw = [_sb(sB, "kw%d" % j, [128, 12, 2, 128], BF16) for j in range(2)]; kwB = [Buf(), Buf()]
    vwx = [_sb(sB, "vwx%d" % j, [128, 12, 4, 65], BF16) for j in range(2)]; vwB = [Buf(), Buf()]
    esel = _sb(sB, "esel", [128, 64, 128], BF16)
    cmask = _sb(sB, "cmask", [128, NSLOT, 4, 128], BF16)
    dzmask = _sb(sB, "dzmask", [128, 8, 128], BF16)
    wmask = _sb(sB, "wmask", [128, 12, 128], BF16); mB = Buf()
    bonus = _sb(sB, "bonus", [128, NSLOT, 128], F32)
    qT = _sb(sB, "qT", [128, NSLOT, 8, 128], BF16)
    gat = _sb(sB, "gat", [128, NSLOT, 48], F32); qB = Buf()
    pT = [_sb(sB, "pT%d" % j, [128, 512], BF16) for j in range(3)]; pTB = [Buf() for _ in range(3)]
    imp = [_sb(sB, "imp%d" % j, [128, 128], F32) for j in range(2)]; impB = Buf()
    m8 = _sb(sB, "m8", [128, 16], F32)
    negq = _sb(sB, "negq", [128, 128], BF16); negqB = Buf()
    negT = _sb(sB, "negT", [128, 128], BF16); negTB = Buf()
    rz = _sb(sB, "rz", [128, 12], F32); rzB = Buf()
    rz2 = _sb(sB, "rz2", [128, 4], F32); rz2B = Buf()
    m8B = Buf()
    oacc = _sb(sB, "oacc", [128, 4, 64], F32); oaccB = Buf()
    otok = _sb(sB, "otok", [128, 1024], BF16); otokB = Buf()

    cx.op(cx.dve, lambda: nc.vector.memset(vsx[:, :, :, 64:65], 1.0), writes=[vsB])
    for j in range(2):
        cx.op(cx.dve, lambda: nc.vector.memset(vwx[j][:, :, :, 64:65], 1.0), writes=[vwB[j]])
    cx.dma(cx.sp, [(esel[:], esel_d), (cmask[:], cmask_d), (dzmask[:], dzmask_d), (wmask[:], wmask_d),
                   (bonus[:], bonus_d)], writes=[mB])
    cx.dma(cx.sp, [(qT[:], qT_d), (gat[:], gates_d)], writes=[qB])
    for kb in range(8):
        sl = slice(kb * 8, kb * 8 + 8)
        cx.dma(cx.sp, [(ks[:, sl].rearrange("p r g t -> p r (g t)"), KG[:, :, 2, kb].rearrange("r p g t -> p r (g t)"))],
               reads=[gB_], writes=[ksB])
        cx.dma(cx.sp, [(vsx[:, kb * 8 + r, :, 0:64], VG[r, :, 0, kb, :].rearrange("p (g e) -> p g e", g=4))
                       for r in range(8)], reads=[gB_], writes=[vsB])

    scnt = 0
    pcnt = 0
    UC = [ps[2], ps[3]]; UCB = [psB[2], psB[3]]
    US, USB = ps[4], psB[4]
    UW, UWB = ps[5], psB[5]

    def score_tile(lhsT_k, rhs_q, masks, kreads):
        nonlocal scnt, pcnt
        sb_ = scnt % 2; scnt += 1
        pj = pcnt % 3; pcnt += 1
        cx.op(cx.pe, lambda: nc.tensor.matmul(ps[sb_][:], lhsT=lhsT_k, rhs=rhs_q, start=True, stop=False),
              reads=kreads + [qB], writes=[psB[sb_]], mark=False)
        for mi, (ml, mr, mrd) in enumerate(masks):
            last = (mi == len(masks) - 1)
            cx.op(cx.pe, lambda: nc.tensor.matmul(ps[sb_][:].rearrange("p (h n) -> p h n", h=4), lhsT=ml,
                                                  rhs=mr.unsqueeze(1).to_broadcast([128, 4, 128]),
                                                  start=False, stop=last),
                  reads=mrd, writes=[psB[sb_]], mark=last)
        cx.op(cx.act, lambda: nc.scalar.activation(out=pT[pj][:], in_=ps[sb_][:], func=AF.Exp, scale=SCALE),
              reads=[psB[sb_]], writes=[pTB[pj]])
        return pT[pj], pTB[pj]

    pend = [None]

    def flush():
        if pend[0] is not None:
            f, a = pend[0]
            pend[0] = None
            f()
            if a is not None:
                a()

    def tile(lhsT_k, rhs_q, masks, kreads, pv_fn, after):
        p_, pB_ = score_tile(lhsT_k, rhs_q, masks, kreads)
        flush()
        pend[0] = ((lambda: pv_fn(p_, pB_)), after)

    for i in range(NSLOT):
        wj = i % 2
        lo = max(0, 8 * i - 4); r0 = lo - (8 * i - 4)
        nt_w = 8 * i + 8 - lo
        cx.dma(cx.sp, [(kw[wj][:, r0 + q_].rearrange("p g t -> p (g t)"),
                        KG[(lo + q_) % 8, :, 3, (lo + q_) // 8].rearrange("p g t -> p (g t)")) for q_ in range(nt_w)],
               reads=[gB_], writes=[kwB[wj]])
        cx.dma(cx.sp, [(vwx[wj][:, r0 + q_, :, 0:64],
                        VG[(lo + q_) % 8, :, 1, (lo + q_) // 8, :].rearrange("p (g e) -> p g e", g=4)) for q_ in range(nt_w)],
               reads=[gB_], writes=[vwB[wj]])
        for g in range(4):
            pb0 = 64 * (g % 2); gh = g // 2
            QT = qT[pb0:pb0 + 64, i, gh * 4:gh * 4 + 4, :].rearrange("p a b -> p (a b)")
            gsl = (lambda i_, g_: (lambda b: gat[:, i_, g_ * 12 + b:g_ * 12 + 12:3]))(i, g)

            def chain(i=i, g=g, gsl=gsl):
                for r in range(4):
                    cx.op(cx.dve, lambda: nc.vector.tensor_scalar(
                        out=rz[:, r:r + 1], in0=UC[r // 2][:, (r % 2) * 193 + 192:(r % 2) * 193 + 193],
                        scalar1=1e-30, scalar2=None, op0=ALU.max), reads=[UCB[r // 2]], writes=[rzB])
                cx.op(cx.dve, lambda: nc.vector.reciprocal(out=rz[:, 0:4], in_=rz[:, 0:4]), reads=[rzB], writes=[rzB])
                im = imp[0]
                cx.op(cx.dve, lambda: nc.vector.tensor_scalar(out=im[:], in0=UC[0][:, 64:192], scalar1=rz[:, 0:1],
                                                              scalar2=None, op0=ALU.mult),
                      reads=[UCB[0], rzB], writes=[impB])
                for r in range(1, 4):
                    cx.op(cx.dve, lambda: nc.vector.scalar_tensor_tensor(
                        out=im[:], in0=UC[r // 2][:, (r % 2) * 193 + 64:(r % 2) * 193 + 192], scalar=rz[:, r:r + 1],
                        in1=im[:], op0=ALU.mult, op1=ALU.add), reads=[UCB[r // 2], rzB, impB], writes=[impB])
                cx.op(cx.dve, lambda: nc.vector.tensor_add(out=im[:], in0=im[:], in1=bonus[:, i, :]),
                      reads=[impB, mB], writes=[impB])
                cx.op(cx.dve, lambda: nc.vector.max(out=m8[:, 0:8], in_=im[:]), reads=[impB], writes=[m8B])
                cx.op(cx.dve, lambda: nc.vector.match_replace(out=imp[1][:], in_to_replace=m8[:, 0:8], in_values=im[:],
                                                              imm_value=-1e30), reads=[impB, m8B], writes=[impB])
                cx.op(cx.dve, lambda: nc.vector.max(out=m8[:, 8:16], in_=imp[1][:]), reads=[impB], writes=[m8B])
                cx.op(cx.dve, lambda: nc.vector.tensor_scalar(out=negq[:], in0=im[:], scalar1=m8[:, 15:16], scalar2=NEG,
                                                              op0=ALU.is_lt, op1=ALU.mult),
                      reads=[impB, m8B], writes=[negqB])
                cx.op(cx.pe, lambda: nc.tensor.transpose(out=ptr[:, 0:128], in_=negq[:], identity=identb[:]),
                      reads=[negqB, cB], writes=[ptrB])
                cx.op(cx.act, lambda: nc.scalar.copy(out=negT[:], in_=ptr[:, 0:128]), reads=[ptrB], writes=[negTB])
                cx.op(cx.dve, lambda: nc.vector.tensor_mul(out=rz[:, 4:8], in0=rz[:, 0:4], in1=gsl(0)),
                      reads=[rzB, qB], writes=[rzB])
                for r in range(4):
                    cx.op(cx.dve, lambda: nc.vector.tensor_scalar(out=oacc[:, r, :],
                                                                  in0=UC[r // 2][:, (r % 2) * 193:(r % 2) * 193 + 64],
                                                                  scalar1=rz[:, 4 + r:5 + r], scalar2=None, op0=ALU.mult),
                          reads=[UCB[r // 2], rzB], writes=[oaccB])

            def combine(i=i, g=g, gsl=gsl):
                for bi, (U, UB) in enumerate(((UW, UWB), (US, USB))):
                    cx.op(cx.dve, lambda: nc.vector.reciprocal(out=rz2[:, 0:4], in_=U[:, 64:260:65]),
                          reads=[UB], writes=[rz2B])
                    cx.op(cx.dve, lambda: nc.vector.tensor_mul(out=rz2[:, 0:4], in0=rz2[:, 0:4], in1=gsl(2 - bi)),
                          reads=[rz2B, qB], writes=[rz2B])
                    for r in range(4):
                        dst = oacc[:, r, :] if bi == 0 else otok[:, g * 256 + r * 64:g * 256 + (r + 1) * 64]
                        cx.op(cx.dve, lambda: nc.vector.scalar_tensor_tensor(out=dst, in0=U[:, r * 65:r * 65 + 64],
                                                                             scalar=rz2[:, r:r + 1], in1=oacc[:, r, :],
                                                                             op0=ALU.mult, op1=ALU.add),
                              reads=[UB, rz2B, oaccB], writes=[oaccB, otokB])

            NT = i // 2 + 1
            for nt in range(NT):
                def pv_c(p_, pB_, nt=nt, g=g, NT=NT):
                    for r in range(4):
                        cx.op(cx.pe, lambda: nc.tensor.matmul(UC[r // 2][:, (r % 2) * 193:(r % 2) * 193 + 193],
                                                              lhsT=p_[:, r * 128:(r + 1) * 128], rhs=vcx[:, nt, g, :],
                                                              start=(nt == 0 and r % 2 == 0), stop=(nt == NT - 1),
                                                              skip_group_check=True),
                              reads=[pB_, vcB], writes=[UCB[r // 2]], mark=(r % 2 == 1))
                tile(kcT[pb0:pb0 + 64, gh, nt * 128:(nt + 1) * 128], QT,
                     [(identb[:], cmask[:, i, nt, :], [mB, cB])], [kcB], pv_c, chain if nt == NT - 1 else None)
            for rho in range(r0, 12):
                def pv_w(p_, pB_, rho=rho, g=g, wj=wj, r0=r0):
                    for r in range(4):
                        cx.op(cx.pe, lambda: nc.tensor.matmul(UW[:, r * 65:(r + 1) * 65], lhsT=p_[:, r * 128:(r + 1) * 128],
                                                              rhs=vwx[wj][:, rho, g, :], start=(rho == r0 and r == 0),
                                                              stop=(rho == 11), skip_group_check=True),
                              reads=[pB_, vwB[wj]], writes=[UWB], mark=(r == 3))
                tile(kw[wj][pb0:pb0 + 64, rho, gh, :], QT, [(identb[:], wmask[:, rho, :], [mB, cB])], [kwB[wj]], pv_w, None)
            nkt = 8 * i + 8
            for kt in range(nkt):
                masks = [(esel[:, kt, :], negT[:], [mB, negTB])]
                if kt >= 8 * i:
                    masks.append((identb[:], dzmask[:, kt - 8 * i, :], [mB, cB]))
                def pv_s(p_, pB_, kt=kt, g=g, nkt=nkt):
                    for r in range(4):
                        cx.op(cx.pe, lambda: nc.tensor.matmul(US[:, r * 65:(r + 1) * 65], lhsT=p_[:, r * 128:(r + 1) * 128],
                                                              rhs=vsx[:, kt, g, :], start=(kt == 0 and r == 0),
                                                              stop=(kt == nkt - 1), skip_group_check=True),
                              reads=[pB_, vsB], writes=[USB], mark=(r == 3))
                tile(ks[pb0:pb0 + 64, kt, gh, :], QT, masks, [ksB], pv_s, combine if kt == nkt - 1 else None)
        flush()
        for m in range(8):
            cx.op(cx.pe, lambda: nc.tensor.transpose(out=ptr[:, m * 128:(m + 1) * 128], in_=otok[:, m * 128:(m + 1) * 128],
                                                     identity=identb[:]),
                  reads=[otokB, cB], writes=[ptrB], mark=(m == 7))
        cx.op(cx.act, lambda: nc.scalar.copy(out=oT[:, :, i * 128:(i + 1) * 128],
                                             in_=ptr[:].rearrange("p (a b) -> p a b", a=8)),
              reads=[ptrB], writes=[oTB])
    if debug:
        cx.dma(cx.sp, [(oT_o, oT[:])], reads=[oTB], is_out=True)
    cx.barrier()
    sB.close()
    sAtt.close()

    sX = _scope(nc)
    xT = _sb(sX, "xT", [128, DC, TL], F32); xTB = [Buf() for _ in range(DC)]
    cx.dma(cx.sp, [(xT[:, 0:8], xT_d[:, 0:8])], writes=xTB[0:8])
    cx.dma(cx.sp, [(xT[:, 8:16], xT_d[:, 8:16])], writes=xTB[8:16])
    sC = _scope(nc)
    cT = _sb(sC, "cT", [128, 8, TL], BF16); cTB = Buf()
    cx.dma(cx.sp, [(cT[:], cT_d)], writes=[cTB])
    wso = WStream(cx, sC, "wo", DC, 512, 2)
    cnt = 0
    for wt in range(4):
        w, wb = wso.load(w_out_d, wt * 512, 512)
        for m in range(4):
            och = wt * 4 + m
            for tb in range(2):
                pb = cnt % 7; cnt += 1
                for k in range(DC):
                    rhs = cT[:, k, tb * 512:(tb + 1) * 512] if k < 8 else oT[:, k - 8, tb * 512:(tb + 1) * 512]
                    cx.op(cx.pe, lambda: nc.tensor.matmul(ps[pb][:], lhsT=w[:, k, m * 128:(m + 1) * 128], rhs=rhs,
                                                          start=(k == 0), stop=(k == DC - 1)),
                          reads=[wb, cTB, oTB], writes=[psB[pb]], mark=(k == DC - 1))
                cx.op(cx.dve, lambda: nc.vector.tensor_add(out=xT[:, och, tb * 512:(tb + 1) * 512], in0=ps[pb][:],
                                                           in1=xT[:, och, tb * 512:(tb + 1) * 512]),
                      reads=[psB[pb]], writes=[xTB[och]])
    cx.barrier()
    sC.close()

    sM = _scope(nc)
    mlp_stage(cx, sM, xT, xTB, gm_sb, w1_d, w2_d, ps, psB, onesD, epsT)
    cx.dma(cx.sp, [(xT_o, xT[:])], reads=xTB, is_out=True)
    if T.get("halo_out") is not None:
        cx.dma(cx.sp, [(T["halo_out"][:, k], xT[:, k, :].rearrange("p (i t) -> p i t", i=NSLOT)[:, :, 128 - PH:128])
                       for k in range(DC)], reads=xTB, writes=[T["haloB"]], is_out=True)
        T["after_halo"]()
    cx.barrier()
    sM.close(); sX.close(); s0.close()


PH = 16
PW = PH + 128


def emit_p3(nc, cx, T, ps, psB):
    xT_d, icnt_d, pw_d, gv_d, identf_d = T["xT2"], T["invcnt"], T["pool_w"], T["gvec"], T["ident"]
    w1_d, w2_d, out_o = T["w_mlp_in1"], T["w_mlp_out1"], T["out"]

    s0 = _scope(nc)
    identf = _sb(s0, "identf", [128, 128], F32); cB = Buf()
    onesD = _sb(s0, "onesD", [128, 128], F32)
    epsT = _sb(s0, "epsT", [128, 1], F32)
    gv = _sb(s0, "gv", [128, 4, DC], F32)
    cx.dma(cx.sp, [(identf[:], identf_d), (gv[:], gv_d)], writes=[cB])
    cx.op(cx.dve, lambda: nc.vector.memset(onesD[:], 1.0 / D), writes=[cB])
    cx.op(cx.dve, lambda: nc.vector.memset(epsT[:], EPS), writes=[cB])
    xT = _sb(s0, "xT", [128, DC, TL], F32); xTB = [Buf() for _ in range(DC)]
    cx.dma(cx.sp, [(xT[:, 0:8], xT_d[:, 0:8])], writes=xTB[0:8])
    cx.dma(cx.sp, [(xT[:, 8:16], xT_d[:, 8:16])], writes=xTB[8:16])

    sP = _scope(nc)
    xh = _sb(sP, "xh", [128, DC, NSLOT * PH], F32); xhB = [Buf() for _ in range(DC)]
    T["load_halo"](cx, sP, xh, xhB)
    sq = [_sb(sP, "sq%d" % j, [128, TL], F32) for j in range(2)]; sqB = [Buf(), Buf()]
    rstd = _sb(sP, "rstd", [128, TL], F32); rstdB = Buf()
    rstdh = _sb(sP, "rstdh", [128, NSLOT * PH], F32); rstdhB = Buf()
    icnt = _sb(sP, "icnt", [128, 4, TL], F32); icB = Buf()
    cx.dma(cx.sp, [(icnt[:], icnt_d)], writes=[icB])
    xn = _sb(sP, "xn", [128, 4, NSLOT, PW], F32); xnB = Buf()
    sa = _sb(sP, "sa", [128, 4, NSLOT, PW], F32); saB = Buf()
    sb2 = _sb(sP, "sb2", [128, 4, NSLOT, PW], F32); sbB = Buf()
    pT = _sb(sP, "pT", [128, 4, TL], BF16); pTB = Buf()
    wsp = WStream(cx, sP, "pw", 4, 512, 2)
    rms_stats_fm(cx, lambda k: xT[:, k, :], xTB, TL, onesD,
                 dict(sq=sq, sqb=sqB, ps=ps[0:2], psb=psB[0:2], rstd=rstd, rstdb=rstdB, eps=epsT))
    rms_stats_fm(cx, lambda k: xh[:, k, :], xhB, NSLOT * PH, onesD,
                 dict(sq=sq, sqb=sqB, ps=ps[2:3], psb=psB[2:3], rstd=rstdh, rstdb=rstdhB, eps=epsT))
    cnt = 0
    v8 = lambda ap: ap.rearrange("p (i t) -> p i t", i=NSLOT)
    for gi in range(4):
        w = 2 ** (gi + 1)
        for kk in range(4):
            k = gi * 4 + kk
            rms_apply_fm(cx, k, v8(xT[:, k, :]), xTB[k], gv[:, 0, :], xn[:, kk, :, PH:PW], xnB, v8(rstd[:, :]), rstdB)
            rms_apply_fm(cx, k, v8(xh[:, k, :]), xhB[k], gv[:, 0, :], xn[:, kk, :, 0:PH], xnB, v8(rstdh[:, :]), rstdhB)
        fl = lambda t, lo, hi: t[:].rearrange("p a i w -> p (a i) w")[:, :, lo:hi]
        src, srcB = xn, xnB
        bufs = [(sa, saB), (sb2, sbB)]
        sh = 1
        step = 0
        while sh < w:
            dst, dstB = bufs[step % 2]
            cx.op(cx.dve, lambda: nc.vector.tensor_add(out=fl(dst, sh, PW), in0=fl(src, sh, PW), in1=fl(src, 0, PW - sh)),
                  reads=[srcB], writes=[dstB])
            src, srcB = dst, dstB
            sh *= 2; step += 1
        oth, othB = bufs[step % 2]
        for kk in range(4):
            cx.op(cx.dve, lambda: nc.vector.tensor_mul(out=oth[:, kk, :, PH:PW], in0=src[:, kk, :, PH:PW],
                                                       in1=v8(icnt[:, gi, :])), reads=[srcB, icB], writes=[othB])
            cx.op(cx.dve, lambda: nc.vector.tensor_sub(out=v8(pT[:, kk, :]), in0=oth[:, kk, :, PH:PW],
                                                       in1=xn[:, kk, :, PH:PW]), reads=[othB, xnB], writes=[pTB])
        wt, wb = wsp.load(pw_d[gi], 0, 512, kch=4)
        for m in range(4):
            och = gi * 4 + m
            for tb in range(2):
                pb = 3 + cnt % 4; cnt += 1
                for kk in range(4):
                    cx.op(cx.pe, lambda: nc.tensor.matmul(ps[pb][:], lhsT=wt[:, kk, m * 128:(m + 1) * 128],
                                                          rhs=pT[:, kk, tb * 512:(tb + 1) * 512],
                                                          start=(kk == 0), stop=(kk == 3)),
                          reads=[wb, pTB], writes=[psB[pb]], mark=(kk == 3))
                cx.op(cx.dve, lambda: nc.vector.scalar_tensor_tensor(
                    out=xT[:, och, tb * 512:(tb + 1) * 512], in0=ps[pb][:], scalar=gv[:, 1, och:och + 1],
                    in1=xT[:, och, tb * 512:(tb + 1) * 512], op0=ALU.mult, op1=ALU.add),
                    reads=[psB[pb], cB], writes=[xTB[och]])
    cx.barrier()
    sP.close()

    sM = _scope(nc)
    gm = gv[:, 2, :]
    mlp_stage(cx, sM, xT, xTB, gm, w1_d, w2_d, ps, psB, onesD, epsT)
    cx.barrier()
    sM.close()

    sF = _scope(nc)
    sq = [_sb(sF, "fsq%d" % j, [128, TL], F32) for j in range(2)]; sqB = [Buf(), Buf()]
    rstd = _sb(sF, "frstd", [128, TL], F32); rstdB = Buf()
    otk = [_sb(sF, "otk%d" % j, [128, D], F32) for j in range(2)]; otkB = [Buf(), Buf()]
    rms_norm_fm(cx, lambda k: xT[:, k, :], xTB, TL, gv[:, 3, :], lambda k: xT[:, k, :], xTB, onesD,
                dict(sq=sq, sqb=sqB, ps=ps[0:2], psb=psB[0:2], rstd=rstd, rstdb=rstdB, eps=epsT))
    cnt = 0
    for i in range(NSLOT):
        j = i % 2
        for k4 in range(4):
            pb = 2 + cnt % 5; cnt += 1
            for kk in range(4):
                k = k4 * 4 + kk
                cx.op(cx.pe, lambda: nc.tensor.transpose(out=ps[pb][:, kk * 128:(kk + 1) * 128],
                                                         in_=xT[:, k, i * 128:(i + 1) * 128], identity=identf[:]),
                      reads=[xTB[k], cB], writes=[psB[pb]], mark=(kk == 3))
            eng = cx.act if k4 % 2 else cx.dve
            if k4 % 2:
                cx.op(cx.act, lambda: nc.scalar.copy(out=otk[j][:, k4 * 512:(k4 + 1) * 512], in_=ps[pb][:]),
                      reads=[psB[pb]], writes=[otkB[j]])
            else:
                cx.op(cx.dve, lambda: nc.vector.tensor_copy(out=otk[j][:, k4 * 512:(k4 + 1) * 512], in_=ps[pb][:]),
                      reads=[psB[pb]], writes=[otkB[j]])
        cx.dma(cx.sp, [(out_o[i], otk[j][:])], reads=[otkB[j]], is_out=True)
    cx.barrier()
    sF.close(); s0.close()

def _tile_of(c, i):
    return 8 * i + c


def host_p1_inputs(x, mix_norm, w_in, conv_w, conv_b, conv_ln_g, conv_ln_b):
    x2 = x.reshape(S, D)
    xpad = np.concatenate([np.zeros((HALO, D), np.float32), x2], 0)
    fm = lambda v, n: np.ascontiguousarray(v.reshape(n, 128).T)
    common = {
        "w_in": np.ascontiguousarray(w_in[0]),
        "g0": fm(mix_norm[0], DC),
        "convw": np.ascontiguousarray(conv_w[0].reshape(31, 8, 128).transpose(2, 1, 0)),
        "cpar": np.ascontiguousarray(np.stack([fm(conv_b[0], 8), fm(conv_ln_g[0], 8), fm(conv_ln_b[0], 8)], 1)),
        "ident": np.eye(128, dtype=np.float32),
    }
    maps = []
    for c in range(NCORES):
        xh = np.stack([xpad[128 * _tile_of(c, i):128 * _tile_of(c, i) + HALO + 128] for i in range(NSLOT)], 0)
        m = dict(common)
        m["xh"] = np.ascontiguousarray(xh)
        maps.append(m)
    return maps


BF = ml_dtypes.bfloat16


def host_masks(c):
    kl = np.arange(128)[:, None]
    ql = np.arange(128)[None, :]
    cmask = np.zeros((128, NSLOT, 4, 128), np.float32)
    bonus = np.zeros((128, NSLOT, 128), np.float32)
    for i in range(NSLOT):
        t = 128 * (8 * i + c) + np.arange(128)
        for nt in range(4):
            n = 128 * nt + np.arange(128)
            vis = (16 * n[:, None] + 31 <= t[None, :]) & (n[:, None] <= 510)
            cmask[:, i, nt, :] = np.where(vis, 0.0, NEG)
        cur = t // 64
        j = np.arange(128)[None, :]
        forced = (j == 0) | (j == cur[:, None]) | (j == cur[:, None] - 1)
        bonus[:, i, :] = np.where(forced, 1000.0, 0.0)
    dz = np.zeros((128, 8, 128), np.float32)
    for z in range(8):
        dz[:, z, :] = np.where(128 * z + kl <= 128 * c + ql, 0.0, NEG)
    wm = np.zeros((128, 12, 128), np.float32)
    for rho in range(12):
        dlt = 128 * (c + 4 - rho)
        d = kl - ql
        wm[:, rho, :] = np.where((d <= dlt) & (d > dlt - 512), 0.0, NEG)
    return {"cmask": cmask.astype(BF), "dzmask": dz.astype(BF), "wmask": wm.astype(BF), "bonus": bonus}


def host_shared_consts():
    esel = np.zeros((128, 64, 128), np.float32)
    for kt in range(64):
        for b in range(2):
            esel[2 * kt + b, kt, 64 * b:64 * b + 64] = 1.0
    ovl = np.zeros((128, 4, 128), np.float32)
    for nt in range(4):
        n = 128 * nt + np.arange(128)
        j = np.arange(128)
        ov = (16 * n[:, None] <= 64 * j[None, :] + 63) & (16 * n[:, None] + 31 >= 64 * j[None, :]) & (n[:, None] <= 510)
        ovl[:, nt, :] = ov
    return {"esel": esel.astype(BF), "ovl": ovl.astype(BF), "identb": np.eye(128, dtype=np.float32).astype(BF)}


def _fm(v, n):
    return np.ascontiguousarray(np.asarray(v).reshape(n, 128).T)


def host_inputs(inp):
    m1 = host_p1_inputs(inp["x"], inp["mix_norm"], inp["w_in"], inp["conv_w"], inp["conv_b"],
                        inp["conv_ln_g"], inp["conv_ln_b"])
    shared = host_shared_consts()
    gvec = np.ascontiguousarray(np.stack([_fm(inp["mix_norm"][1], DC), _fm(inp["pool_scale"][0], DC),
                                          _fm(inp["mlp_norm"][1], DC), _fm(inp["final_norm"], DC)], 1))
    shared.update({
        "cw1_k": np.ascontiguousarray(inp["cmp_k_w1"][0]), "cw1_v": np.ascontiguousarray(inp["cmp_v_w1"][0]),
        "cw2_k": np.ascontiguousarray(inp["cmp_k_w2"][0]), "cw2_v": np.ascontiguousarray(inp["cmp_v_w2"][0]),
        "posT": np.ascontiguousarray(np.stack([inp["cmp_pos_k"][0].T, inp["cmp_pos_v"][0].T], 1)),
        "w_out": np.ascontiguousarray(inp["w_out"][0]),
        "w_mlp_in0": np.ascontiguousarray(inp["w_mlp_in"][0]), "w_mlp_out0": np.ascontiguousarray(inp["w_mlp_out"][0]),
        "w_mlp_in1": np.ascontiguousarray(inp["w_mlp_in"][1]), "w_mlp_out1": np.ascontiguousarray(inp["w_mlp_out"][1]),
        "g_mlp0": _fm(inp["mlp_norm"][0], DC),
        "pool_w": np.ascontiguousarray(inp["pool_w"][0]), "gvec": gvec,
    })
    maps = []
    for c in range(NCORES):
        m = dict(m1[c])
        m.update(shared)
        m.update(host_masks(c))
        icnt = np.zeros((4, TL), np.float32)
        for i in range(NSLOT):
            pos1 = 128 * (8 * i + c) + np.arange(128) + 1
            for gi in range(4):
                icnt[gi, i * 128:(i + 1) * 128] = 1.0 / np.minimum(pos1, 2 ** (gi + 1))
        m["invcnt"] = np.ascontiguousarray(np.broadcast_to(icnt[None], (128, 4, TL)))
        sel = np.zeros((128, 9), np.float32)
        if c > 0:
            sel[:, c - 1] = 1.0
        else:
            sel[:, 8] = 1.0
        m["halosel"] = sel
        maps.append(m)
    return maps


def build_fused(stage=3):
    nc = bass.Bass("TRN2", target_bir_lowering=False)
    cx = Ctx(nc)
    din = lambda name, shape, dt: nc.dram_tensor(name, shape, dt, kind="ExternalInput").ap()
    itn = lambda name, shape, dt: nc.dram_tensor(name, shape, dt)
    T = {}
    for name, shape, dt in [
        ("xh", [NSLOT, HALO + 128, D], F32), ("w_in", [D, IN_COLS], F32), ("g0", [128, DC], F32),
        ("convw", [128, 8, 31], F32), ("cpar", [128, 3, 8], F32), ("ident", [128, 128], F32),
        ("cw1_k", [2048, 128], F32), ("cw1_v", [2048, 128], F32), ("cw2_k", [128, 64], F32), ("cw2_v", [128, 64], F32),
        ("posT", [64, 2, 32], F32), ("cmask", [128, NSLOT, 4, 128], BF16), ("dzmask", [128, 8, 128], BF16),
        ("wmask", [128, 12, 128], BF16), ("esel", [128, 64, 128], BF16), ("bonus", [128, NSLOT, 128], F32),
        ("ovl", [128, 4, 128], BF16), ("identb", [128, 128], BF16), ("w_out", [D, D], F32),
        ("w_mlp_in0", [D, DFF], F32), ("w_mlp_out0", [DFF, D], F32), ("g_mlp0", [128, DC], F32),
        ("w_mlp_in1", [D, DFF], F32), ("w_mlp_out1", [DFF, D], F32),
        ("invcnt", [128, 4, TL], F32), ("pool_w", [4, 512, 512], F32), ("gvec", [128, 4, DC], F32),
        ("halosel", [128, 9], F32),
    ]:
        T[name] = din(name, shape, dt)
    T["out"] = nc.dram_tensor("out", [NSLOT, 128, D], F32, kind="ExternalOutput").ap()
    T["xT"] = itn("h_xT", [128, DC, TL], F32).ap()
    T["xT2"] = itn("h_xT2", [128, DC, TL], F32).ap()
    T["qT"] = itn("h_qT", [128, NSLOT, 8, 128], BF16).ap()
    T["cT"] = itn("h_cT", [128, 8, TL], BF16).ap()
    T["gates"] = itn("h_gates", [128, NSLOT, 48], F32).ap()
    KW = 4 * NSLOT * 2 * 128
    VW = 2 * NSLOT * 256
    kvloc = itn("kvloc", [128, (KW + VW) // 2], F32)
    kvgat = itn("kvgat", [NCORES * 128, (KW + VW) // 2], F32)
    lb = kvloc.bitcast(BF16).ap()
    gb = kvgat.bitcast(BF16).ap().rearrange("(r p) c -> r p c", r=NCORES)
    T["kfm"] = lb[:, 0:KW].rearrange("p (k i g t) -> p k i g t", k=4, i=NSLOT, g=2)
    T["vtm"] = lb[:, KW:KW + VW].rearrange("p (k i e) -> p k i e", k=2, i=NSLOT)
    T["kfm_g"] = gb[:, :, 0:KW].rearrange("r p (k i g t) -> r p k i g t", k=4, i=NSLOT, g=2)
    T["vtm_g"] = gb[:, :, KW:KW + VW].rearrange("r p (k i e) -> r p k i e", k=2, i=NSLOT)
    T["kvB"] = Buf(); T["kvgB"] = Buf()
    hloc = itn("hloc", [128, DC * NSLOT * PH], F32)
    hgat = itn("hgat", [NCORES * 128, DC * NSLOT * PH], F32)
    T["halo_out"] = hloc.ap().rearrange("p (k i t) -> p k i t", k=DC, i=NSLOT)
    H = hgat.ap().rearrange("(r p) (k i t) -> r p k i t", r=NCORES, k=DC, i=NSLOT)
    T["haloB"] = Buf(); hgB = Buf()
    rg = [list(range(NCORES))]

    def after_kv():
        cx.op(cx.pool, lambda: nc.gpsimd.collective_compute("AllGather", ALU.bypass, replica_groups=rg,
                                                            ins=[kvloc.ap().opt()], outs=[kvgat.ap().opt()]),
              reads=[T["kvB"]], writes=[T["kvgB"]])

    def after_halo():
        cx.op(cx.pool, lambda: nc.gpsimd.collective_compute("AllGather", ALU.bypass, replica_groups=rg,
                                                            ins=[hloc.ap().opt()], outs=[hgat.ap().opt()]),
              reads=[T["haloB"]], writes=[hgB])

    def load_halo(cx_, st, xh, xhB):
        sel = _sb(st, "hsel", [128, 9], F32); selB = Buf()
        cand = _sb(st, "hcand", [128, 9, 4, NSLOT * PH], F32); candB = Buf()
        cx.dma(cx.sp, [(sel[:], T["halosel"])], writes=[selB])
        for kg in range(4):
            ks_ = slice(kg * 4, kg * 4 + 4)
            cx.op(cx.dve, lambda: nc.vector.memset(cand[:, 8, :, 0:PH], 0.0), writes=[candB])
            prs = [(cand[:, j], H[j, :, ks_].rearrange("p k i t -> p k (i t)")) for j in range(8)]
            prs.append((cand[:, 8, :, PH:NSLOT * PH], H[7, :, ks_, 0:NSLOT - 1, :].rearrange("p k i t -> p k (i t)")))
            cx.dma(cx.sp, prs, reads=[hgB], writes=[candB])
            cx.op(cx.dve, lambda: nc.vector.tensor_scalar(out=xh[:, ks_, :], in0=cand[:, 0], scalar1=sel[:, 0:1],
                                                          scalar2=None, op0=ALU.mult),
                  reads=[candB, selB], writes=xhB[ks_])
            for j in range(1, 9):
                cx.op(cx.dve, lambda: nc.vector.scalar_tensor_tensor(out=xh[:, ks_, :], in0=cand[:, j],
                                                                     scalar=sel[:, j:j + 1], in1=xh[:, ks_, :],
                                                                     op0=ALU.mult, op1=ALU.add),
                      reads=[candB, selB], writes=xhB[ks_])

    T["after_kv"] = after_kv
    T["after_halo"] = after_halo
    T["load_halo"] = load_halo
    ps = [_ps(nc, "ps%d" % j, [128, 512], F32) for j in range(7)]; psB = [Buf() for _ in range(7)]
    ptr = _ps(nc, "ptr", [128, 1024], BF16); ptrB = Buf()
    emit_p1(nc, cx, T, ps, psB)
    if stage == 1:
        dbg = nc.dram_tensor("dbg", [NCORES * 128, (KW + VW) // 2], F32, kind="ExternalOutput").ap()
        cx.dma(cx.sp, [(dbg, kvgat.ap())], reads=[T["kvgB"]], is_out=True)
        cx.finish()
        return nc
    emit_p2(nc, cx, T, ps, psB, ptr, ptrB)
    if stage == 2:
        dbg = nc.dram_tensor("dbg", [NCORES * 128, DC * NSLOT * PH], F32, kind="ExternalOutput").ap()
        cx.dma(cx.sp, [(dbg, hgat.ap())], reads=[hgB], is_out=True)
        dbg2 = nc.dram_tensor("dbg2", [128, DC, TL], F32, kind="ExternalOutput").ap()
        cx.dma(cx.sp, [(dbg2, T["xT2"])], is_out=True)
        cx.finish()
        return nc
    emit_p3(nc, cx, T, ps, psB)
    cx.finish()
    return nc


_PROGS = {}


def kernel(x, mix_norm, mlp_norm, w_mlp_in, w_mlp_out, w_in, w_out, conv_w, conv_b,
           conv_ln_g, conv_ln_b, cmp_pos_k, cmp_pos_v, cmp_k_w1, cmp_k_w2,
           cmp_v_w1, cmp_v_w2, pool_w, pool_scale, final_norm):
    inp = dict(x=x, mix_norm=mix_norm, mlp_norm=mlp_norm, w_mlp_in=w_mlp_in, w_mlp_out=w_mlp_out, w_in=w_in,
               w_out=w_out, conv_w=conv_w, conv_b=conv_b, conv_ln_g=conv_ln_g, conv_ln_b=conv_ln_b,
               cmp_pos_k=cmp_pos_k, cmp_pos_v=cmp_pos_v, cmp_k_w1=cmp_k_w1, cmp_k_w2=cmp_k_w2,
               cmp_v_w1=cmp_v_w1, cmp_v_w2=cmp_v_w2, pool_w=pool_w, pool_scale=pool_scale, final_norm=final_norm)
    inp = {k: np.asarray(v, dtype=np.float32) for k, v in inp.items()}
    if "fused" not in _PROGS:
        _PROGS["fused"] = build_fused()
    maps = host_inputs(inp)
    res = run_bass_kernel_spmd(_PROGS["fused"], maps, core_ids=list(range(NCORES))).results
    out = np.zeros((S, D), np.float32)
    for c in range(NCORES):
        o = np.asarray(res[c]["out"])
        for i in range(NSLOT):
            kt = 8 * i + c
            out[128 * kt:128 * kt + 128] = o[i]
    return out.reshape(1, S, D)
```

```python
import numpy as np
from contextlib import ExitStack
import ml_dtypes
import concourse.bass as bass
import concourse.mybir as mybir
from concourse.bass_utils import run_bass_kernel_spmd

F32 = mybir.dt.float32
BF16 = mybir.dt.bfloat16
ALU = mybir.AluOpType
AF = mybir.ActivationFunctionType
AX = mybir.AxisListType

NCORES = 8
D = 2048
S = 8192
NSLOT = 8
TL = 1024
DC = 16
HALO = 32
IN_COLS = 4656
DFF = 8192
EPS = 1e-6
NEG = -30000.0
SCALE = 0.125


class Buf:
    __slots__ = ("name", "w", "r")

    def __init__(self, name=""):
        self.name = name
        self.w = None
        self.r = []


class EngS:
    def __init__(self, nc, eng, name, ndma=0):
        self.nc = nc
        self.e = eng
        self.name = name
        self.sem = nc.alloc_semaphore("sem_" + name)
        self.count = 0
        self.waited = {}
        self.dsems = [[nc.alloc_semaphore("dsem_%s_%d" % (name, i)), 0] for i in range(ndma)]
        self.dnext = 0

    def wait(self, tk):
        if tk is None:
            return
        sem, val = tk
        key = id(sem)
        if self.waited.get(key, 0) >= val:
            return
        self.waited[key] = val
        self.e.wait_ge(sem, val)


class Ctx:
    def __init__(self, nc):
        self.nc = nc
        self.pe = EngS(nc, nc.tensor, "pe")
        self.act = EngS(nc, nc.scalar, "act", ndma=4)
        self.dve = EngS(nc, nc.vector, "dve")
        self.pool = EngS(nc, nc.gpsimd, "pool", ndma=8)
        self.sp = EngS(nc, nc.sync, "sp", ndma=12)
        self.out_tickets = []

    def _deps(self, E, reads, writes, same_engine_ok=False):
        for b in reads:
            if b.w is not None:
                if not (same_engine_ok and b.w[0] is E.sem):
                    E.wait(b.w)
        for b in writes:
            if b.w is not None:
                if not (same_engine_ok and b.w[0] is E.sem):
                    E.wait(b.w)
            for t in b.r:
                if not (same_engine_ok and t[0] is E.sem):
                    E.wait(t)

    def _commit(self, tk, reads, writes):
        for b in reads:
            b.r.append(tk)
            if len(b.r) > 24:
                last = {}
                for t in b.r:
                    k = id(t[0])
                    if k not in last or last[k][1] < t[1]:
                        last[k] = t
                b.r = list(last.values())
        for b in writes:
            b.w = tk
            b.r = []

    def op(self, E, fn, reads=(), writes=(), mark=True):
        self._deps(E, reads, writes, same_engine_ok=(E is self.pe))
        ins = fn()
        if mark:
            E.count += 1
            ins.then_inc(E.sem, 1)
            tk = (E.sem, E.count)
        else:
            tk = (E.sem, E.count + 1)
        self._commit(tk, reads, writes)
        return tk

    def dma(self, Q, pairs, reads=(), writes=(), is_out=False):
        self._deps(Q, reads, writes)
        slot = Q.dsems[Q.dnext]
        Q.dnext = (Q.dnext + 1) % len(Q.dsems)
        sem = slot[0]
        Q.wait((sem, slot[1]))
        for (o, i) in pairs:
            Q.e.dma_start(out=o, in_=i).then_inc(sem, 16)
            slot[1] += 16
        tk = (sem, slot[1])
        self._commit(tk, reads, writes)
        if is_out:
            self.out_tickets.append(tk)
        return tk

    def barrier(self):
        engs = (self.pe, self.act, self.dve, self.pool, self.sp)
        for E in engs:
            for Fo in engs:
                if Fo is not E and Fo.count:
                    E.wait((Fo.sem, Fo.count))
                for sv in Fo.dsems:
                    if sv[1]:
                        E.wait((sv[0], sv[1]))

    def finish(self):
        for tk in self.out_tickets:
            self.sp.wait(tk)
        for E in (self.pe, self.act, self.dve, self.pool):
            if E.count:
                self.sp.wait((E.sem, E.count))
            for s, v in E.dsems:
                if v:
                    self.sp.wait((s, v))
        for s, v in self.sp.dsems:
            if v:
                self.sp.wait((s, v))


_SBN = [0]


def _sb(st, name, shape, dt):
    _SBN[0] += 1
    return st.enter_context(st.nc.sbuf_tensor("s%d_%s" % (_SBN[0], name), shape, dt))


def _scope(nc):
    st = ExitStack()
    st.nc = nc
    return st


def _ps(nc, name, shape, dt):
    return nc.alloc_psum_tensor(name, shape, dt)


class WStream:
    def __init__(self, cx, st, name, kch, cols, nbuf):
        nc = st
        self.cx = cx
        self.kch = kch
        self.cols = cols
        self.t = [_sb(nc, "%s_%d" % (name, j), [128, kch, cols], BF16) for j in range(nbuf)]
        self.b = [Buf() for _ in range(nbuf)]
        self.n = 0

    def load(self, w_ap, c0, ncols, kch=None, r0=0):
        kch = kch or self.kch
        j = self.n % len(self.t)
        self.n += 1
        src = w_ap[r0:r0 + kch * 128, c0:c0 + ncols].rearrange("(k p) c -> p k c", p=128)
        self.cx.dma(self.cx.pool, [(self.t[j][:, 0:kch, 0:ncols], src)], writes=[self.b[j]])
        return self.t[j], self.b[j]

    def load_q(self, w_ap, c0):
        j = self.n % len(self.t)
        self.n += 1
        src = w_ap[:, c0:c0 + 512].rearrange("(k p) (two r e) -> p k r two e", p=128, two=2, r=4)
        dst = self.t[j][:, :, 0:512].rearrange("p k (r two e) -> p k r two e", two=2, r=4)
        self.cx.dma(self.cx.pool, [(dst[:, :, r, two], src[:, :, r, two]) for r in range(4) for two in range(2)], writes=[self.b[j]])
        return self.t[j], self.b[j]


def rms_stats_fm(cx, xT_fn, xbufs, ncols, onesD, scratch):
    nc = cx.nc
    sq, sqb = scratch["sq"], scratch["sqb"]
    ps, psb = scratch["ps"], scratch["psb"]
    rstd, rstdb = scratch["rstd"], scratch["rstdb"]
    nblk = (ncols + 511) // 512
    for k in range(DC):
        j = k % 2
        cx.op(cx.act, lambda: nc.scalar.activation(out=sq[j][:, 0:ncols], in_=xT_fn(k), func=AF.Square),
              reads=[xbufs[k]], writes=[sqb[j]])
        for b in range(nblk):
            c0, c1 = b * 512, min(ncols, b * 512 + 512)
            cx.op(cx.pe, lambda: nc.tensor.matmul(ps[b][:, 0:c1 - c0], lhsT=onesD[:], rhs=sq[j][:, c0:c1],
                                                  start=(k == 0), stop=(k == DC - 1)),
                  reads=[sqb[j]], writes=[psb[b]])
    for b in range(nblk):
        c0, c1 = b * 512, min(ncols, b * 512 + 512)
        cx.op(cx.act, lambda: nc.scalar.activation(out=rstd[:, c0:c1], in_=ps[b][:, 0:c1 - c0], func=AF.Sqrt,
                                                   bias=scratch["eps"][:, 0:1], scale=1.0),
              reads=[psb[b]], writes=[rstdb])
    cx.op(cx.dve, lambda: nc.vector.reciprocal(out=rstd[:, 0:ncols], in_=rstd[:, 0:ncols]),
          reads=[rstdb], writes=[rstdb])


def rms_apply_fm(cx, k, xT_ap, xbuf, g_sb, out_ap, obuf, rstd_ap, rstdb):
    nc = cx.nc
    cx.op(cx.dve, lambda: nc.vector.scalar_tensor_tensor(out=out_ap, in0=xT_ap, scalar=g_sb[:, k:k + 1],
                                                         in1=rstd_ap, op0=ALU.mult, op1=ALU.mult),
          reads=[xbuf, rstdb], writes=[obuf])


def rms_norm_fm(cx, xT_fn, xbufs, ncols, g_sb, out_fn, obufs, onesD, scratch):
    rms_stats_fm(cx, xT_fn, xbufs, ncols, onesD, scratch)
    for k in range(DC):
        rms_apply_fm(cx, k, xT_fn(k), xbufs[k], g_sb, out_fn(k), obufs[k], scratch["rstd"][:, 0:ncols], scratch["rstdb"])


def emit_p1(nc, cx, T, ps, psB):
    W = HALO + 128
    NCOL = NSLOT * W
    xh, w_in, g0, convw, cpar, ident_d = T["xh"], T["w_in"], T["g0"], T["convw"], T["cpar"], T["ident"]
    xT_o, qT_o, cT_o, gates_o, kfm_o, vtm_o = T["xT"], T["qT"], T["cT"], T["gates"], T["kfm"], T["vtm"]
    kvB = T["kvB"]

    s0 = _scope(nc)
    ident = _sb(s0, "ident", [128, 128], F32); identB = Buf()
    onesD = _sb(s0, "onesD", [128, 128], F32); onesB = Buf()
    ones1k = _sb(s0, "ones1k", [128, 128], F32)
    epsT = _sb(s0, "epsT", [128, 1], F32)
    g_sb = _sb(s0, "g_sb", [128, DC], F32); gB = Buf()
    cw_sb = _sb(s0, "cw_sb", [128, 8, 31], F32)
    cp_sb = _sb(s0, "cp_sb", [128, 3, 8], F32)
    cfm = _sb(s0, "cfm", [128, 8, NSLOT, W], F32); cfmB = [Buf() for _ in range(8)]
    s1 = _scope(nc)
    xnO = _sb(s1, "xnO", [128, DC, TL], BF16); xnB = [Buf() for _ in range(DC)]
    xnH = _sb(s1, "xnH", [128, DC, NSLOT * HALO], BF16); xnHB = [Buf() for _ in range(DC)]
    sA = _scope(nc)
    xtok = [_sb(sA, "xtok%d" % j, [128, D], F32) for j in range(1)] * 2; xtokB = [Buf()] * 2
    xhal = [_sb(sA, "xhal%d" % j, [HALO, D], F32) for j in range(1)] * 2; xhalB = [Buf()] * 2
    xTo = _sb(sA, "xTo", [128, DC, TL], F32); xToB = [Buf() for _ in range(DC)]
    xTl = _sb(sA, "xTl", [128, DC, NSLOT * HALO], F32); xTlB = [Buf() for _ in range(DC)]
    sq = [_sb(sA, "sq%d" % j, [128, NCOL], F32) for j in range(2)]; sqB = [Buf(), Buf()]
    rstd = _sb(sA, "rstd", [128, NCOL], F32); rstdB = Buf()

    cx.dma(cx.sp, [(ident[:], ident_d)], writes=[identB])
    cx.dma(cx.sp, [(g_sb[:], g0), (cw_sb[:], convw), (cp_sb[:], cpar)], writes=[gB])
    cx.op(cx.dve, lambda: nc.vector.memset(onesD[:], 1.0 / D), writes=[onesB])
    cx.op(cx.dve, lambda: nc.vector.memset(ones1k[:], 1.0 / 1024), writes=[onesB])
    cx.op(cx.dve, lambda: nc.vector.memset(epsT[:], EPS), writes=[onesB])

    for i in range(NSLOT):
        j = i % 2
        cx.dma(cx.sp, [(xtok[j][:], xh[i, HALO:W, :])], writes=[xtokB[j]])
        cx.dma(cx.sp, [(xhal[j][:], xh[i, 0:HALO, :])], writes=[xhalB[j]])
        for k4 in range(4):
            pb = (i * 4 + k4) % 4
            for kk in range(4):
                k = k4 * 4 + kk
                cx.op(cx.pe, lambda: nc.tensor.transpose(out=ps[pb][:, kk * 128:(kk + 1) * 128],
                                                         in_=xtok[j][:, k * 128:(k + 1) * 128], identity=ident[:]),
                      reads=[xtokB[j], identB], writes=[psB[pb]], mark=(kk == 3))
            cx.op(cx.dve, lambda: nc.vector.tensor_copy(
                out=xTo[:, k4 * 4:k4 * 4 + 4, i * 128:(i + 1) * 128],
                in_=ps[pb][:].rearrange("p (a b) -> p a b", a=4)),
                reads=[psB[pb]], writes=xToB[k4 * 4:k4 * 4 + 4])
        pb = 4 + (i % 2)
        for k in range(DC):
            cx.op(cx.pe, lambda: nc.tensor.transpose(out=ps[pb][:, k * HALO:(k + 1) * HALO],
                                                     in_=xhal[j][:, k * 128:(k + 1) * 128],
                                                     identity=ident[0:HALO, 0:HALO]),
                  reads=[xhalB[j], identB], writes=[psB[pb]], mark=(k == DC - 1))
        cx.op(cx.act, lambda: nc.scalar.copy(out=xTl[:, :, i * HALO:(i + 1) * HALO],
                                             in_=ps[pb][:].rearrange("p (a b) -> p a b", a=DC)),
              reads=[psB[pb]], writes=xTlB)

    cx.dma(cx.sp, [(xT_o, xTo[:])], reads=xToB, is_out=True)

    scratch = dict(sq=sq, sqb=sqB, ps=ps[0:3], psb=psB[0:3], rstd=rstd, rstdb=rstdB, eps=epsT)
    rms_norm_fm(cx, lambda k: xTo[:, k, :], xToB, TL, g_sb, lambda k: xnO[:, k, :], xnB, onesD, scratch)
    rms_norm_fm(cx, lambda k: xTl[:, k, :], xTlB, NSLOT * HALO, g_sb, lambda k: xnH[:, k, :], xnHB, onesD, scratch)

    cx.barrier()
    sA.close()
    sB = _scope(nc)
    ws = WStream(cx, sB, "w1", DC, 512, 3)
    a_sb = _sb(sB, "a_sb", [128, NCOL], F32); aB = Buf()
    sg_sb = _sb(sB, "sg_sb", [128, NCOL], F32); sgB = Buf()
    qT = _sb(sB, "qT", [128, NSLOT, 8, 128], BF16); qB = Buf()
    kfm = _sb(sB, "kfm", [128, 4, NSLOT, 2, 128], BF16); kfB = Buf()
    vtm = _sb(sB, "vtm", [128, 2, NSLOT, 256], BF16); vtB = Buf()
    gat = _sb(sB, "gat", [128, NSLOT, 48], F32); gatB = Buf()
    blocks = [(0, 512), (512, 512), (1024, 256)]
    pscnt = [0]

    def formB_all(wt, wb, c0, evac):
        for bi, (b0, bn) in enumerate(blocks):
            pb = pscnt[0] % 7; pscnt[0] += 1
            for k in range(DC):
                rhs = xnO[:, k, b0:b0 + bn] if bi < 2 else xnH[:, k, :]
                cx.op(cx.pe, lambda: nc.tensor.matmul(ps[pb][:, 0:bn], lhsT=wt[:, k, c0:c0 + 128], rhs=rhs,
                                                      start=(k == 0), stop=(k == DC - 1)),
                      reads=[wb, xnB[k], xnHB[k]], writes=[psB[pb]], mark=(k == DC - 1))
            evac(ps[pb][:, 0:bn], psB[pb], b0, b0 + bn)

    kind_of = {(0, 0): 0, (0, 1): 1, (1, 0): 2, (2, 0): 3}
    for blk in range(3):
        wk, wkb = ws.load(w_in, 3072 + blk * 512, 512)
        for hf in range(2):
            if (blk, hf) in kind_of:
                kind = kind_of[(blk, hf)]
                for gh in range(2):
                    c0 = hf * 256 + gh * 128
                    for tb in range(2):
                        pb = pscnt[0] % 7; pscnt[0] += 1
                        for k in range(DC):
                            cx.op(cx.pe, lambda: nc.tensor.matmul(ps[pb][:], lhsT=wk[:, k, c0:c0 + 128],
                                                                  rhs=xnO[:, k, tb * 512:(tb + 1) * 512],
                                                                  start=(k == 0), stop=(k == DC - 1)),
                                  reads=[wkb, xnB[k]], writes=[psB[pb]], mark=(k == DC - 1))
                        cx.op(cx.act, lambda: nc.scalar.copy(out=kfm[:, kind, tb * 4:tb * 4 + 4, gh, :],
                                                             in_=ps[pb][:].rearrange("p (a b) -> p a b", a=4)),
                              reads=[psB[pb]], writes=[kfB])
            else:
                vk = 0 if blk == 1 else 1
                for i in range(NSLOT):
                    pb = pscnt[0] % 7; pscnt[0] += 1
                    for k in range(DC):
                        cx.op(cx.pe, lambda: nc.tensor.matmul(ps[pb][:, 0:256], lhsT=xnO[:, k, i * 128:(i + 1) * 128],
                                                              rhs=wk[:, k, 256:512],
                                                              start=(k == 0), stop=(k == DC - 1)),
                              reads=[wkb, xnB[k]], writes=[psB[pb]], mark=(k == DC - 1))
                    cx.op(cx.act, lambda: nc.scalar.copy(out=vtm[:, vk, i, :], in_=ps[pb][:, 0:256]),
                          reads=[psB[pb]], writes=[vtB])
    wgl, wglb = ws.load(w_in, 4608, 48)
    for i in range(NSLOT):
        pb = pscnt[0] % 7; pscnt[0] += 1
        for k in range(DC):
            cx.op(cx.pe, lambda: nc.tensor.matmul(ps[pb][:, 0:48], lhsT=xnO[:, k, i * 128:(i + 1) * 128], rhs=wgl[:, k, 0:48],
                                                  start=(k == 0), stop=(k == DC - 1)),
                  reads=[wglb, xnB[k]], writes=[psB[pb]], mark=(k == DC - 1))
        cx.op(cx.act, lambda: nc.scalar.activation(out=gat[:, i, :], in_=ps[pb][:, 0:48], func=AF.Sigmoid),
              reads=[psB[pb]], writes=[gatB])
    cx.dma(cx.sp, [(kfm_o, kfm[:]), (vtm_o, vtm[:])], reads=[kfB, vtB], writes=[kvB], is_out=True)
    if T.get("after_kv"):
        T["after_kv"]()
    cx.dma(cx.sp, [(gates_o, gat[:])], reads=[gatB], is_out=True)
    for half in range(2):
        wa, wab = ws.load(w_in, half * 512, 512)
        wg, wgb = ws.load(w_in, 1024 + half * 512, 512)
        for mm in range(4):
            m = half * 4 + mm
            formB_all(wa, wab, mm * 128,
                      lambda p, pbuf, lo, hi: cx.op(cx.act, lambda: nc.scalar.copy(out=a_sb[:, lo:hi], in_=p),
                                                    reads=[pbuf], writes=[aB]))
            formB_all(wg, wgb, mm * 128,
                      lambda p, pbuf, lo, hi: cx.op(cx.act, lambda: nc.scalar.activation(out=sg_sb[:, lo:hi], in_=p,
                                                                                        func=AF.Sigmoid),
                                                    reads=[pbuf], writes=[sgB]))
            cx.op(cx.dve, lambda: nc.vector.tensor_mul(out=cfm[:, m, :, HALO:W],
                                                       in0=a_sb[:, 0:TL].rearrange("p (i t) -> p i t", i=NSLOT),
                                                       in1=sg_sb[:, 0:TL].rearrange("p (i t) -> p i t", i=NSLOT)),
                  reads=[aB, sgB], writes=[cfmB[m]])
            cx.op(cx.dve, lambda: nc.vector.tensor_mul(out=cfm[:, m, :, 0:HALO],
                                                       in0=a_sb[:, TL:NCOL].rearrange("p (i t) -> p i t", i=NSLOT),
                                                       in1=sg_sb[:, TL:NCOL].rearrange("p (i t) -> p i t", i=NSLOT)),
                  reads=[aB, sgB], writes=[cfmB[m]])

    for gh in range(2):
        wq, wqb = ws.load_q(w_in, 2048 + gh * 512)
        for r in range(4):
            lhs_fn = lambda k: wq[:, k, r * 128:(r + 1) * 128]
            for tb in range(2):
                pb = pscnt[0] % 7; pscnt[0] += 1
                for k in range(DC):
                    cx.op(cx.pe, lambda: nc.tensor.matmul(ps[pb][:], lhsT=lhs_fn(k),
                                                          rhs=xnO[:, k, tb * 512:(tb + 1) * 512],
                                                          start=(k == 0), stop=(k == DC - 1)),
                          reads=[wqb, xnB[k]], writes=[psB[pb]], mark=(k == DC - 1))
                cx.op(cx.act, lambda: nc.scalar.copy(out=qT[:, tb * 4:tb * 4 + 4, gh * 4 + r, :],
                                                     in_=ps[pb][:].rearrange("p (a b) -> p a b", a=4)),
                      reads=[psB[pb]], writes=[qB])

    cx.dma(cx.sp, [(qT_o, qT[:])], reads=[qB], is_out=True)
    cx.barrier()
    sB.close()
    s1.close()
    sC = _scope(nc)
    acc = _sb(sC, "acc", [128, 8, NSLOT, 128], F32); accB = [Buf() for _ in range(8)]
    cT = _sb(sC, "cT", [128, 8, TL], BF16); cTB = Buf()
    mean_sb = _sb(sC, "mean_sb", [128, TL], F32); meanB = Buf()
    sq = [_sb(sC, "sqc%d" % j, [128, TL], F32) for j in range(2)]; sqB = [Buf(), Buf()]
    rstd = _sb(sC, "rstdc", [128, TL], F32); rstdB = Buf()
    for m in range(8):
        cx.op(cx.dve, lambda: nc.vector.tensor_scalar(out=acc[:, m], in0=cfm[:, m, :, HALO:W],
                                                      scalar1=cw_sb[:, m, 30:31], scalar2=cp_sb[:, 0, m:m + 1],
                                                      op0=ALU.mult, op1=ALU.add),
              reads=[cfmB[m], gB], writes=[accB[m]])
        for k in range(1, 31):
            cx.op(cx.dve, lambda: nc.vector.scalar_tensor_tensor(out=acc[:, m], in0=cfm[:, m, :, HALO - k:W - k],
                                                                 scalar=cw_sb[:, m, 30 - k:31 - k], in1=acc[:, m],
                                                                 op0=ALU.mult, op1=ALU.add),
                  reads=[cfmB[m], accB[m]], writes=[accB[m]])
    for m in range(8):
        j = m % 2
        af = acc[:, m].rearrange("p i w -> p (i w)")
        cx.op(cx.act, lambda: nc.scalar.activation(out=sq[j][:, 0:TL], in_=af, func=AF.Square),
              reads=[accB[m]], writes=[sqB[j]])
        for tb in range(2):
            cx.op(cx.pe, lambda: nc.tensor.matmul(ps[tb][:], lhsT=ones1k[:], rhs=af[:, tb * 512:(tb + 1) * 512],
                                                  start=(m == 0), stop=(m == 7)),
                  reads=[accB[m], onesB], writes=[psB[tb]])
            cx.op(cx.pe, lambda: nc.tensor.matmul(ps[2 + tb][:], lhsT=ones1k[:], rhs=sq[j][:, tb * 512:(tb + 1) * 512],
                                                  start=(m == 0), stop=(m == 7)),
                  reads=[sqB[j], onesB], writes=[psB[2 + tb]])
    for tb in range(2):
        sl = slice(tb * 512, (tb + 1) * 512)
        cx.op(cx.act, lambda: nc.scalar.copy(out=mean_sb[:, sl], in_=ps[tb][:]), reads=[psB[tb]], writes=[meanB])
        cx.op(cx.dve, lambda: nc.vector.tensor_mul(out=rstd[:, sl], in0=mean_sb[:, sl], in1=mean_sb[:, sl]),
              reads=[meanB], writes=[rstdB])
        cx.op(cx.dve, lambda: nc.vector.tensor_sub(out=rstd[:, sl], in0=ps[2 + tb][:], in1=rstd[:, sl]),
              reads=[psB[2 + tb], rstdB], writes=[rstdB])
    cx.op(cx.act, lambda: nc.scalar.activation(out=rstd[:, 0:TL], in_=rstd[:, 0:TL], func=AF.Sqrt,
                                               bias=epsT[:, 0:1], scale=1.0), reads=[rstdB], writes=[rstdB])
    cx.op(cx.dve, lambda: nc.vector.reciprocal(out=rstd[:, 0:TL], in_=rstd[:, 0:TL]), reads=[rstdB], writes=[rstdB])
    for m in range(8):
        j = m % 2
        af = acc[:, m].rearrange("p i w -> p (i w)")
        cx.op(cx.dve, lambda: nc.vector.tensor_sub(out=sq[j][:, 0:TL], in0=af, in1=mean_sb[:]),
              reads=[accB[m], meanB], writes=[sqB[j]])
        cx.op(cx.dve, lambda: nc.vector.tensor_mul(out=sq[j][:, 0:TL], in0=sq[j][:, 0:TL], in1=rstd[:, 0:TL]),
              reads=[sqB[j], rstdB], writes=[sqB[j]])
        cx.op(cx.act, lambda: nc.scalar.activation(out=cT[:, m, :], in_=sq[j][:, 0:TL], func=AF.Silu,
                                                   bias=cp_sb[:, 2, m:m + 1], scale=cp_sb[:, 1, m:m + 1]),
              reads=[sqB[j], gB], writes=[cTB])
    cx.dma(cx.sp, [(cT_o, cT[:])], reads=[cTB], is_out=True)
    cx.barrier()
    sC.close()
    s0.close()


def mlp_stage(cx, st, xT, xTB, g_sb, w1_ap, w2_ap, ps, psB, onesD, epsT):
    nc = cx.nc
    xnT = _sb(st, "m_xnT", [128, DC, TL], BF16); xnB = [Buf() for _ in range(DC)]
    sq = [_sb(st, "m_sq%d" % j, [128, TL], F32) for j in range(2)]; sqB = [Buf(), Buf()]
    rstd = _sb(st, "m_rstd", [128, TL], F32); rstdB = Buf()
    hT = [_sb(st, "m_hT%d" % j, [128, 8, TL], BF16) for j in range(1)]; hB = [[Buf() for _ in range(8)] for _ in range(1)]
    tmp = [_sb(st, "m_tmp%d" % j, [128, 512], F32) for j in range(2)]; tmpB = [Buf(), Buf()]
    ws = WStream(cx, st, "m_w", DC, 512, 3)
    scratch = dict(sq=sq, sqb=sqB, ps=ps[0:2], psb=psB[0:2], rstd=rstd, rstdb=rstdB, eps=epsT)
    rms_norm_fm(cx, lambda k: xT[:, k, :], xTB, TL, g_sb, lambda k: xnT[:, k, :], xnB, onesD, scratch)
    cnt = 0
    NPS = len(ps)
    for fb in range(8):
        hj = 0
        for wt in range(2):
            w, wb = ws.load(w1_ap, fb * 1024 + wt * 512, 512)
            for m in range(4):
                fch = wt * 4 + m
                for tb in range(2):
                    pb = cnt % NPS; cnt += 1
                    for k in range(DC):
                        cx.op(cx.pe, lambda: nc.tensor.matmul(ps[pb][:], lhsT=w[:, k, m * 128:(m + 1) * 128],
                                                              rhs=xnT[:, k, tb * 512:(tb + 1) * 512],
                                                              start=(k == 0), stop=(k == DC - 1)),
                              reads=[wb, xnB[k]], writes=[psB[pb]], mark=(k == DC - 1))
                    tj = cnt % 2
                    cx.op(cx.act, lambda: nc.scalar.activation(out=tmp[tj][:], in_=ps[pb][:], func=AF.Square),
                          reads=[psB[pb]], writes=[tmpB[tj]])
                    cx.op(cx.dve, lambda: nc.vector.scalar_tensor_tensor(
                        out=hT[hj][:, fch, tb * 512:(tb + 1) * 512], in0=ps[pb][:], scalar=0.0, in1=tmp[tj][:],
                        op0=ALU.is_gt, op1=ALU.mult), reads=[psB[pb], tmpB[tj]], writes=[hB[hj][fch]])
        for wt in range(4):
            w, wb = ws.load(w2_ap, wt * 512, 512, kch=8, r0=fb * 1024)
            for m in range(4):
                och = wt * 4 + m
                for tb in range(2):
                    pb = cnt % NPS; cnt += 1
                    for k in range(8):
                        cx.op(cx.pe, lambda: nc.tensor.matmul(ps[pb][:], lhsT=w[:, k, m * 128:(m + 1) * 128],
                                                              rhs=hT[hj][:, k, tb * 512:(tb + 1) * 512],
                                                              start=(k == 0), stop=(k == 7)),
                              reads=[wb, hB[hj][k]], writes=[psB[pb]], mark=(k == 7))
                    cx.op(cx.dve, lambda: nc.vector.tensor_add(out=xT[:, och, tb * 512:(tb + 1) * 512], in0=ps[pb][:],
                                                               in1=xT[:, och, tb * 512:(tb + 1) * 512]),
                          reads=[psB[pb]], writes=[xTB[och]])


def emit_p2(nc, cx, T, ps, psB, ptr, ptrB, debug=False):
    qT_d, gates_d, cT_d, xT_d = T["qT"], T["gates"], T["cT"], T["xT"]
    KG, VG, gB_ = T["kfm_g"], T["vtm_g"], T["kvgB"]
    cw1_d = [T["cw1_k"], T["cw1_v"]]
    cw2_d = [T["cw2_k"], T["cw2_v"]]
    posT_d, cmask_d, dzmask_d, wmask_d = T["posT"], T["cmask"], T["dzmask"], T["wmask"]
    esel_d, bonus_d, ovl_d, identb_d = T["esel"], T["bonus"], T["ovl"], T["identb"]
    w_out_d, w1_d, w2_d, gm_d = T["w_out"], T["w_mlp_in0"], T["w_mlp_out0"], T["g_mlp0"]
    xT_o = T["xT2"]
    if debug:
        oT_o, kc_o, vc_o = T["oT_dbg"], T["kc_dbg"], T["vc_dbg"]

    s0 = _scope(nc)
    identb = _sb(s0, "identb", [128, 128], BF16); cB = Buf()
    onesD = _sb(s0, "onesD", [128, 128], F32)
    epsT = _sb(s0, "epsT", [128, 1], F32)
    gm_sb = _sb(s0, "gm", [128, DC], F32)
    cx.dma(cx.sp, [(identb[:], identb_d), (gm_sb[:], gm_d)], writes=[cB])
    cx.op(cx.dve, lambda: nc.vector.memset(onesD[:], 1.0 / D), writes=[cB])
    cx.op(cx.dve, lambda: nc.vector.memset(epsT[:], EPS), writes=[cB])
    oT = _sb(s0, "oT", [128, 8, TL], BF16); oTB = Buf()

    sAtt = _scope(nc)
    kcT = _sb(sAtt, "kcT", [128, 2, 512], BF16); kcB = Buf()
    vcx = _sb(sAtt, "vcx", [128, 4, 4, 193], BF16); vcB = Buf()
    cx.op(cx.dve, lambda: nc.vector.memset(kcT[:], 0.0), writes=[kcB])
    cx.op(cx.dve, lambda: nc.vector.memset(vcx[:], 0.0), writes=[vcB])

    sCmp = _scope(nc)
    raw = _sb(sCmp, "raw", [128, 2, S], BF16); rawB = Buf()
    raw2 = _sb(sCmp, "raw2", [128, 2, 16, 512], BF16); raw2B = Buf()
    w1s = _sb(sCmp, "w1s", [128, 32, 128], BF16); w1B = Buf()
    w2k = _sb(sCmp, "w2k", [128, 128], BF16)
    w2v = _sb(sCmp, "w2v", [128, 64], BF16); w2B = Buf()
    posT = _sb(sCmp, "posT", [64, 2, 32], BF16); posB = Buf()
    hTs = [_sb(sCmp, "hTs%d" % j, [128, 512], BF16) for j in range(2)]; hTB = [Buf(), Buf()]
    bias_sb = _sb(sCmp, "cbias", [128, 2], F32); biasB = Buf()
    ovl_sb = _sb(sCmp, "ovl", [128, 4, 128], BF16); ovlB = Buf()
    cx.dma(cx.pool, [(posT[:], posT_d)], writes=[posB])
    cx.dma(cx.pool, [(w2k[:, 0:64], cw2_d[0]), (w2k[:, 64:128], cw2_d[0]), (w2v[:], cw2_d[1])], writes=[w2B])
    cx.dma(cx.sp, [(ovl_sb[:], ovl_d)], writes=[ovlB])
    for j in range(2):
        cx.op(cx.dve, lambda: nc.vector.memset(hTs[j][:], 0.0), writes=[hTB[j]])
    hcnt = 0
    for kind in range(2):
        cx.dma(cx.sp, [(raw[:, gh, :].rearrange("p (i r t) -> p r i t", r=8, t=128)[:, r], KG[r, :, kind, :, gh, :])
                       for gh in range(2) for r in range(8)], reads=[gB_], writes=[rawB])
        for gh_ in range(2):
            for l16 in range(16):
                eng_ = cx.dve if (l16 % 2 == 0) else cx.act
                if l16 % 2 == 0:
                    cx.op(cx.dve, lambda: nc.vector.tensor_copy(out=raw2[:, gh_, l16, :], in_=raw[:, gh_, l16:S:16]),
                          reads=[rawB], writes=[raw2B])
                else:
                    cx.op(cx.act, lambda: nc.scalar.copy(out=raw2[:, gh_, l16, :], in_=raw[:, gh_, l16:S:16]),
                          reads=[rawB], writes=[raw2B])
        w1src = cw1_d[kind].rearrange("(l d) j -> d l j", d=64)
        cx.dma(cx.pool, [(w1s[0:64], w1src), (w1s[64:128], w1src)], writes=[w1B])
        for l in range(32):
            cx.op(cx.pe, lambda: nc.tensor.matmul(ps[6][:, 0:1], lhsT=w1s[0:64, l, :], rhs=posT[:, kind, l:l + 1],
                                                  start=(l == 0), stop=(l == 31)),
                  reads=[w1B, posB], writes=[psB[6]], mark=(l == 31))
        cx.op(cx.act, lambda: nc.scalar.copy(out=bias_sb[:, kind:kind + 1], in_=ps[6][:, 0:1]),
              reads=[psB[6]], writes=[biasB])
        for g in range(4):
            pb0 = 64 * (g % 2); gh = g // 2
            hp = hcnt % 2; hcnt += 1
            for l in range(32):
                cx.op(cx.pe, lambda: nc.tensor.matmul(ps[hp][:, 0:511], lhsT=w1s[pb0:pb0 + 64, l, :],
                                                      rhs=raw2[pb0:pb0 + 64, gh, l % 16, l // 16:l // 16 + 511],
                                                      start=(l == 0), stop=(l == 31)),
                      reads=[w1B, raw2B], writes=[psB[hp]], mark=(l == 31))
            cx.op(cx.act, lambda: nc.scalar.activation(out=hTs[hp][:, 0:511], in_=ps[hp][:, 0:511], func=AF.Silu,
                                                       bias=bias_sb[:, kind:kind + 1], scale=1.0),
                  reads=[psB[hp], biasB], writes=[hTB[hp]])
            if kind == 0:
                cx.op(cx.pe, lambda: nc.tensor.matmul(ps[2 + hp][:, 0:511], lhsT=w2k[:], rhs=hTs[hp][:, 0:511],
                                                      start=True, stop=True),
                      reads=[w2B, hTB[hp]], writes=[psB[2 + hp]])
                cx.op(cx.dve, lambda: nc.vector.tensor_copy(out=kcT[pb0:pb0 + 64, gh, 0:511],
                                                            in_=ps[2 + hp][pb0:pb0 + 64, 0:511]),
                      reads=[psB[2 + hp]], writes=[kcB])
            else:
                for nt in range(4):
                    cx.op(cx.pe, lambda: nc.tensor.matmul(ps[2 + hp][:, nt * 64:(nt + 1) * 64],
                                                          lhsT=hTs[hp][:, nt * 128:(nt + 1) * 128], rhs=w2v[:],
                                                          start=True, stop=True),
                          reads=[w2B, hTB[hp]], writes=[psB[2 + hp]], mark=(nt == 3))
                cx.op(cx.dve, lambda: nc.vector.tensor_copy(out=vcx[:, :, g, 0:64],
                                                            in_=ps[2 + hp][:, 0:256].rearrange("p (a b) -> p a b", a=4)),
                      reads=[psB[2 + hp]], writes=[vcB])
    for g in range(4):
        cx.op(cx.dve, lambda: nc.vector.tensor_copy(out=vcx[:, :, g, 64:192], in_=ovl_sb[:]), reads=[ovlB], writes=[vcB])
    cx.op(cx.dve, lambda: nc.vector.memset(vcx[:, :, :, 192:193], 1.0), writes=[vcB])
    if debug:
        cx.dma(cx.sp, [(kc_o, kcT[:])], reads=[kcB], is_out=True)
        cx.dma(cx.sp, [(vc_o, vcx[:])], reads=[vcB], is_out=True)
    cx.barrier()
    sCmp.close()

    sB = _scope(nc)
    ks = _sb(sB, "ks", [128, 64, 2, 128], BF16); ksB = Buf()
    vsx = _sb(sB, "vsx", [128, 64, 4, 65], BF16); vsB = Buf()
    kw = [_sb(sB, "kw%d" % j, [128, 12, 2, 128], BF16) for j in range(2)]; kwB = [Buf(), Buf()]
    vwx = [_sb(sB, "vwx%d" % j, [128, 12, 4, 65], BF16) for j in range(2)]; vwB = [Buf(), Buf()]
    esel = _sb(sB, "esel", [128, 64, 128], BF16)
    cmask = _sb(sB, "cmask", [128, NSLOT, 4, 128], BF16)
    dzmask = _sb(sB, "dzmask", [128, 8, 128], BF16)
    wmask = _sb(sB, "wmask", [128, 12, 128], BF16); mB = Buf()
    bonus = _sb(sB, "bonus", [128, NSLOT, 128], F32)
    qT = _sb(sB, "qT", [128, NSLOT, 8, 128], BF16)
    gat = _sb(sB, "gat", [128, NSLOT, 48], F32); qB = Buf()
    pT = [_sb(sB, "pT%d" % j, [128, 512], BF16) for j in range(3)]; pTB = [Buf() for _ in range(3)]
    imp = [_sb(sB, "imp%d" % j, [128, 128], F32) for j in range(2)]; impB = Buf()
    m8 = _sb(sB, "m8", [128, 16], F32)
    negq = _sb(sB, "negq", [128, 128], BF16); negqB = Buf()
    negT = _sb(sB, "negT", [128, 128], BF16); negTB = Buf()
    rz = _sb(sB, "rz", [128, 12], F32); rzB = Buf()
    rz2 = _sb(sB, "rz2", [128, 4], F32); rz2B = Buf()
    m8B = Buf()
    oacc = _sb(sB, "oacc", [128, 4, 64], F32); oaccB = Buf()
    otok = _sb(sB, "otok", [128, 1024], BF16); otokB = Buf()

    cx.op(cx.dve, lambda: nc.vector.memset(vsx[:, :, :, 64:65], 1.0), writes=[vsB])
    for j in range(2):
        cx.op(cx.dve, lambda: nc.vector.memset(vwx[j][:, :, :, 64:65], 1.0), writes=[vwB[j]])
    cx.dma(cx.sp, [(esel[:], esel_d), (cmask[:], cmask_d), (dzmask[:], dzmask_d), (wmask[:], wmask_d),
                   (bonus[:], bonus_d)], writes=[mB])
    cx.dma(cx.sp, [(qT[:], qT_d), (gat[:], gates_d)], writes=[qB])
    for kb in range(8):
        sl = slice(kb * 8, kb * 8 + 8)
        cx.dma(cx.sp, [(ks[:, sl].rearrange("p r g t -> p r (g t)"), KG[:, :, 2, kb].rearrange("r p g t -> p r (g t)"))],
               reads=[gB_], writes=[ksB])
        cx.dma(cx.sp, [(vsx[:, kb * 8 + r, :, 0:64], VG[r, :, 0, kb, :].rearrange("p (g e) -> p g e", g=4))
                       for r in range(8)], reads=[gB_], writes=[vsB])

    scnt = 0
    pcnt = 0
    UC = [ps[2], ps[3]]; UCB = [psB[2], psB[3]]
    US, USB = ps[4], psB[4]
    UW, UWB = ps[5], psB[5]

    def score_tile(lhsT_k, rhs_q, masks, kreads):
        nonlocal scnt, pcnt
        sb_ = scnt % 2; scnt += 1
        pj = pcnt % 3; pcnt += 1
        cx.op(cx.pe, lambda: nc.tensor.matmul(ps[sb_][:], lhsT=lhsT_k, rhs=rhs_q, start=True, stop=False),
              reads=kreads + [qB], writes=[psB[sb_]], mark=False)
        for mi, (ml, mr, mrd) in enumerate(masks):
            last = (mi == len(masks) - 1)
            cx.op(cx.pe, lambda: nc.tensor.matmul(ps[sb_][:].rearrange("p (h n) -> p h n", h=4), lhsT=ml,
                                                  rhs=mr.unsqueeze(1).to_broadcast([128, 4, 128]),
                                                  start=False, stop=last),
                  reads=mrd, writes=[psB[sb_]], mark=last)
        cx.op(cx.act, lambda: nc.scalar.activation(out=pT[pj][:], in_=ps[sb_][:], func=AF.Exp, scale=SCALE),
              reads=[psB[sb_]], writes=[pTB[pj]])
        return pT[pj], pTB[pj]

    pend = [None]

    def flush():
        if pend[0] is not None:
            f, a = pend[0]
            pend[0] = None
            f()
            if a is not None:
                a()

    def tile(lhsT_k, rhs_q, masks, kreads, pv_fn, after):
        p_, pB_ = score_tile(lhsT_k, rhs_q, masks, kreads)
        flush()
        pend[0] = ((lambda: pv_fn(p_, pB_)), after)

    for i in range(NSLOT):
        wj = i % 2
        lo = max(0, 8 * i - 4); r0 = lo - (8 * i - 4)
        nt_w = 8 * i + 8 - lo
        cx.dma(cx.sp, [(kw[wj][:, r0 + q_].rearrange("p g t -> p (g t)"),
                        KG[(lo + q_) % 8, :, 3, (lo + q_) // 8].rearrange("p g t -> p (g t)")) for q_ in range(nt_w)],
               reads=[gB_], writes=[kwB[wj]])
        cx.dma(cx.sp, [(vwx[wj][:, r0 + q_, :, 0:64],
                        VG[(lo + q_) % 8, :, 1, (lo + q_) // 8, :].rearrange("p (g e) -> p g e", g=4)) for q_ in range(nt_w)],
               reads=[gB_], writes=[vwB[wj]])
        for g in range(4):
            pb0 = 64 * (g % 2); gh = g // 2
            QT = qT[pb0:pb0 + 64, i, gh * 4:gh * 4 + 4, :].rearrange("p a b -> p (a b)")
            gsl = (lambda i_, g_: (lambda b: gat[:, i_, g_ * 12 + b:g_ * 12 + 12:3]))(i, g)

            def chain(i=i, g=g, gsl=gsl):
                for r in range(4):
                    cx.op(cx.dve, lambda: nc.vector.tensor_scalar(
                        out=rz[:, r:r + 1], in0=UC[r // 2][:, (r % 2) * 193 + 192:(r % 2) * 193 + 193],
                        scalar1=1e-30, scalar2=None, op0=ALU.max), reads=[UCB[r // 2]], writes=[rzB])
                cx.op(cx.dve, lambda: nc.vector.reciprocal(out=rz[:, 0:4], in_=rz[:, 0:4]), reads=[rzB], writes=[rzB])
                im = imp[0]
                cx.op(cx.dve, lambda: nc.vector.tensor_scalar(out=im[:], in0=UC[0][:, 64:192], scalar1=rz[:, 0:1],
                                                              scalar2=None, op0=ALU.mult),
                      reads=[UCB[0], rzB], writes=[impB])
                for r in range(1, 4):
                    cx.op(cx.dve, lambda: nc.vector.scalar_tensor_tensor(
                        out=im[:], in0=UC[r // 2][:, (r % 2) * 193 + 64:(r % 2) * 193 + 192], scalar=rz[:, r:r + 1],
                        in1=im[:], op0=ALU.mult, op1=ALU.add), reads=[UCB[r // 2], rzB, impB], writes=[impB])
                cx.op(cx.dve, lambda: nc.vector.tensor_add(out=im[:], in0=im[:], in1=bonus[:, i, :]),
                      reads=[impB, mB], writes=[impB])
                cx.op(cx.dve, lambda: nc.vector.max(out=m8[:, 0:8], in_=im[:]), reads=[impB], writes=[m8B])
                cx.op(cx.dve, lambda: nc.vector.match_replace(out=imp[1][:], in_to_replace=m8[:, 0:8], in_values=im[:],
                                                              imm_value=-1e30), reads=[impB, m8B], writes=[impB])
                cx.op(cx.dve, lambda: nc.vector.max(out=m8[:, 8:16], in_=imp[1][:]), reads=[impB], writes=[m8B])
                cx.op(cx.dve, lambda: nc.vector.tensor_scalar(out=negq[:], in0=im[:], scalar1=m8[:, 15:16], scalar2=NEG,
                                                              op0=ALU.is_lt, op1=ALU.mult),
                      reads=[impB, m8B], writes=[negqB])
                cx.op(cx.pe, lambda: nc.tensor.transpose(out=ptr[:, 0:128], in_=negq[:], identity=identb[:]),
                      reads=[negqB, cB], writes=[ptrB])
                cx.op(cx.act, lambda: nc.scalar.copy(out=negT[:], in_=ptr[:, 0:128]), reads=[ptrB], writes=[negTB])
                cx.op(cx.dve, lambda: nc.vector.tensor_mul(out=rz[:, 4:8], in0=rz[:, 0:4], in1=gsl(0)),
                      reads=[rzB, qB], writes=[rzB])
                for r in range(4):
                    cx.op(cx.dve, lambda: nc.vector.tensor_scalar(out=oacc[:, r, :],
                                                                  in0=UC[r // 2][:, (r % 2) * 193:(r % 2) * 193 + 64],
                                                                  scalar1=rz[:, 4 + r:5 + r], scalar2=None, op0=ALU.mult),
                          reads=[UCB[r // 2], rzB], writes=[oaccB])

            def combine(i=i, g=g, gsl=gsl):
                for bi, (U, UB) in enumerate(((UW, UWB), (US, USB))):
                    cx.op(cx.dve, lambda: nc.vector.reciprocal(out=rz2[:, 0:4], in_=U[:, 64:260:65]),
                          reads=[UB], writes=[rz2B])
                    cx.op(cx.dve, lambda: nc.vector.tensor_mul(out=rz2[:, 0:4], in0=rz2[:, 0:4], in1=gsl(2 - bi)),
                          reads=[rz2B, qB], writes=[rz2B])
                    for r in range(4):
                        dst = oacc[:, r, :] if bi == 0 else otok[:, g * 256 + r * 64:g * 256 + (r + 1) * 64]
                        cx.op(cx.dve, lambda: nc.vector.scalar_tensor_tensor(out=dst, in0=U[:, r * 65:r * 65 + 64],
                                                                             scalar=rz2[:, r:r + 1], in1=oacc[:, r, :],
                                                                             op0=ALU.mult, op1=ALU.add),
                              reads=[UB, rz2B, oaccB], writes=[oaccB, otokB])

            NT = i // 2 + 1
            for nt in range(NT):
                def pv_c(p_, pB_, nt=nt, g=g, NT=NT):
                    for r in range(4):
                        cx.op(cx.pe, lambda: nc.tensor.matmul(UC[r // 2][:, (r % 2) * 193:(r % 2) * 193 + 193],
                                                              lhsT=p_[:, r * 128:(r + 1) * 128], rhs=vcx[:, nt, g, :],
                                                              start=(nt == 0 and r % 2 == 0), stop=(nt == NT - 1),
                                                              skip_group_check=True),
                              reads=[pB_, vcB], writes=[UCB[r // 2]], mark=(r % 2 == 1))
                tile(kcT[pb0:pb0 + 64, gh, nt * 128:(nt + 1) * 128], QT,
                     [(identb[:], cmask[:, i, nt, :], [mB, cB])], [kcB], pv_c, chain if nt == NT - 1 else None)
            for rho in range(r0, 12):
                def pv_w(p_, pB_, rho=rho, g=g, wj=wj, r0=r0):
                    for r in range(4):
                        cx.op(cx.pe, lambda: nc.tensor.matmul(UW[:, r * 65:(r + 1) * 65], lhsT=p_[:, r * 128:(r + 1) * 128],
                                                              rhs=vwx[wj][:, rho, g, :], start=(rho == r0 and r == 0),
                                                              stop=(rho == 11), skip_group_check=True),
                              reads=[pB_, vwB[wj]], writes=[UWB], mark=(r == 3))
                tile(kw[wj][pb0:pb0 + 64, rho, gh, :], QT, [(identb[:], wmask[:, rho, :], [mB, cB])], [kwB[wj]], pv_w, None)
            nkt = 8 * i + 8
            for kt in range(nkt):
                masks = [(esel[:, kt, :], negT[:], [mB, negTB])]
                if kt >= 8 * i:
                    masks.append((identb[:], dzmask[:, kt - 8 * i, :], [mB, cB]))
                def pv_s(p_, pB_, kt=kt, g=g, nkt=nkt):
                    for r in range(4):
                        cx.op(cx.pe, lambda: nc.tensor.matmul(US[:, r * 65:(r + 1) * 65], lhsT=p_[:, r * 128:(r + 1) * 128],
                                                              rhs=vsx[:, kt, g, :], start=(kt == 0 and r == 0),
                                                              stop=(kt == nkt - 1), skip_group_check=True),
                              reads=[pB_, vsB], writes=[USB], mark=(r == 3))
                tile(ks[pb0:pb0 + 64, kt, gh, :], QT, masks, [ksB], pv_s, combine if kt == nkt - 1 else None)
        flush()
        for m in range(8):
            cx.op(cx.pe, lambda: nc.tensor.transpose(out=ptr[:, m * 128:(m + 1) * 128], in_=otok[:, m * 128:(m + 1) * 128],
                                                     identity=identb[:]),
                  reads=[otokB, cB], writes=[ptrB], mark=(m == 7))
        cx.op(cx.act, lambda: nc.scalar.copy(out=oT[:, :, i * 128:(i + 1) * 128],
                                             in_=ptr[:].rearrange("p (a b) -> p a b", a=8)),
              reads=[ptrB], writes=[oTB])
    if debug:
        cx.dma(cx.sp, [(oT_o, oT[:])], reads=[oTB], is_out=True)
    cx.barrier()
    sB.close()
    sAtt.close()

    sX = _scope(nc)
    xT = _sb(sX, "xT", [128, DC, TL], F32); xTB = [Buf() for _ in range(DC)]
    cx.dma(cx.sp, [(xT[:, 0:8], xT_d[:, 0:8])], writes=xTB[0:8])
    cx.dma(cx.sp, [(xT[:, 8:16], xT_d[:, 8:16])], writes=xTB[8:16])
    sC = _scope(nc)
    cT = _sb(sC, "cT", [128, 8, TL], BF16); cTB = Buf()
    cx.dma(cx.sp, [(cT[:], cT_d)], writes=[cTB])
    wso = WStream(cx, sC, "wo", DC, 512, 2)
    cnt = 0
    for wt in range(4):
        w, wb = wso.load(w_out_d, wt * 512, 512)
        for m in range(4):
            och = wt * 4 + m
            for tb in range(2):
                pb = cnt % 7; cnt += 1
                for k in range(DC):
                    rhs = cT[:, k, tb * 512:(tb + 1) * 512] if k < 8 else oT[:, k - 8, tb * 512:(tb + 1) * 512]
                    cx.op(cx.pe, lambda: nc.tensor.matmul(ps[pb][:], lhsT=w[:, k, m * 128:(m + 1) * 128], rhs=rhs,
                                                          start=(k == 0), stop=(k == DC - 1)),
                          reads=[wb, cTB, oTB], writes=[psB[pb]], mark=(k == DC - 1))
                cx.op(cx.dve, lambda: nc.vector.tensor_add(out=xT[:, och, tb * 512:(tb + 1) * 512], in0=ps[pb][:],
                                                           in1=xT[:, och, tb * 512:(tb + 1) * 512]),
                      reads=[psB[pb]], writes=[xTB[och]])
    cx.barrier()
    sC.close()

    sM = _scope(nc)
    mlp_stage(cx, sM, xT, xTB, gm_sb, w1_d, w2_d, ps, psB, onesD, epsT)
    cx.dma(cx.sp, [(xT_o, xT[:])], reads=xTB, is_out=True)
    if T.get("halo_out") is not None:
        cx.dma(cx.sp, [(T["halo_out"][:, k], xT[:, k, :].rearrange("p (i t) -> p i t", i=NSLOT)[:, :, 128 - PH:128])
                       for k in range(DC)], reads=xTB, writes=[T["haloB"]], is_out=True)
        T["after_halo"]()
    cx.barrier()
    sM.close(); sX.close(); s0.close()


PH = 16
PW = PH + 128


def emit_p3(nc, cx, T, ps, psB):
    xT_d, icnt_d, pw_d, gv_d, identf_d = T["xT2"], T["invcnt"], T["pool_w"], T["gvec"], T["ident"]
    w1_d, w2_d, out_o = T["w_mlp_in1"], T["w_mlp_out1"], T["out"]

    s0 = _scope(nc)
    identf = _sb(s0, "identf", [128, 128], F32); cB = Buf()
    onesD = _sb(s0, "onesD", [128, 128], F32)
    epsT = _sb(s0, "epsT", [128, 1], F32)
    gv = _sb(s0, "gv", [128, 4, DC], F32)
    cx.dma(cx.sp, [(identf[:], identf_d), (gv[:], gv_d)], writes=[cB])
    cx.op(cx.dve, lambda: nc.vector.memset(onesD[:], 1.0 / D), writes=[cB])
    cx.op(cx.dve, lambda: nc.vector.memset(epsT[:], EPS), writes=[cB])
    xT = _sb(s0, "xT", [128, DC, TL], F32); xTB = [Buf() for _ in range(DC)]
    cx.dma(cx.sp, [(xT[:, 0:8], xT_d[:, 0:8])], writes=xTB[0:8])
    cx.dma(cx.sp, [(xT[:, 8:16], xT_d[:, 8:16])], writes=xTB[8:16])

    sP = _scope(nc)
    xh = _sb(sP, "xh", [128, DC, NSLOT * PH], F32); xhB = [Buf() for _ in range(DC)]
    T["load_halo"](cx, sP, xh, xhB)
    sq = [_sb(sP, "sq%d" % j, [128, TL], F32) for j in range(2)]; sqB = [Buf(), Buf()]
    rstd = _sb(sP, "rstd", [128, TL], F32); rstdB = Buf()
    rstdh = _sb(sP, "rstdh", [128, NSLOT * PH], F32); rstdhB = Buf()
    icnt = _sb(sP, "icnt", [128, 4, TL], F32); icB = Buf()
    cx.dma(cx.sp, [(icnt[:], icnt_d)], writes=[icB])
    xn = _sb(sP, "xn", [128, 4, NSLOT, PW], F32); xnB = Buf()
    sa = _sb(sP, "sa", [128, 4, NSLOT, PW], F32); saB = Buf()
    sb2 = _sb(sP, "sb2", [128, 4, NSLOT, PW], F32); sbB = Buf()
    pT = _sb(sP, "pT", [128, 4, TL], BF16); pTB = Buf()
    wsp = WStream(cx, sP, "pw", 4, 512, 2)
    rms_stats_fm(cx, lambda k: xT[:, k, :], xTB, TL, onesD,
                 dict(sq=sq, sqb=sqB, ps=ps[0:2], psb=psB[0:2], rstd=rstd, rstdb=rstdB, eps=epsT))
    rms_stats_fm(cx, lambda k: xh[:, k, :], xhB, NSLOT * PH, onesD,
                 dict(sq=sq, sqb=sqB, ps=ps[2:3], psb=psB[2:3], rstd=rstdh, rstdb=rstdhB, eps=epsT))
    cnt = 0
    v8 = lambda ap: ap.rearrange("p (i t) -> p i t", i=NSLOT)
    for gi in range(4):
        w = 2 ** (gi + 1)
        for kk in range(4):
            k = gi * 4 + kk
            rms_apply_fm(cx, k, v8(xT[:, k, :]), xTB[k], gv[:, 0, :], xn[:, kk, :, PH:PW], xnB, v8(rstd[:, :]), rstdB)
            rms_apply_fm(cx, k, v8(xh[:, k, :]), xhB[k], gv[:, 0, :], xn[:, kk, :, 0:PH], xnB, v8(rstdh[:, :]), rstdhB)
        fl = lambda t, lo, hi: t[:].rearrange("p a i w -> p (a i) w")[:, :, lo:hi]
        src, srcB = xn, xnB
        bufs = [(sa, saB), (sb2, sbB)]
        sh = 1
        step = 0
        while sh < w:
            dst, dstB = bufs[step % 2]
            cx.op(cx.dve, lambda: nc.vector.tensor_add(out=fl(dst, sh, PW), in0=fl(src, sh, PW), in1=fl(src, 0, PW - sh)),
                  reads=[srcB], writes=[dstB])
            src, srcB = dst, dstB
            sh *= 2; step += 1
        oth, othB = bufs[step % 2]
        for kk in range(4):
            cx.op(cx.dve, lambda: nc.vector.tensor_mul(out=oth[:, kk, :, PH:PW], in0=src[:, kk, :, PH:PW],
                                                       in1=v8(icnt[:, gi, :])), reads=[srcB, icB], writes=[othB])
            cx.op(cx.dve, lambda: nc.vector.tensor_sub(out=v8(pT[:, kk, :]), in0=oth[:, kk, :, PH:PW],
                                                       in1=xn[:, kk, :, PH:PW]), reads=[othB, xnB], writes=[pTB])
        wt, wb = wsp.load(pw_d[gi], 0, 512, kch=4)
        for m in range(4):
            och = gi * 4 + m
            for tb in range(2):
                pb = 3 + cnt % 4; cnt += 1
                for kk in range(4):
                    cx.op(cx.pe, lambda: nc.tensor.matmul(ps[pb][:], lhsT=wt[:, kk, m * 128:(m + 1) * 128],
                                                          rhs=pT[:, kk, tb * 512:(tb + 1) * 512],
                                                          start=(kk == 0), stop=(kk == 3)),
                          reads=[wb, pTB], writes=[psB[pb]], mark=(kk == 3))
                cx.op(cx.dve, lambda: nc.vector.scalar_tensor_tensor(
                    out=xT[:, och, tb * 512:(tb + 1) * 512], in0=ps[pb][:], scalar=gv[:, 1, och:och + 1],
                    in1=xT[:, och, tb * 512:(tb + 1) * 512], op0=ALU.mult, op1=ALU.add),
                    reads=[psB[pb], cB], writes=[xTB[och]])
    cx.barrier()
    sP.close()

    sM = _scope(nc)
    gm = gv[:, 2, :]
    mlp_stage(cx, sM, xT, xTB, gm, w1_d, w2_d, ps, psB, onesD, epsT)
    cx.barrier()
    sM.close()

    sF = _scope(nc)
    sq = [_sb(sF, "fsq%d" % j, [128, TL], F32) for j in range(2)]; sqB = [Buf(), Buf()]
    rstd = _sb(sF, "frstd", [128, TL], F32); rstdB = Buf()
    otk = [_sb(sF, "otk%d" % j, [128, D], F32) for j in range(2)]; otkB = [Buf(), Buf()]
    rms_norm_fm(cx, lambda k: xT[:, k, :], xTB, TL, gv[:, 3, :], lambda k: xT[:, k, :], xTB, onesD,
                dict(sq=sq, sqb=sqB, ps=ps[0:2], psb=psB[0:2], rstd=rstd, rstdb=rstdB, eps=epsT))
    cnt = 0
    for i in range(NSLOT):
        j = i % 2
        for k4 in range(4):
            pb = 2 + cnt % 5; cnt += 1
            for kk in range(4):
                k = k4 * 4 + kk
                cx.op(cx.pe, lambda: nc.tensor.transpose(out=ps[pb][:, kk * 128:(kk + 1) * 128],
                                                         in_=xT[:, k, i * 128:(i + 1) * 128], identity=identf[:]),
                      reads=[xTB[k], cB], writes=[psB[pb]], mark=(kk == 3))
            eng = cx.act if k4 % 2 else cx.dve
            if k4 % 2:
                cx.op(cx.act, lambda: nc.scalar.copy(out=otk[j][:, k4 * 512:(k4 + 1) * 512], in_=ps[pb][:]),
                      reads=[psB[pb]], writes=[otkB[j]])
            else:
                cx.op(cx.dve, lambda: nc.vector.tensor_copy(out=otk[j][:, k4 * 512:(k4 + 1) * 512], in_=ps[pb][:]),
                      reads=[psB[pb]], writes=[otkB[j]])
        cx.dma(cx.sp, [(out_o[i], otk[j][:])], reads=[otkB[j]], is_out=True)
    cx.barrier()
    sF.close(); s0.close()

def _tile_of(c, i):
    return 8 * i + c


def host_p1_inputs(x, mix_norm, w_in, conv_w, conv_b, conv_ln_g, conv_ln_b):
    x2 = x.reshape(S, D)
    xpad = np.concatenate([np.zeros((HALO, D), np.float32), x2], 0)
    fm = lambda v, n: np.ascontiguousarray(v.reshape(n, 128).T)
    common = {
        "w_in": np.ascontiguousarray(w_in[0]),
        "g0": fm(mix_norm[0], DC),
        "convw": np.ascontiguousarray(conv_w[0].reshape(31, 8, 128).transpose(2, 1, 0)),
        "cpar": np.ascontiguousarray(np.stack([fm(conv_b[0], 8), fm(conv_ln_g[0], 8), fm(conv_ln_b[0], 8)], 1)),
        "ident": np.eye(128, dtype=np.float32),
    }
    maps = []
    for c in range(NCORES):
        xh = np.stack([xpad[128 * _tile_of(c, i):128 * _tile_of(c, i) + HALO + 128] for i in range(NSLOT)], 0)
        m = dict(common)
        m["xh"] = np.ascontiguousarray(xh)
        maps.append(m)
    return maps


BF = ml_dtypes.bfloat16


def host_masks(c):
    kl = np.arange(128)[:, None]
    ql = np.arange(128)[None, :]
    cmask = np.zeros((128, NSLOT, 4, 128), np.float32)
    bonus = np.zeros((128, NSLOT, 128), np.float32)
    for i in range(NSLOT):
        t = 128 * (8 * i + c) + np.arange(128)
        for nt in range(4):
            n = 128 * nt + np.arange(128)
            vis = (16 * n[:, None] + 31 <= t[None, :]) & (n[:, None] <= 510)
            cmask[:, i, nt, :] = np.where(vis, 0.0, NEG)
        cur = t // 64
        j = np.arange(128)[None, :]
        forced = (j == 0) | (j == cur[:, None]) | (j == cur[:, None] - 1)
        bonus[:, i, :] = np.where(forced, 1000.0, 0.0)
    dz = np.zeros((128, 8, 128), np.float32)
    for z in range(8):
        dz[:, z, :] = np.where(128 * z + kl <= 128 * c + ql, 0.0, NEG)
    wm = np.zeros((128, 12, 128), np.float32)
    for rho in range(12):
        dlt = 128 * (c + 4 - rho)
        d = kl - ql
        wm[:, rho, :] = np.where((d <= dlt) & (d > dlt - 512), 0.0, NEG)
    return {"cmask": cmask.astype(BF), "dzmask": dz.astype(BF), "wmask": wm.astype(BF), "bonus": bonus}


def host_shared_consts():
    esel = np.zeros((128, 64, 128), np.float32)
    for kt in range(64):
        for b in range(2):
            esel[2 * kt + b, kt, 64 * b:64 * b + 64] = 1.0
    ovl = np.zeros((128, 4, 128), np.float32)
    for nt in range(4):
        n = 128 * nt + np.arange(128)
        j = np.arange(128)
        ov = (16 * n[:, None] <= 64 * j[None, :] + 63) & (16 * n[:, None] + 31 >= 64 * j[None, :]) & (n[:, None] <= 510)
        ovl[:, nt, :] = ov
    return {"esel": esel.astype(BF), "ovl": ovl.astype(BF), "identb": np.eye(128, dtype=np.float32).astype(BF)}


def _fm(v, n):
    return np.ascontiguousarray(np.asarray(v).reshape(n, 128).T)


def host_inputs(inp):
    m1 = host_p1_inputs(inp["x"], inp["mix_norm"], inp["w_in"], inp["conv_w"], inp["conv_b"],
                        inp["conv_ln_g"], inp["conv_ln_b"])
    shared = host_shared_consts()
    gvec = np.ascontiguousarray(np.stack([_fm(inp["mix_norm"][1], DC), _fm(inp["pool_scale"][0], DC),
                                          _fm(inp["mlp_norm"][1], DC), _fm(inp["final_norm"], DC)], 1))
    shared.update({
        "cw1_k": np.ascontiguousarray(inp["cmp_k_w1"][0]), "cw1_v": np.ascontiguousarray(inp["cmp_v_w1"][0]),
        "cw2_k": np.ascontiguousarray(inp["cmp_k_w2"][0]), "cw2_v": np.ascontiguousarray(inp["cmp_v_w2"][0]),
        "posT": np.ascontiguousarray(np.stack([inp["cmp_pos_k"][0].T, inp["cmp_pos_v"][0].T], 1)),
        "w_out": np.ascontiguousarray(inp["w_out"][0]),
        "w_mlp_in0": np.ascontiguousarray(inp["w_mlp_in"][0]), "w_mlp_out0": np.ascontiguousarray(inp["w_mlp_out"][0]),
        "w_mlp_in1": np.ascontiguousarray(inp["w_mlp_in"][1]), "w_mlp_out1": np.ascontiguousarray(inp["w_mlp_out"][1]),
        "g_mlp0": _fm(inp["mlp_norm"][0], DC),
        "pool_w": np.ascontiguousarray(inp["pool_w"][0]), "gvec": gvec,
    })
    maps = []
    for c in range(NCORES):
        m = dict(m1[c])
        m.update(shared)
        m.update(host_masks(c))
        icnt = np.zeros((4, TL), np.float32)
        for i in range(NSLOT):
            pos1 = 128 * (8 * i + c) + np.arange(128) + 1
            for gi in range(4):
                icnt[gi, i * 128:(i + 1) * 128] = 1.0 / np.minimum(pos1, 2 ** (gi + 1))
        m["invcnt"] = np.ascontiguousarray(np.broadcast_to(icnt[None], (128, 4, TL)))
        sel = np.zeros((128, 9), np.float32)
        if c > 0:
            sel[:, c - 1] = 1.0
        else:
            sel[:, 8] = 1.0
        m["halosel"] = sel
        maps.append(m)
    return maps


def build_fused(stage=3):
    nc = bass.Bass("TRN2", target_bir_lowering=False)
    cx = Ctx(nc)
    din = lambda name, shape, dt: nc.dram_tensor(name, shape, dt, kind="ExternalInput").ap()
    itn = lambda name, shape, dt: nc.dram_tensor(name, shape, dt)
    T = {}
    for name, shape, dt in [
        ("xh", [NSLOT, HALO + 128, D], F32), ("w_in", [D, IN_COLS], F32), ("g0", [128, DC], F32),
        ("convw", [128, 8, 31], F32), ("cpar", [128, 3, 8], F32), ("ident", [128, 128], F32),
        ("cw1_k", [2048, 128], F32), ("cw1_v", [2048, 128], F32), ("cw2_k", [128, 64], F32), ("cw2_v", [128, 64], F32),
        ("posT", [64, 2, 32], F32), ("cmask", [128, NSLOT, 4, 128], BF16), ("dzmask", [128, 8, 128], BF16),
        ("wmask", [128, 12, 128], BF16), ("esel", [128, 64, 128], BF16), ("bonus", [128, NSLOT, 128], F32),
        ("ovl", [128, 4, 128], BF16), ("identb", [128, 128], BF16), ("w_out", [D, D], F32),
        ("w_mlp_in0", [D, DFF], F32), ("w_mlp_out0", [DFF, D], F32), ("g_mlp0", [128, DC], F32),
        ("w_mlp_in1", [D, DFF], F32), ("w_mlp_out1", [DFF, D], F32),
        ("invcnt", [128, 4, TL], F32), ("pool_w", [4, 512, 512], F32), ("gvec", [128, 4, DC], F32),
        ("halosel", [128, 9], F32),
    ]:
        T[name] = din(name, shape, dt)
    T["out"] = nc.dram_tensor("out", [NSLOT, 128, D], F32, kind="ExternalOutput").ap()
    T["xT"] = itn("h_xT", [128, DC, TL], F32).ap()
    T["xT2"] = itn("h_xT2", [128, DC, TL], F32).ap()
    T["qT"] = itn("h_qT", [128, NSLOT, 8, 128], BF16).ap()
    T["cT"] = itn("h_cT", [128, 8, TL], BF16).ap()
    T["gates"] = itn("h_gates", [128, NSLOT, 48], F32).ap()
    KW = 4 * NSLOT * 2 * 128
    VW = 2 * NSLOT * 256
    kvloc = itn("kvloc", [128, (KW + VW) // 2], F32)
    kvgat = itn("kvgat", [NCORES * 128, (KW + VW) // 2], F32)
    lb = kvloc.bitcast(BF16).ap()
    gb = kvgat.bitcast(BF16).ap().rearrange("(r p) c -> r p c", r=NCORES)
    T["kfm"] = lb[:, 0:KW].rearrange("p (k i g t) -> p k i g t", k=4, i=NSLOT, g=2)
    T["vtm"] = lb[:, KW:KW + VW].rearrange("p (k i e) -> p k i e", k=2, i=NSLOT)
    T["kfm_g"] = gb[:, :, 0:KW].rearrange("r p (k i g t) -> r p k i g t", k=4, i=NSLOT, g=2)
    T["vtm_g"] = gb[:, :, KW:KW + VW].rearrange("r p (k i e) -> r p k i e", k=2, i=NSLOT)
    T["kvB"] = Buf(); T["kvgB"] = Buf()
    hloc = itn("hloc", [128, DC * NSLOT * PH], F32)
    hgat = itn("hgat", [NCORES * 128, DC * NSLOT * PH], F32)
    T["halo_out"] = hloc.ap().rearrange("p (k i t) -> p k i t", k=DC, i=NSLOT)
    H = hgat.ap().rearrange("(r p) (k i t) -> r p k i t", r=NCORES, k=DC, i=NSLOT)
    T["haloB"] = Buf(); hgB = Buf()
    rg = [list(range(NCORES))]

    def after_kv():
        cx.op(cx.pool, lambda: nc.gpsimd.collective_compute("AllGather", ALU.bypass, replica_groups=rg,
                                                            ins=[kvloc.ap().opt()], outs=[kvgat.ap().opt()]),
              reads=[T["kvB"]], writes=[T["kvgB"]])

    def after_halo():
        cx.op(cx.pool, lambda: nc.gpsimd.collective_compute("AllGather", ALU.bypass, replica_groups=rg,
                                                            ins=[hloc.ap().opt()], outs=[hgat.ap().opt()]),
              reads=[T["haloB"]], writes=[hgB])

    def load_halo(cx_, st, xh, xhB):
        sel = _sb(st, "hsel", [128, 9], F32); selB = Buf()
        cand = _sb(st, "hcand", [128, 9, 4, NSLOT * PH], F32); candB = Buf()
        cx.dma(cx.sp, [(sel[:], T["halosel"])], writes=[selB])
        for kg in range(4):
            ks_ = slice(kg * 4, kg * 4 + 4)
            cx.op(cx.dve, lambda: nc.vector.memset(cand[:, 8, :, 0:PH], 0.0), writes=[candB])
            prs = [(cand[:, j], H[j, :, ks_].rearrange("p k i t -> p k (i t)")) for j in range(8)]
            prs.append((cand[:, 8, :, PH:NSLOT * PH], H[7, :, ks_, 0:NSLOT - 1, :].rearrange("p k i t -> p k (i t)")))
            cx.dma(cx.sp, prs, reads=[hgB], writes=[candB])
            cx.op(cx.dve, lambda: nc.vector.tensor_scalar(out=xh[:, ks_, :], in0=cand[:, 0], scalar1=sel[:, 0:1],
                                                          scalar2=None, op0=ALU.mult),
                  reads=[candB, selB], writes=xhB[ks_])
            for j in range(1, 9):
                cx.op(cx.dve, lambda: nc.vector.scalar_tensor_tensor(out=xh[:, ks_, :], in0=cand[:, j],
                                                                     scalar=sel[:, j:j + 1], in1=xh[:, ks_, :],
                                                                     op0=ALU.mult, op1=ALU.add),
                      reads=[candB, selB], writes=xhB[ks_])

    T["after_kv"] = after_kv
    T["after_halo"] = after_halo
    T["load_halo"] = load_halo
    ps = [_ps(nc, "ps%d" % j, [128, 512], F32) for j in range(7)]; psB = [Buf() for _ in range(7)]
    ptr = _ps(nc, "ptr", [128, 1024], BF16); ptrB = Buf()
    emit_p1(nc, cx, T, ps, psB)
    if stage == 1:
        dbg = nc.dram_tensor("dbg", [NCORES * 128, (KW + VW) // 2], F32, kind="ExternalOutput").ap()
        cx.dma(cx.sp, [(dbg, kvgat.ap())], reads=[T["kvgB"]], is_out=True)
        cx.finish()
        return nc
    emit_p2(nc, cx, T, ps, psB, ptr, ptrB)
    if stage == 2:
        dbg = nc.dram_tensor("dbg", [NCORES * 128, DC * NSLOT * PH], F32, kind="ExternalOutput").ap()
        cx.dma(cx.sp, [(dbg, hgat.ap())], reads=[hgB], is_out=True)
        dbg2 = nc.dram_tensor("dbg2", [128, DC, TL], F32, kind="ExternalOutput").ap()
        cx.dma(cx.sp, [(dbg2, T["xT2"])], is_out=True)
        cx.finish()
        return nc
    emit_p3(nc, cx, T, ps, psB)
    cx.finish()
    return nc


_PROGS = {}


def kernel(x, mix_norm, mlp_norm, w_mlp_in, w_mlp_out, w_in, w_out, conv_w, conv_b,
           conv_ln_g, conv_ln_b, cmp_pos_k, cmp_pos_v, cmp_k_w1, cmp_k_w2,
           cmp_v_w1, cmp_v_w2, pool_w, pool_scale, final_norm):
    inp = dict(x=x, mix_norm=mix_norm, mlp_norm=mlp_norm, w_mlp_in=w_mlp_in, w_mlp_out=w_mlp_out, w_in=w_in,
               w_out=w_out, conv_w=conv_w, conv_b=conv_b, conv_ln_g=conv_ln_g, conv_ln_b=conv_ln_b,
               cmp_pos_k=cmp_pos_k, cmp_pos_v=cmp_pos_v, cmp_k_w1=cmp_k_w1, cmp_k_w2=cmp_k_w2,
               cmp_v_w1=cmp_v_w1, cmp_v_w2=cmp_v_w2, pool_w=pool_w, pool_scale=pool_scale, final_norm=final_norm)
    inp = {k: np.asarray(v, dtype=np.float32) for k, v in inp.items()}
    if "fused" not in _PROGS:
        _PROGS["fused"] = build_fused()
    maps = host_inputs(inp)
    res = run_bass_kernel_spmd(_PROGS["fused"], maps, core_ids=list(range(NCORES))).results
    out = np.zeros((S, D), np.float32)
    for c in range(NCORES):
        o = np.asarray(res[c]["out"])
        for i in range(NSLOT):
            kt = 8 * i + c
            out[128 * kt:128 * kt + 128] = o[i]
    return out.reshape(1, S, D)
```
